# Optimizing a Trainium2 kernel written in Bass

```python
import jax, jax.numpy as jnp
from jax import lax
import numpy as np

D_MODEL = 2048
BATCH = 8
SEQ = 2048
DEPTH = 4

GRID_W = 64
CTX_LEN = 256
HEAD_DIM = 64
MIX_WIDTH = D_MODEL
LRU_WIDTH = MIX_WIDTH // 4
LRU_BLOCKS = LRU_WIDTH // HEAD_DIM
LRU_CONV = 4
LRU_CONV_PAD = 2
LRU_C = 8.0
NA_WIDTH = MIX_WIDTH // 2
NA_HEADS = NA_WIDTH // HEAD_DIM
NA_ROWS = 8
NA_KC = 16
NA_QC = 16
NA_SPAN = NA_QC + NA_KC
ROPE_BASE = 10000.0
ROPE_FREQS = HEAD_DIM // 4
SGU_WIDTH = MIX_WIDTH - LRU_WIDTH - NA_WIDTH
SGU_GROUPS = SGU_WIDTH // HEAD_DIM
SGU_CHUNK = 128
PROJ_WIDTH = 2 * LRU_WIDTH + 3 * NA_WIDTH + 2 * SGU_WIDTH
FFN_HIDDEN = 5504
FFN_CONV = 3
NORM_EPS = 1e-6
NEG_INF = -1e30

kernel_name = 'hybrid_lru_natten_sgu_dit'


def rmsnorm(x, g):
    xf = x.astype(jnp.float32)
    y = xf * lax.rsqrt(jnp.mean(xf * xf, axis=-1, keepdims=True) + NORM_EPS)
    return (y * g.astype(jnp.float32)).astype(x.dtype)


def layernorm(x, g, b):
    xf = x.astype(jnp.float32)
    mu = jnp.mean(xf, axis=-1, keepdims=True)
    var = jnp.mean(jnp.square(xf - mu), axis=-1, keepdims=True)
    return ((xf - mu) * lax.rsqrt(var + NORM_EPS) * g.astype(jnp.float32) + b.astype(jnp.float32)).astype(x.dtype)


def adaln(cvec, w, b):
    return jnp.split(jax.nn.silu(cvec) @ w + b, 6, axis=-1)


def modulate(h, shift, scale):
    return h * (1 + scale) + shift


def dwconv(x, w, b, pad_left):
    K, T = w.shape[0], x.shape[1]
    xp = jnp.pad(x, ((0, 0), (pad_left, K - 1 - pad_left), (0, 0)))
    y = b
    for k in range(K):
        y = y + xp[:, k:k + T] * w[k]
    return y


def split_proj(z):
    a = LRU_WIDTH
    idx = [a, 2 * a, 2 * a + NA_WIDTH, 2 * a + 2 * NA_WIDTH, 2 * a + 3 * NA_WIDTH,
           2 * a + 3 * NA_WIDTH + SGU_WIDTH]
    return jnp.split(z, idx, axis=-1)


def to_heads(t):
    return t.reshape(t.shape[0], t.shape[1], NA_HEADS, HEAD_DIM)


def rglru_coeffs(x, gate_w, gate_b, lam):
    B, T, W = x.shape
    xg = x.reshape(B, T, LRU_BLOCKS, HEAD_DIM)
    g = jnp.einsum('btgi,kgio->kbtgo', xg, gate_w).reshape(2, B, T, W) + gate_b[:, None, None, :]
    r = jax.nn.sigmoid(g[0].astype(jnp.float32))
    i = jax.nn.sigmoid(g[1].astype(jnp.float32))
    log_a = -LRU_C * jax.nn.softplus(-lam.astype(jnp.float32)) * r
    a = jnp.exp(log_a)
    b = jnp.sqrt(-jnp.expm1(2.0 * log_a)) * (i * x.astype(jnp.float32))
    return a, b


def _lin_combine(left, right):
    a_l, b_l = left
    a_r, b_r = right
    return a_l * a_r, a_r * b_l + b_r


def linear_scan(a, b, h0, reverse):
    if reverse:
        a, b = jnp.flip(a, 1), jnp.flip(b, 1)
    b = b.at[:, 0].add(a[:, 0] * h0)
    _, h = lax.associative_scan(_lin_combine, (a, b), axis=1)
    return jnp.flip(h, 1) if reverse else h


def rglru_mixer(xl, yl, xc, yc, conv_w, conv_b, gate_w, gate_b, lam, need_ctx):
    xl = dwconv(xl, conv_w, conv_b, LRU_CONV_PAD)
    xc = dwconv(xc, conv_w, conv_b, LRU_CONV_PAD)
    B, _, W = xc.shape
    h_lat, h_ctx = [], []
    for d in range(2):
        rev = d == 1
        a_c, b_c = rglru_coeffs(xc, gate_w[d], gate_b[d], lam[d])
        hc = linear_scan(a_c, b_c, jnp.zeros((B, W), jnp.float32), rev)
        h_final = hc[:, 0] if rev else hc[:, -1]
        a_l, b_l = rglru_coeffs(xl, gate_w[d], gate_b[d], lam[d])
        h_lat.append(linear_scan(a_l, b_l, h_final, rev))
        h_ctx.append(hc)
    out_l = (h_lat[0] + h_lat[1]).astype(xl.dtype) * jax.nn.gelu(yl)
    if not need_ctx:
        return out_l, None
    out_c = (h_ctx[0] + h_ctx[1]).astype(xc.dtype) * jax.nn.gelu(yc)
    return out_l, out_c


def rope2d_tables(S):
    t = jnp.arange(S)
    pos = jnp.stack([t // GRID_W, t % GRID_W], axis=-1).astype(jnp.float32)
    inv = ROPE_BASE ** (-jnp.arange(ROPE_FREQS, dtype=jnp.float32) / ROPE_FREQS)
    ang = pos[:, :, None] * inv
    return jnp.cos(ang), jnp.sin(ang)


def rope2d(x, cos, sin):
    B, S, H, dh = x.shape
    xr = x.reshape(B, S, H, 2, 2, ROPE_FREQS)
    x1, x2 = xr[..., 0, :], xr[..., 1, :]
    cs = cos[None, :, None].astype(x.dtype)
    sn = sin[None, :, None].astype(x.dtype)
    return jnp.stack([x1 * cs - x2 * sn, x2 * cs + x1 * sn], axis=-2).reshape(B, S, H, dh)


def neighbourhood_attention(q, k, v, k_ctx, v_ctx, rpb, cos, sin):
    B, S, H, dh = q.shape
    rows = S // GRID_W
    kr = min(NA_ROWS, rows)
    scale = dh ** -0.5
    grid = lambda t: t.reshape(B, rows, GRID_W, H, dh)
    q_rot, k_g, v_g, q_raw = grid(rope2d(q, cos, sin)), grid(rope2d(k, cos, sin)), grid(v), grid(q)
    r_idx = jnp.arange(rows)
    row_start = jnp.clip(r_idx - kr // 2, 0, rows - kr)
    key_rows = row_start[:, None] + jnp.arange(kr)[None, :]
    rel_r = key_rows - r_idx[:, None] + (NA_ROWS - 1)
    n_win = kr * NA_SPAN

    def block(j):
        c0 = j * NA_QC
        q_cols = c0 + jnp.arange(NA_QC)
        col_start = jnp.clip(q_cols - NA_KC // 2, 0, GRID_W - NA_KC)
        key_cols = jnp.clip(c0 - NA_KC // 2, 0, GRID_W - NA_SPAN) + jnp.arange(NA_SPAN)
        in_win = (key_cols[None, :] >= col_start[:, None]) & (key_cols[None, :] < col_start[:, None] + NA_KC)
        rel_c = jnp.clip(key_cols[None, :] - q_cols[:, None] + (NA_KC - 1), 0, 2 * NA_KC - 2)
        kb = k_g[:, key_rows[:, :, None], key_cols[None, None, :]]
        vb = v_g[:, key_rows[:, :, None], key_cols[None, None, :]]
        qb = lax.dynamic_slice_in_dim(q_rot, c0, NA_QC, axis=2)
        qb_raw = lax.dynamic_slice_in_dim(q_raw, c0, NA_QC, axis=2)
        bias = rpb[:, rel_r[:, None, :, None], rel_c[None, :, None, :]]
        s_win = jnp.einsum('brqhd,brkmhd->bhrqkm', qb, kb).astype(jnp.float32) * scale
        s_win = s_win + bias[None].astype(jnp.float32)
        s_win = jnp.where(in_win[:, None, :], s_win, NEG_INF).reshape(B, H, rows, NA_QC, n_win)
        s_ctx = jnp.einsum('brqhd,blhd->bhrql', qb_raw, k_ctx).astype(jnp.float32) * scale
        p = jax.nn.softmax(jnp.concatenate([s_win, s_ctx], axis=-1), axis=-1).astype(v.dtype)
        o = jnp.einsum('bhrqk,brkhd->brqhd', p[..., :n_win], vb.reshape(B, rows, n_win, H, dh))
        return o + jnp.einsum('bhrql,blhd->brqhd', p[..., n_win:], v_ctx)

    o = lax.map(block, jnp.arange(GRID_W // NA_QC))
    return jnp.moveaxis(o, 0, 2).reshape(B, S, H * dh)


def context_attention(q, k, v):
    B, L, H, dh = q.shape
    s = jnp.einsum('blhd,bmhd->bhlm', q, k).astype(jnp.float32) * (dh ** -0.5)
    p = jax.nn.softmax(s, axis=-1).astype(v.dtype)
    return jnp.einsum('bhlm,bmhd->blhd', p, v).reshape(B, L, H * dh)


def spatial_gating(u, v, ln_g, ln_b, w_s, b_s):
    u = jax.nn.gelu(u)
    v = layernorm(jax.nn.gelu(v), ln_g, ln_b)
    B, T, _ = v.shape
    vg = v.reshape(B, T // SGU_CHUNK, SGU_CHUNK, SGU_GROUPS, HEAD_DIM)
    mixed = jnp.einsum('gpq,bnqgd->bnpgd', w_s, vg) + b_s.T[None, None, :, :, None]
    return u * mixed.reshape(B, T, SGU_WIDTH)


def hybrid_mixer(hx, hc, w_in, lru_conv_w, lru_conv_b, lru_gate_w, lru_gate_b, lru_lambda,
                 na_rpb, sgu_ln_g, sgu_ln_b, sgu_w, sgu_b, cos, sin, need_ctx):
    ax, ay, q, k, v, su, sv = split_proj(hx @ w_in)
    cax, cay, cq, ck, cv, csu, csv = split_proj(hc @ w_in)
    ck_h, cv_h = to_heads(ck), to_heads(cv)
    out_a, cout_a = rglru_mixer(ax, ay, cax, cay, lru_conv_w, lru_conv_b, lru_gate_w, lru_gate_b,
                                lru_lambda, need_ctx)
    out_b = neighbourhood_attention(to_heads(q), to_heads(k), to_heads(v), ck_h, cv_h, na_rpb, cos, sin)
    out_c = spatial_gating(su, sv, sgu_ln_g, sgu_ln_b, sgu_w, sgu_b)
    mix_x = jnp.concatenate([out_a, out_b, out_c], axis=-1)
    if not need_ctx:
        return mix_x, None
    cout_b = context_attention(to_heads(cq), ck_h, cv_h)
    cout_c = spatial_gating(csu, csv, sgu_ln_g, sgu_ln_b, sgu_w, sgu_b)
    return mix_x, jnp.concatenate([cout_a, cout_b, cout_c], axis=-1)


def conv_ffn(h, w_up, conv_w, conv_b, w_down):
    u = dwconv(h @ w_up, conv_w, conv_b, 1)
    a, g = jnp.split(u, 2, axis=-1)
    return (jax.nn.silu(g) * a) @ w_down


def setup_inputs(seed: int = 0) -> dict:
    key = jax.random.key(seed)
    ks = jax.random.split(key, 26)
    D = D_MODEL
    nrm = lambda k, shape, s: jax.random.normal(k, shape, jnp.float32) * s
    a_pow = jax.random.uniform(ks[14], (DEPTH, 2, LRU_WIDTH), jnp.float32, 0.9, 0.999)
    sig = a_pow ** (1.0 / LRU_C)
    return {
        'x': nrm(ks[0], (BATCH, SEQ, D), 1.0),
        'c': nrm(ks[1], (BATCH, D), 1.0),
        'ctx': nrm(ks[2], (BATCH, CTX_LEN, D), 1.0),
        'c_ctx': nrm(ks[3], (D,), 1.0),
        'w_ada': nrm(ks[4], (DEPTH, D, 6 * D), 0.5 * D ** -0.5),
        'b_ada': nrm(ks[5], (DEPTH, 6 * D), 0.01),
        'norm_mix_g': 1.0 + nrm(ks[6], (DEPTH, D), 0.02),
        'norm_ffn_g': 1.0 + nrm(ks[7], (DEPTH, D), 0.02),
        'w_in': nrm(ks[8], (DEPTH, D, PROJ_WIDTH), D ** -0.5),
        'lru_conv_w': nrm(ks[9], (DEPTH, LRU_CONV, LRU_WIDTH), LRU_CONV ** -0.5),
        'lru_conv_b': nrm(ks[10], (DEPTH, LRU_WIDTH), 0.01),
        'lru_gate_w': nrm(ks[11], (DEPTH, 2, 2, LRU_BLOCKS, HEAD_DIM, HEAD_DIM), HEAD_DIM ** -0.5),
        'lru_gate_b': nrm(ks[12], (DEPTH, 2, 2, LRU_WIDTH), 0.01),
        'lru_lambda': jnp.log(sig) - jnp.log1p(-sig),
        'na_rpb': nrm(ks[15], (DEPTH, NA_HEADS, 2 * NA_ROWS - 1, 2 * NA_KC - 1), 0.1),
        'sgu_ln_g': 1.0 + nrm(ks[16], (DEPTH, SGU_WIDTH), 0.02),
        'sgu_ln_b': nrm(ks[17], (DEPTH, SGU_WIDTH), 0.01),
        'sgu_w': nrm(ks[18], (DEPTH, SGU_GROUPS, SGU_CHUNK, SGU_CHUNK), 0.5 * SGU_CHUNK ** -0.5),
        'sgu_b': 1.0 + nrm(ks[19], (DEPTH, SGU_GROUPS, SGU_CHUNK), 0.01),
        'w_out': nrm(ks[20], (DEPTH, MIX_WIDTH, D), MIX_WIDTH ** -0.5),
        'ffn_up': nrm(ks[21], (DEPTH, D, 2 * FFN_HIDDEN), D ** -0.5),
        'ffn_conv_w': nrm(ks[22], (DEPTH, FFN_CONV, 2 * FFN_HIDDEN), FFN_CONV ** -0.5),
        'ffn_conv_b': nrm(ks[23], (DEPTH, 2 * FFN_HIDDEN), 0.01),
        'ffn_down': nrm(ks[24], (DEPTH, FFN_HIDDEN, D), FFN_HIDDEN ** -0.5),
        'final_norm_g': 1.0 + nrm(ks[25], (D,), 0.02),
    }


def reference(x, c, ctx, c_ctx, w_ada, b_ada, norm_mix_g, norm_ffn_g, w_in, lru_conv_w, lru_conv_b,
              lru_gate_w, lru_gate_b, lru_lambda, na_rpb, sgu_ln_g, sgu_ln_b, sgu_w, sgu_b, w_out,
              ffn_up, ffn_conv_w, ffn_conv_b, ffn_down, final_norm_g):
    cos, sin = rope2d_tables(x.shape[1])
    for i in range(DEPTH):
        need_ctx = i < DEPTH - 1
        sh_a, sc_a, g_a, sh_f, sc_f, g_f = adaln(c, w_ada[i], b_ada[i])
        csh_a, csc_a, cg_a, csh_f, csc_f, cg_f = adaln(c_ctx, w_ada[i], b_ada[i])
        hx = modulate(rmsnorm(x, norm_mix_g[i]), sh_a[:, None], sc_a[:, None])
        hc = modulate(rmsnorm(ctx, norm_mix_g[i]), csh_a, csc_a)
        mix_x, mix_c = hybrid_mixer(hx, hc, w_in[i], lru_conv_w[i], lru_conv_b[i], lru_gate_w[i],
                                    lru_gate_b[i], lru_lambda[i], na_rpb[i], sgu_ln_g[i], sgu_ln_b[i],
                                    sgu_w[i], sgu_b[i], cos, sin, need_ctx)
        x = x + g_a[:, None] * (mix_x @ w_out[i])
        hx = modulate(rmsnorm(x, norm_ffn_g[i]), sh_f[:, None], sc_f[:, None])
        x = x + g_f[:, None] * conv_ffn(hx, ffn_up[i], ffn_conv_w[i], ffn_conv_b[i], ffn_down[i])
        if need_ctx:
            ctx = ctx + cg_a * (mix_c @ w_out[i])
            hc = modulate(rmsnorm(ctx, norm_ffn_g[i]), csh_f, csc_f)
            ctx = ctx + cg_f * conv_ffn(hc, ffn_up[i], ffn_conv_w[i], ffn_conv_b[i], ffn_down[i])
    return rmsnorm(x, final_norm_g)
```

```python
import numpy as np
import concourse.bass as bass
import concourse.mybir as mybir
from concourse.bass_utils import run_bass_kernel_spmd

F32 = mybir.dt.float32
BF16 = mybir.dt.bfloat16
AF = mybir.ActivationFunctionType
ALU = mybir.AluOpType

D = 2048
S = 2048
CL = 256
T = S + CL
L = 4
NK = 16
PROJ = 5120
FH = 5504
NJ = 43
TBS = [(0, 512), (512, 512), (1024, 512), (1536, 512), (2048, 256)]
EPS = 1e-6
NEG = -1.0e4
GC = 1.5957691216057308


class Tk:
    def __init__(self, name):
        self.name = name
        self.w = {}
        self.r = {}
        self.dsem = None
        self.dkey = None
        self.dcnt = 0


class Sched:
    def __init__(self, nc):
        self.nc = nc
        self.E = {"pe": nc.tensor, "act": nc.scalar, "dve": nc.vector, "pool": nc.gpsimd, "sp": nc.sync}
        self.esem = {k: nc.alloc_semaphore("es_" + k) for k in ("pe", "act", "dve", "pool")}
        self.ecnt = {k: 0 for k in self.esem}
        self.pend = {k: False for k in self.esem}
        self.waited = {k: {} for k in self.E}
        self.dpool = []
        self.dpool_sw = []
        self.dall = {}
        self.nd = 0
        self.ninst = 0

    def _deps(self, reads, writes, partial):
        deps = {}

        def add(d):
            for k, (sem, v) in d.items():
                if k not in deps or deps[k][1] < v:
                    deps[k] = (sem, v)
        for t in reads:
            add(t.w)
        for t in writes:
            add(t.r)
            if not partial:
                add(t.w)
        return deps

    def _wait(self, e, deps):
        for k, (sem, v) in deps.items():
            if e == "pe" and k == "e_pe":
                continue
            if k in self.dall:
                v = max(v, self.dall[k][1])
            if self.waited[e].get(k, 0) < v:
                self.E[e].wait_ge(sem, v)
                self.waited[e][k] = v
                self.ninst += 1

    def _commit(self, tok, reads, writes, partial):
        k, sem, v = tok
        for t in writes:
            if not partial:
                t.w = {}
                t.r = {}
            t.w[k] = (sem, v)
        for t in reads:
            t.r[k] = (sem, v)

    def op(self, e, fn, reads=(), writes=(), partial=False, inc=True):
        self._wait(e, self._deps(reads, writes, partial))
        ins = fn(self.E[e])
        self.ninst += 1
        if inc:
            self.ecnt[e] += 1
            ins.then_inc(self.esem[e], 1)
            v = self.ecnt[e]
            self.pend[e] = False
        else:
            assert e == "pe"
            v = self.ecnt[e] + 1
            self.pend[e] = True
        self._commit(("e_" + e, self.esem[e], v), reads, writes, partial)

    def _getsem(self, t, q):
        if t.dsem is None:
            t.dq = "hw" if q in ("sp", "act") else "sw"
            pool_ = self.dpool if t.dq == "hw" else self.dpool_sw
            if pool_:
                k, sem, cnt = pool_.pop()
            else:
                k = "d%d" % self.nd
                self.nd += 1
                sem = self.nc.alloc_semaphore("ds_" + k)
                cnt = 0
                self.dall[k] = [sem, 0]
            t.dkey, t.dsem, t.dcnt = k, sem, cnt

    def dma(self, q, out, in_, reads=(), writes=(), semt=None, partial=False, **kw):
        self._wait(q, self._deps(reads, writes, partial))
        self._getsem(semt, q)
        assert semt.dq == ("hw" if q in ("sp", "act") else "sw")
        ins = self.E[q].dma_start(out=out, in_=in_, **kw)
        self.ninst += 1
        semt.dcnt += 16
        ins.then_inc(semt.dsem, 16)
        self.dall[semt.dkey][1] = semt.dcnt
        self._commit((semt.dkey, semt.dsem, semt.dcnt), reads, writes, partial)

    def release(self, tks):
        for t in tks:
            if t.dsem is not None:
                (self.dpool if t.dq == "hw" else self.dpool_sw).append((t.dkey, t.dsem, t.dcnt))
                t.dsem = None

    def barrier(self):
        for e in self.esem:
            assert not self.pend[e]
        deps = {("e_" + k): (self.esem[k], self.ecnt[k]) for k in self.esem if self.ecnt[k] > 0}
        for k, (sem, cnt) in self.dall.items():
            if cnt > 0:
                deps[k] = (sem, cnt)
        for e in self.E:
            self._wait(e, deps)


class Phase:
    _n = [0]

    def __init__(self, K, name):
        self.K = K
        Phase._n[0] += 1
        self.name = "%s%d" % (name, Phase._n[0])
        self.cms = []
        self.tks = []

    def __enter__(self):
        return self

    def sb(self, name, shape, dt):
        cm = self.K.nc.sbuf_tensor(self.name + "_" + name, shape, dt)
        h = cm.__enter__()
        self.cms.append(cm)
        return h.ap() if hasattr(h, "ap") else h

    def psum(self, name, shape):
        cm = self.K.nc.psum_tensor(self.name + "_" + name, shape, F32)
        h = cm.__enter__()
        self.cms.append(cm)
        return h.ap() if hasattr(h, "ap") else h

    def tk(self, name):
        t = Tk(self.name + "_" + name)
        self.tks.append(t)
        return t

    def __exit__(self, *a):
        self.K.sc.barrier()
        self.K.clear_tokens()
        self.K.sc.release(self.tks)
        for cm in reversed(self.cms):
            cm.__exit__(None, None, None)
        return False


def na_tiles(m):
    if m == 0:
        return [0, 1, 2, 3], 5, 7, 0
    if m == 1:
        return [0, 1, 2, 3], 9, 5, 0
    if m == 14:
        return [12, 13, 14, 15], 13, 3, 0
    if m == 15:
        return [12, 13, 14, 15], 17, 1, 0
    return list(range(m - 2, m + 3)), 0, 3, 1


NA_PATTERNS = [(0, 5, 3, 1), (5, 4, 7, 0), (9, 4, 5, 0), (13, 4, 3, 0), (17, 4, 1, 0)]


class Kern:
    def __init__(self, nlayers=L, dbg=False):
        self.nl = nlayers
        self.dbg = dbg
        nc = self.nc = bass.Bass("TRN2", target_bir_lowering=False)
        self.sc = Sched(nc)
        di = lambda n, s, dt=F32: nc.dram_tensor(n, s, dt, kind="ExternalInput").ap()
        self.I = dict(
            xT0=di("xT0", [D, T]), cT=di("cT", [128, NK, 2]),
            w_ada=di("w_ada", [L, D, 6 * D]), b_adaT=di("b_adaT", [L, 128, 96]),
            gmix=di("gmix", [L, 128, NK]), gffn=di("gffn", [L, 128, NK]), gfin=di("gfin", [128, NK]),
            w_in=di("w_in", [L, D, PROJ]), w_out=di("w_out", [L, D, D]),
            ffn_up=di("ffn_up", [L, D, 2 * FH]), ffn_down=di("ffn_down", [L, FH, D]),
            lcw=di("lcw", [L, 128, 4, 4]), lcb=di("lcb", [L, 128, 4]),
            lgw=di("lgw", [L, 2, 2, 8, 64, 64]), lgb=di("lgb", [L, 128, 2, 2, 4]), llam=di("llam", [L, 128, 2, 4]),
            natab=di("natab", [L, 16, 2, 16, 64, 64]),
            cos=di("cos", [128, S]), sin=di("sin", [128, S]), perm=di("perm", [128, 128]),
            slng=di("slng", [L, 512]), slnb=di("slnb", [L, 512]),
            swT=di("swT", [L, 8, 128, 128]), sbs=di("sbs", [L, 1, 1024]),
            fcw=di("fcw", [L, 128, 86, 3]), fcb=di("fcb", [L, 128, 86]),
        )
        self.out = nc.dram_tensor("outT", [D, S], F32, kind="ExternalOutput").ap()
        kind = "ExternalOutput" if dbg else "Internal"
        ds = lambda n, s, dt: nc.dram_tensor(n, s, dt, kind=kind).ap()
        self.Sx = dict(
            xres=ds("xres", [D, T], F32), LX=ds("LX", [512, T], F32), LG=ds("LG", [512, T], F32),
            QR=ds("QR", [1024, S], BF16), QW=ds("QW", [1024, T], BF16), KK=ds("KK", [1024, T], BF16),
            VV=ds("VV", [T, 1024], BF16), SU=ds("SU", [512, T], BF16), SV=ds("SV", [T, 512], BF16),
            MIX=ds("MIX", [D, T], BF16),
        )
        self.tkd = {k: Tk("D_" + k) for k in self.Sx if k != "xres"}
        self.tkx = {(oc, tb): Tk("D_x%d_%d" % (oc, tb)) for oc in range(NK) for tb in range(5)}
        self.build()

    def clear_tokens(self):
        for t in self._persist():
            t.w = {}
            t.r = {}

    def _persist(self):
        out = list(self.tkx.values()) + list(self.tkd.values())
        for nm in ("tk_act",):
            if hasattr(self, nm):
                out += list(getattr(self, nm).values())
        for nm in ("tk_mod", "tk_cols", "tk_const", "tk_gains"):
            if hasattr(self, nm):
                out.append(getattr(self, nm))
        return out

    def settle_x(self, txb):
        deps = {t.dkey: (t.dsem, t.dcnt) for t in txb if t.dsem is not None}
        self.sc._wait("sp", deps)
        for t in self.tkx.values():
            t.w = {k: v for k, v in t.w.items() if k not in deps}

    def mm(self, out, lhsT, rhs, start, stop, reads, writes, inc=None):
        if inc is None:
            inc = stop
        self.sc.op("pe", lambda e: e.matmul(out, lhsT=lhsT, rhs=rhs, start=start, stop=stop),
                   reads=reads, writes=writes, partial=True, inc=inc)

    def act(self, out, in_, func, reads, writes, partial=False, **kw):
        self.sc.op("act", lambda e: e.activation(out=out, in_=in_, func=func, **kw), reads=reads, writes=writes,
                   partial=partial)

    def tt(self, out, in0, in1, op, reads, writes, partial=False, eng="dve"):
        self.sc.op(eng, lambda e: e.tensor_tensor(out=out, in0=in0, in1=in1, op=op), reads=reads, writes=writes,
                   partial=partial)

    def ts(self, out, in0, s1, s2, op0, op1, reads, writes, partial=False, eng="dve"):
        if op1 is None:
            self.sc.op(eng, lambda e: e.tensor_scalar(out=out, in0=in0, scalar1=s1, scalar2=None, op0=op0),
                       reads=reads, writes=writes, partial=partial)
        else:
            self.sc.op(eng, lambda e: e.tensor_scalar(out=out, in0=in0, scalar1=s1, scalar2=s2, op0=op0, op1=op1),
                       reads=reads, writes=writes, partial=partial)

    def stt(self, out, in0, scalar, in1, op0, op1, reads, writes, partial=False):
        self.sc.op("dve", lambda e: e.scalar_tensor_tensor(out=out, in0=in0, scalar=scalar, in1=in1, op0=op0, op1=op1),
                   reads=reads, writes=writes, partial=partial)

    def gelu(self, ph, out, src, n, reads, writes, tmp, tmp_tk, rows=slice(0, 128)):
        t = tmp[rows, 0:n]
        self.act(t, src, AF.Square, reads, [tmp_tk])
        self.ts(t, t, 0.044715, 1.0, ALU.mult, ALU.add, [tmp_tk], [tmp_tk])
        self.tt(t, t, src, ALU.mult, [tmp_tk] + list(reads), [tmp_tk])
        self.act(t, t, AF.Sigmoid, [tmp_tk], [tmp_tk], scale=GC)
        self.tt(out, t, src, ALU.mult, [tmp_tk] + list(reads), writes, partial=True)

    def build(self):
        nc, sc, I, Sx = self.nc, self.sc, self.I, self.Sx
        P0 = Phase(self, "g")
        self.P0 = P0
        self.actT = P0.sb("actT", [128, NK, T], BF16)
        self.tk_act = {(kc, tb): Tk("act%d_%d" % (kc, tb)) for kc in range(NK) for tb in range(5)}
        self.mod = P0.sb("mod", [128, L, 96, 2], F32)
        self.tk_mod = P0.tk("mod")
        self.cols = P0.sb("cols", [128, L, 4, NK, 2], F32)
        self.tk_cols = P0.tk("cols")
        self.ones32 = P0.sb("ones32", [128, 128], F32)
        self.onesb = P0.sb("onesb", [128, 128], BF16)
        self.tk_const = P0.tk("const")
        self.gm = P0.sb("gm", [128, L, NK], F32)
        self.gf = P0.sb("gf", [128, L, NK], F32)
        self.gfin = P0.sb("gfin", [128, NK], F32)
        self.zero1 = P0.sb("zero1", [128, 1], F32)
        sc.op("dve", lambda e: e.memset(self.ones32[:], 1.0), writes=[self.tk_const], partial=True)
        sc.op("dve", lambda e: e.memset(self.onesb[:], 1.0), writes=[self.tk_const], partial=True)
        sc.op("dve", lambda e: e.memset(self.zero1[:], 0.0), writes=[self.tk_const], partial=True)
        tg = P0.tk("gains")
        sc.dma("sp", self.gm[:], I["gmix"].rearrange("l p k -> p l k"), writes=[tg], semt=tg, partial=True)
        sc.dma("sp", self.gf[:], I["gffn"].rearrange("l p k -> p l k"), writes=[tg], semt=tg, partial=True)
        sc.dma("sp", self.gfin[:], I["gfin"], writes=[tg], semt=tg, partial=True)
        self.tk_gains = tg
        for oc in range(NK):
            for tb, (t0, n) in enumerate(TBS):
                tkx = self.tkx[(oc, tb)]
                sc.dma("sp", Sx["xres"][oc * 128:(oc + 1) * 128, t0:t0 + n], I["xT0"][oc * 128:(oc + 1) * 128, t0:t0 + n],
                       writes=[tkx], semt=tg)
        import os
        self.adaln_setup()
        self.adaln_standalone(0)
        for l in range(self.nl):
            self.layer(l)
        if os.environ.get("KFIN", "1") == "1":
            self.norm(None, final=True)
        P0.__exit__()
        sc.barrier()

    def adaln_setup(self):
        sc, I, P0 = self.sc, self.I, self.P0
        cs32 = P0.sb("cs32", [128, NK, 2], F32)
        self.csb = P0.sb("csb", [128, NK, 2], BF16)
        self.tcs = P0.tk("cs")
        self.bT = P0.sb("bT", [128, L, 96], F32)
        self.tbT = P0.tk("bT")
        sc.dma("sp", cs32[:], I["cT"], writes=[self.tcs], semt=self.tcs)
        sc.dma("sp", self.bT[:], I["b_adaT"].rearrange("l p k -> p l k"), writes=[self.tbT], semt=self.tbT)
        self.act(self.csb[:], cs32[:], AF.Silu, [self.tcs], [self.tcs])

    def adaln_begin(self, ph, l):
        st = dict(l=l, wb=[ph.sb("aw%d" % i, [128, NK, 512], BF16) for i in range(2)],
                  twb=[ph.tk("aw%d" % i) for i in range(2)], ps=ph.psum("aps", [128, 512]), tps=ph.tk("aps"),
                  wv=self.I["w_ada"][l].rearrange("(kc p) n -> p kc n", p=128), slot=0)
        return st

    def adaln_slot(self, st):
        s_ = st["slot"]
        if s_ > 24:
            return False
        st["slot"] += 1
        if s_ < 24:
            i = s_ % 2
            self.sc.dma("pool", st["wb"][i][:], st["wv"][:, :, s_ * 512:(s_ + 1) * 512], writes=[st["twb"][i]],
                        semt=st["twb"][i])
        if s_ >= 1:
            nb = s_ - 1
            i = nb % 2
            for s4 in range(4):
                oc = nb * 4 + s4
                for kc in range(NK):
                    self.mm(st["ps"][:, oc * 2:oc * 2 + 2], st["wb"][i][:, kc, s4 * 128:(s4 + 1) * 128],
                            self.csb[:, kc, :], kc == 0, kc == NK - 1, [st["twb"][i], self.tcs], [st["tps"]])
        return True

    def adaln_finish(self, st):
        while self.adaln_slot(st):
            pass
        l, ps, tps = st["l"], st["ps"], st["tps"]
        for r in range(2):
            self.tt(self.mod[:, l, :, r], ps[:, r:192:2], self.bT[:, l, :], ALU.add, [tps, self.tbT], [self.tk_mod],
                    partial=True)
        for r in range(2):
            self.ts(self.cols[:, l, 0, :, r], self.mod[:, l, 16:32, r], 1.0, None, ALU.add, None,
                    [self.tk_mod], [self.tk_cols], partial=True)
            self.tt(self.cols[:, l, 0, :, r], self.cols[:, l, 0, :, r], self.gm[:, l, :], ALU.mult,
                    [self.tk_cols, self.tk_gains], [self.tk_cols])
            self.ts(self.cols[:, l, 2, :, r], self.mod[:, l, 64:80, r], 1.0, None, ALU.add, None,
                    [self.tk_mod], [self.tk_cols], partial=True)
            self.tt(self.cols[:, l, 2, :, r], self.cols[:, l, 2, :, r], self.gf[:, l, :], ALU.mult,
                    [self.tk_cols, self.tk_gains], [self.tk_cols])

    def adaln_standalone(self, l):
        with Phase(self, "ada") as ph:
            st = self.adaln_begin(ph, l)
            self.adaln_finish(st)

    def norm(self, lw, final=False):
        sc, Sx = self.sc, self.Sx
        with Phase(self, "nrm") as ph:
            xt = [ph.sb("xt%d" % i, [128, NK, 512], F32) for i in range(2)]
            txt = [ph.tk("xt%d" % i) for i in range(2)]
            sq = [ph.sb("sq%d" % i, [128, 512], F32) for i in range(2)]
            tsq = [ph.tk("sq%d" % i) for i in range(2)]
            rs = [ph.sb("rs%d" % i, [128, 512], F32) for i in range(2)]
            trs = [ph.tk("rs%d" % i) for i in range(2)]
            tmp = [ph.sb("tmp%d" % i, [128, 512], F32) for i in range(2)]
            ttmp = [ph.tk("tmp%d" % i) for i in range(2)]
            ps = [ph.psum("ps%d" % i, [128, 512]) for i in range(2)]
            tps = [ph.tk("ps%d" % i) for i in range(2)]
            xv = Sx["xres"].rearrange("(kc p) t -> p kc t", p=128)
            nblk = 4 if final else 5
            state = dict(cnt=0, sqc=0)

            def stage_a(tb):
                t0, n = TBS[tb]
                i = tb % 2
                sc.dma("sp", xt[i][:, :, 0:n], xv[:, :, t0:t0 + n], reads=[self.tkx[(kc, tb)] for kc in range(NK)],
                       writes=[txt[i]], semt=txt[i])
                for kc in range(NK):
                    j = state["sqc"] % 2
                    state["sqc"] += 1
                    self.act(sq[j][:, 0:n], xt[i][:, kc, 0:n], AF.Square, [txt[i]], [tsq[j]])
                    self.mm(ps[i][:, 0:n], self.ones32[:], sq[j][:, 0:n], kc == 0, kc == NK - 1, [tsq[j], self.tk_const],
                            [tps[i]], inc=True)
                self.ts(rs[i][:, 0:n], ps[i][:, 0:n], 1.0 / D, EPS, ALU.mult, ALU.add, [tps[i]], [trs[i]])
                self.act(rs[i][:, 0:n], rs[i][:, 0:n], AF.Sqrt, [trs[i]], [trs[i]])
                sc.op("dve", lambda e: e.reciprocal(out=rs[i][:, 0:n], in_=rs[i][:, 0:n]), reads=[trs[i]], writes=[trs[i]])

            def stage_b(tb):
                t0, n = TBS[tb]
                i = tb % 2
                r = 0 if tb < 4 else 1
                for kc in range(NK):
                    j = state["cnt"] % 2
                    state["cnt"] += 1
                    if final:
                        self.stt(tmp[j][:, 0:n], xt[i][:, kc, 0:n], self.gfin[:, kc:kc + 1], rs[i][:, 0:n], ALU.mult,
                                 ALU.mult, [txt[i], trs[i], self.tk_gains], [ttmp[j]])
                        sc.dma("sp", self.out[kc * 128:(kc + 1) * 128, t0:t0 + n], tmp[j][:, 0:n], reads=[ttmp[j]],
                               semt=ttmp[j])
                    else:
                        l, which = lw
                        self.stt(tmp[j][:, 0:n], xt[i][:, kc, 0:n], self.cols[:, l, which, kc, r:r + 1], rs[i][:, 0:n],
                                 ALU.mult, ALU.mult, [txt[i], trs[i], self.tk_cols], [ttmp[j]])
                        sh = 0 if which == 0 else 48
                        self.act(self.actT[:, kc, t0:t0 + n], tmp[j][:, 0:n], AF.Identity, [ttmp[j], self.tk_mod],
                                 [self.tk_act[(kc, tb)]], bias=self.mod[:, l, sh + kc, r:r + 1])

            stage_a(0)
            for tb in range(nblk):
                if tb + 1 < nblk:
                    stage_a(tb + 1)
                stage_b(tb)

    def layer(self, l):
        import os
        stop = int(os.environ.get("KSTOP", "99"))
        steps = [lambda: self.norm((l, 0)), lambda: self.proj_in(l), lambda: self.lru(l), lambda: self.attn(l),
                 lambda: self.sgu(l), lambda: self.proj_res(l, "w_out", 32), lambda: self.norm((l, 2)),
                 lambda: self.ffn(l)]
        for i, st in enumerate(steps):
            if i < stop:
                st()

    def proj_in(self, l):
        sc, I, Sx, tkd = self.sc, self.I, self.Sx, self.tkd
        with Phase(self, "pin") as ph:
            wb = [ph.sb("w%d" % i, [128, NK, 512], BF16) for i in range(2)]
            twb = [ph.tk("w%d" % i) for i in range(2)]
            cos = ph.sb("cos", [128, S], F32)
            sin = ph.sb("sin", [128, S], F32)
            perm = ph.sb("perm", [128, 128], BF16)
            tc_ = ph.tk("c")
            sc.dma("sp", cos[:], I["cos"], writes=[tc_], semt=tc_, partial=True)
            sc.dma("sp", sin[:], I["sin"], writes=[tc_], semt=tc_, partial=True)
            tpm = ph.tk("perm")
            sc.dma("pool", perm[:], I["perm"], writes=[tpm], semt=tpm)
            lng = ph.sb("lng", [128, 512], F32)
            lnb = ph.sb("lnb", [128, 512], F32)
            tln = ph.tk("ln")
            sc.dma("sp", lng[:], I["slng"][l:l + 1, :].broadcast_to([128, 512]), writes=[tln], semt=tln, partial=True)
            sc.dma("sp", lnb[:], I["slnb"][l:l + 1, :].broadcast_to([128, 512]), writes=[tln], semt=tln, partial=True)
            st32 = [ph.sb("st32_%d" % i, [128, T], F32) for i in range(2)]
            tst32 = [ph.tk("st32_%d" % i) for i in range(2)]
            stb = [ph.sb("stb%d" % i, [128, T], BF16) for i in range(4)]
            tstb = [ph.tk("stb%d" % i) for i in range(4)]
            t1 = [ph.sb("t1_%d" % i, [128, 512], F32) for i in range(2)]
            tt1 = [ph.tk("t1_%d" % i) for i in range(2)]
            t2 = [ph.sb("t2_%d" % i, [128, 512], F32) for i in range(2)]
            tt2 = [ph.tk("t2_%d" % i) for i in range(2)]
            vst = [ph.sb("vst%d" % i, [128, 512], BF16) for i in range(2)]
            tvst = [ph.tk("vst%d" % i) for i in range(2)]
            stat = ph.sb("stat", [128, 8], F32)
            tstat = ph.tk("stat")
            ps = [ph.psum("ps%d" % i, [128, 512]) for i in range(4)]
            tps = [ph.tk("ps%d" % i) for i in range(4)]
            psw = [ph.psum("psw%d" % i, [128, 512]) for i in range(2)]
            tpsw = [ph.tk("psw%d" % i) for i in range(2)]
            wv = I["w_in"][l].rearrange("(kc p) n -> p kc n", p=128)
            pc = 0
            c32 = 0
            cb16 = 0
            cq = 0
            import os
            blks = [int(x) for x in os.environ.get("KBLK", "0,1,2,3,4,5,6,7,8,9").split(",")]
            for blk in blks:
                i = blk % 2
                sc.dma("pool", wb[i][:], wv[:, :, blk * 512:(blk + 1) * 512], writes=[twb[i]], semt=twb[i])
                if blk in (6, 7, 9):
                    for tt_ in range(18):
                        p = pc % 4
                        pc += 1
                        tb = min(tt_ // 4, 4)
                        for kc in range(NK):
                            self.mm(ps[p][:], self.actT[:, kc, tt_ * 128:(tt_ + 1) * 128], wb[i][:, kc, :], kc == 0,
                                    kc == NK - 1, [self.tk_act[(kc, tb)], twb[i]], [tps[p]])
                        j = tt_ % 2
                        if blk in (6, 7):
                            self.act(vst[j][:], ps[p][:], AF.Copy, [tps[p]], [tvst[j]])
                            c0 = (blk - 6) * 512
                            sc.dma("sp", Sx["VV"][tt_ * 128:(tt_ + 1) * 128, c0:c0 + 512], vst[j][:], reads=[tvst[j]],
                                   writes=[tkd["VV"]], semt=tvst[j], partial=True)
                        else:
                            g = t1[j]
                            self.act(g[:], ps[p][:], AF.Copy, [tps[p]], [tt1[j]])
                            self.gelu(ph, g[:], g[:], 512, [tt1[j]], [tt1[j]], t2[j], tt2[j])
                            sc.op("dve", lambda e: e.bn_stats(out=stat[:, 0:6], in_=g[:]), reads=[tt1[j]], writes=[tstat])
                            sc.op("dve", lambda e: e.bn_aggr(out=stat[:, 6:8], in_=stat[:, 0:6]), reads=[tstat],
                                  writes=[tstat])
                            self.ts(stat[:, 7:8], stat[:, 7:8], EPS, None, ALU.add, None, [tstat], [tstat])
                            self.act(stat[:, 7:8], stat[:, 7:8], AF.Sqrt, [tstat], [tstat])
                            sc.op("dve", lambda e: e.reciprocal(out=stat[:, 7:8], in_=stat[:, 7:8]), reads=[tstat],
                                  writes=[tstat])
                            self.ts(g[:], g[:], stat[:, 6:7], stat[:, 7:8], ALU.subtract, ALU.mult, [tt1[j], tstat],
                                    [tt1[j]])
                            self.tt(g[:], g[:], lng[:], ALU.mult, [tt1[j], tln], [tt1[j]])
                            self.tt(vst[j][:], g[:], lnb[:], ALU.add, [tt1[j], tln], [tvst[j]])
                            sc.dma("sp", Sx["SV"][tt_ * 128:(tt_ + 1) * 128, :], vst[j][:], reads=[tvst[j]],
                                   writes=[tkd["SV"]], semt=tvst[j], partial=True)
                    continue
                for s4 in range(4):
                    col = blk * 512 + s4 * 128
                    kind = ("lx", "lg", "q", "q", "k", "k", None, None, "su")[blk]
                    if kind in ("lx", "lg"):
                        so = st32[c32 % 2]
                        tso = tst32[c32 % 2]
                        c32 += 1
                    elif kind == "su":
                        so = stb[cb16 % 4]
                        tso = tstb[cb16 % 4]
                        cb16 += 1
                    else:
                        so = stb[cb16 % 4]
                        tso = tstb[cb16 % 4]
                        sr = stb[(cb16 + 1) % 4]
                        tsr = tstb[(cb16 + 1) % 4]
                        cb16 += 2
                    for tb, (t0, n) in enumerate(TBS):
                        p = pc % 4
                        pc += 1
                        for kc in range(NK):
                            self.mm(ps[p][:, 0:n], wb[i][:, kc, s4 * 128:(s4 + 1) * 128], self.actT[:, kc, t0:t0 + n],
                                    kc == 0, kc == NK - 1, [self.tk_act[(kc, tb)], twb[i]], [tps[p]])
                        if kind in ("lx", "lg"):
                            self.act(so[:, t0:t0 + n], ps[p][:, 0:n], AF.Copy, [tps[p]], [tso], partial=True)
                        elif kind == "su":
                            j = cq % 2
                            cq += 1
                            self.act(t1[j][:, 0:n], ps[p][:, 0:n], AF.Copy, [tps[p]], [tt1[j]])
                            self.gelu(ph, so[:, t0:t0 + n], t1[j][:, 0:n], n, [tt1[j]], [tso], t2[j], tt2[j])
                        else:
                            sclq = 0.125 if kind == "q" else 1.0
                            self.act(so[:, t0:t0 + n], ps[p][:, 0:n], AF.Copy, [tps[p]], [tso], partial=True, scale=sclq)
                            if tb < 4 and os.environ.get("KQ", "") != "raw":
                                j = cq % 2
                                cq += 1
                                self.act(t1[j][:, 0:n], ps[p][:, 0:n], AF.Copy, [tps[p]], [tt1[j]], scale=sclq)
                                self.tt(t1[j][:, 0:n], t1[j][:, 0:n], cos[:, t0:t0 + n], ALU.mult, [tt1[j], tc_], [tt1[j]])
                                self.mm(psw[j][:, 0:n], perm[:], so[:, t0:t0 + n], True, True, [tso, tpm], [tpsw[j]])
                                self.act(t2[j][:, 0:n], psw[j][:, 0:n], AF.Copy, [tpsw[j]], [tt2[j]])
                                self.tt(t2[j][:, 0:n], t2[j][:, 0:n], sin[:, t0:t0 + n], ALU.mult, [tt2[j], tc_], [tt2[j]])
                                self.tt(sr[:, t0:t0 + n], t1[j][:, 0:n], t2[j][:, 0:n], ALU.add, [tt1[j], tt2[j]],
                                        [tsr], partial=True)
                    if kind == "lx":
                        sc.dma("sp", Sx["LX"][s4 * 128:(s4 + 1) * 128, :], so[:], reads=[tso], writes=[tkd["LX"]], semt=tso,
                               partial=True)
                    elif kind == "lg":
                        sc.dma("sp", Sx["LG"][s4 * 128:(s4 + 1) * 128, :], so[:], reads=[tso], writes=[tkd["LG"]], semt=tso,
                               partial=True)
                    elif kind == "su":
                        sc.dma("sp", Sx["SU"][s4 * 128:(s4 + 1) * 128, :], so[:], reads=[tso], writes=[tkd["SU"]], semt=tso,
                               partial=True)
                    elif kind == "q":
                        r0 = (blk - 2) * 512 + s4 * 128
                        sc.dma("sp", Sx["QW"][r0:r0 + 128, :], so[:], reads=[tso], writes=[tkd["QW"]], semt=tso, partial=True)
                        if os.environ.get("KQ", "") == "":
                            sc.dma("sp", Sx["QR"][r0:r0 + 128, :], sr[:, 0:S], reads=[tsr], writes=[tkd["QR"]], semt=tsr,
                                   partial=True)
                    elif kind == "k":
                        r0 = (blk - 4) * 512 + s4 * 128
                        sc.dma("sp", Sx["KK"][r0:r0 + 128, S:T], so[:, S:T], reads=[tso], writes=[tkd["KK"]], semt=tso,
                               partial=True)
                        sc.dma("sp", Sx["KK"][r0:r0 + 128, 0:S], sr[:, 0:S], reads=[tsr], writes=[tkd["KK"]], semt=tsr,
                               partial=True)

    def lru(self, l):
        sc, I, Sx, tkd = self.sc, self.I, self.Sx, self.tkd
        with Phase(self, "lru") as ph:
            names = ["X", "Y", "XC", "Rr", "Ri", "A", "Q", "Bv", "H0", "H1"]
            tl = {n_: ph.sb(n_, [128, T], F32) for n_ in names}
            tkl = {n_: ph.tk(n_) for n_ in names}
            ob = ph.sb("ob", [128, T], BF16)
            tob = ph.tk("ob")
            bd = ph.sb("bd", [128, 16, 128], F32)
            tbd = ph.tk("bd")
            cw = ph.sb("cw", [128, 4, 4], F32)
            cb = ph.sb("cb", [128, 4], F32)
            gb = ph.sb("gb", [128, 2, 2, 4], F32)
            lam = ph.sb("lam", [128, 2, 4], F32)
            cl = ph.sb("cl", [128, 2, 4], F32)
            tpar = ph.tk("par")
            tcl = ph.tk("cl")
            ps = [ph.psum("ps%d" % i, [128, 512]) for i in range(2)]
            tps = [ph.tk("ps%d" % i) for i in range(2)]
            sc.op("dve", lambda e: e.memset(bd[:], 0.0), writes=[tbd])
            for d in range(2):
                for j in range(2):
                    for c in range(4):
                        for hb in range(2):
                            sc.dma("sp", bd[hb * 64:(hb + 1) * 64, (d * 2 + j) * 4 + c, hb * 64:(hb + 1) * 64],
                                   I["lgw"][l, d, j, 2 * c + hb], writes=[tbd], semt=tbd, partial=(not (d == 0 and j == 0 and c == 0 and hb == 0)))
            for dst, src in ((cw, "lcw"), (cb, "lcb"), (gb, "lgb"), (lam, "llam")):
                sc.dma("sp", dst[:], I[src][l], writes=[tpar], semt=tpar, partial=True)
            self.act(cl[:], lam[:], AF.Exp, [tpar], [tcl], scale=-1.0)
            self.act(cl[:], cl[:], AF.Ln, [tcl], [tcl], bias=1.0)
            self.ts(cl[:], cl[:], -8.0, None, ALU.mult, None, [tcl], [tcl])
            X, Y, XC, Rr, Ri, A, Q, Bv, H0, H1 = [tl[n_] for n_ in names]
            pcnt = 0
            for c in range(4):
                sc.dma("sp", X[:], Sx["LX"][c * 128:(c + 1) * 128, :], reads=[tkd["LX"]], writes=[tkl["X"]], semt=tkl["X"])
                sc.dma("sp", Y[:], Sx["LG"][c * 128:(c + 1) * 128, :], reads=[tkd["LG"]], writes=[tkl["Y"]], semt=tkl["Y"])
                self.ts(XC[:], X[:], cw[:, c, 2:3], cb[:, c:c + 1], ALU.mult, ALU.add, [tkl["X"], tpar], [tkl["XC"]])
                for (r0, n) in ((0, S), (S, CL)):
                    for k in (0, 1, 3):
                        off = k - 2
                        d0 = max(0, -off)
                        d1 = n - max(0, off)
                        self.stt(XC[:, r0 + d0:r0 + d1], X[:, r0 + d0 + off:r0 + d1 + off], cw[:, c, k:k + 1],
                                 XC[:, r0 + d0:r0 + d1], ALU.mult, ALU.add, [tkl["X"], tkl["XC"], tpar], [tkl["XC"]])
                for d in range(2):
                    for j, Rt, trt in ((0, Rr, tkl["Rr"]), (1, Ri, tkl["Ri"])):
                        for tb, (t0, n) in enumerate(TBS):
                            p = pcnt % 2
                            pcnt += 1
                            self.mm(ps[p][:, 0:n], bd[:, (d * 2 + j) * 4 + c, :], XC[:, t0:t0 + n], True, True,
                                    [tbd, tkl["XC"]], [tps[p]])
                            self.act(Rt[:, t0:t0 + n], ps[p][:, 0:n], AF.Sigmoid, [tps[p], tpar], [trt],
                                     partial=(tb > 0), bias=gb[:, d, j, c:c + 1])
                    self.act(A[:], Rr[:], AF.Exp, [tkl["Rr"], tcl], [tkl["A"]], scale=cl[:, d, c:c + 1])
                    self.act(Q[:], A[:], AF.Square, [tkl["A"]], [tkl["Q"]])
                    self.act(Q[:], Q[:], AF.Sqrt, [tkl["Q"]], [tkl["Q"]], scale=-1.0, bias=1.0)
                    self.tt(Bv[:], Ri[:], XC[:], ALU.mult, [tkl["Ri"], tkl["XC"]], [tkl["Bv"]])
                    self.tt(Bv[:], Bv[:], Q[:], ALU.mult, [tkl["Bv"], tkl["Q"]], [tkl["Bv"]])
                    if d == 0:
                        sc.op("dve", lambda e: e.tensor_tensor_scan(out=H0[:, S:T], data0=A[:, S:T], data1=Bv[:, S:T],
                                                                    initial=0.0, op0=ALU.mult, op1=ALU.add),
                              reads=[tkl["A"], tkl["Bv"]], writes=[tkl["H0"]])
                        sc.op("dve", lambda e: e.tensor_tensor_scan(out=H0[:, 0:S], data0=A[:, 0:S], data1=Bv[:, 0:S],
                                                                    initial=H0[:, T - 1:T], op0=ALU.mult, op1=ALU.add),
                              reads=[tkl["A"], tkl["Bv"], tkl["H0"]], writes=[tkl["H0"]], partial=True)
                    else:
                        sc.op("dve", lambda e: e.tensor_tensor_scan(out=H1[:, S:T][:, ::-1], data0=A[:, S:T][:, ::-1],
                                                                    data1=Bv[:, S:T][:, ::-1], initial=0.0,
                                                                    op0=ALU.mult, op1=ALU.add),
                              reads=[tkl["A"], tkl["Bv"]], writes=[tkl["H1"]])
                        sc.op("dve", lambda e: e.tensor_tensor_scan(out=H1[:, 0:S][:, ::-1], data0=A[:, 0:S][:, ::-1],
                                                                    data1=Bv[:, 0:S][:, ::-1], initial=H1[:, S:S + 1],
                                                                    op0=ALU.mult, op1=ALU.add),
                              reads=[tkl["A"], tkl["Bv"], tkl["H1"]], writes=[tkl["H1"]], partial=True)
                self.tt(H0[:], H0[:], H1[:], ALU.add, [tkl["H0"], tkl["H1"]], [tkl["H0"]])
                self.gelu(ph, H1[:], Y[:], T, [tkl["Y"]], [tkl["H1"]], Q, tkl["Q"])
                self.tt(ob[:], H0[:], H1[:], ALU.mult, [tkl["H0"], tkl["H1"]], [tob])
                sc.dma("sp", Sx["MIX"][c * 128:(c + 1) * 128, :], ob[:], reads=[tob], writes=[tkd["MIX"]], semt=tob,
                       partial=True)

    def attn(self, l):
        sc, I, Sx, tkd = self.sc, self.I, self.Sx, self.tkd
        need_ctx = l < L - 1
        with Phase(self, "att") as ph:
            NB = 2
            qr = [ph.sb("qr%d" % i, [128, S], BF16) for i in range(NB)]
            qw = [ph.sb("qw%d" % i, [128, T], BF16) for i in range(NB)]
            kk = [ph.sb("kk%d" % i, [128, T], BF16) for i in range(NB)]
            vv = [ph.sb("vv%d" % i, [128, 18, 128], BF16) for i in range(NB)]
            tin = [ph.tk("in%d" % i) for i in range(NB)]
            bias = [ph.sb("bias%d" % i, [128, 21, 128], F32) for i in range(2)]
            tbias = [ph.tk("bias%d" % i) for i in range(2)]
            ein = [ph.sb("ein%d" % i, [128, 640], F32) for i in range(2)]
            tein = [ph.tk("ein%d" % i) for i in range(2)]
            pt = [ph.sb("pt%d" % i, [128, 896], BF16) for i in range(2)]
            tpt = [ph.tk("pt%d" % i) for i in range(2)]
            rd = [ph.sb("rd%d" % i, [128, 256], F32) for i in range(2)]
            trd = [ph.tk("rd%d" % i) for i in range(2)]
            ob = [ph.sb("ob%d" % i, [128, T], BF16) for i in range(2)]
            tob = [ph.tk("ob%d" % i) for i in range(2)]
            psS = [ph.psum("S%d" % i, [128, 1024]) for i in range(2)]
            tpsS = [ph.tk("S%d" % i) for i in range(2)]
            psO = [ph.psum("O%d" % i, [128, 512]) for i in range(2)]
            tpsO = [ph.tk("O%d" % i) for i in range(2)]
            cnt = 0
            hcnt = 0
            ada = self.adaln_begin(ph, l + 1) if l + 1 < self.nl else None
            for hp in range(8):
                ib = hp % NB
                r0 = hp * 128
                t_ = tin[ib]
                sc.dma("sp", qr[ib][:], Sx["QR"][r0:r0 + 128, :], reads=[tkd["QR"]], writes=[t_], semt=t_)
                sc.dma("sp", qw[ib][:], Sx["QW"][r0:r0 + 128, :], reads=[tkd["QW"]], writes=[t_], semt=t_, partial=True)
                sc.dma("sp", kk[ib][:], Sx["KK"][r0:r0 + 128, :], reads=[tkd["KK"]], writes=[t_], semt=t_, partial=True)
                sc.dma("sp", vv[ib][:], Sx["VV"][:, r0:r0 + 128].rearrange("(t p) c -> p t c", p=128), reads=[tkd["VV"]],
                       writes=[t_], semt=t_, partial=True)
                o_ = ob[hp % 2]
                to_ = tob[hp % 2]
                its = []
                for hh in range(2):
                    h = hp * 2 + hh
                    bi = hh
                    first = True
                    for (slot, nt, rho0, which) in NA_PATTERNS:
                        for a in range(2):
                            for b in range(2):
                                rs_ = rho0 + a - b
                                src = I["natab"][l, h, which, rs_:rs_ + 2 * nt - 1:2, :, :].rearrange("t k q -> k t q")
                                sc.dma("sp", bias[bi][a * 64:(a + 1) * 64, slot:slot + nt, b * 64:(b + 1) * 64], src,
                                       writes=[tbias[bi]], semt=tbias[bi], partial=(not first))
                                first = False
                    for m in range(16):
                        its.append((hh, m))
                    if need_ctx:
                        its.append((hh, -1))
                    else:
                        pb = hh * 64
                        sc.op("dve", lambda e: e.memset(o_[pb:pb + 64, S:T], 0.0), writes=[to_], partial=True)

                def front(k):
                    hh, m = its[k]
                    pb = hh * 64
                    bi = hh
                    si = (cnt0 + k) % 2
                    S_ = psS[si]
                    if m >= 0:
                        tiles, slot, _, _ = na_tiles(m)
                        nt = len(tiles)
                        for i_, t in enumerate(tiles):
                            self.mm(S_[:, i_ * 128:(i_ + 1) * 128], kk[ib][pb:pb + 64, t * 128:(t + 1) * 128],
                                    qr[ib][pb:pb + 64, m * 128:(m + 1) * 128], True, True, [t_], [tpsS[si]], inc=False)
                        for j in range(2):
                            self.mm(S_[:, (nt + j) * 128:(nt + j + 1) * 128], kk[ib][pb:pb + 64, S + j * 128:S + (j + 1) * 128],
                                    qw[ib][pb:pb + 64, m * 128:(m + 1) * 128], True, True, [t_], [tpsS[si]], inc=(j == 1))
                        self.tt(ein[si][:, 0:nt * 128].rearrange("p (t q) -> p t q", q=128),
                                S_[:, 0:nt * 128].rearrange("p (t q) -> p t q", q=128),
                                bias[bi][:, slot:slot + nt, :], ALU.add, [tpsS[si], tbias[bi]], [tein[si]])
                        self.act(pt[si][:, 0:nt * 128], ein[si][:, 0:nt * 128], AF.Exp, [tein[si]], [tpt[si]])
                        self.act(pt[si][:, nt * 128:(nt + 2) * 128], S_[:, nt * 128:(nt + 2) * 128], AF.Exp,
                                 [tpsS[si], tein[si]], [tpt[si]], partial=True)
                    else:
                        for j in range(2):
                            self.mm(S_[:, j * 256:(j + 1) * 256], kk[ib][pb:pb + 64, S + j * 128:S + (j + 1) * 128],
                                    qw[ib][pb:pb + 64, S:T], True, True, [t_], [tpsS[si]], inc=(j == 1))
                        self.act(pt[si][:, 0:512], S_[:, 0:512], AF.Exp, [tpsS[si]], [tpt[si]])

                def back(k):
                    hh, m = its[k]
                    pb = hh * 64
                    si = (cnt0 + k) % 2
                    O_ = psO[si]
                    if m >= 0:
                        tiles, slot, _, _ = na_tiles(m)
                        vt = tiles + [16, 17]
                        for i_, t in enumerate(vt):
                            self.mm(O_[:, 0:128], vv[ib][:, t, :], pt[si][:, i_ * 128:(i_ + 1) * 128], i_ == 0,
                                    i_ == len(vt) - 1, [t_, tpt[si]], [tpsO[si]], inc=False)
                        for i_ in range(len(vt)):
                            self.mm(O_[:, 128:256], self.onesb[:], pt[si][:, i_ * 128:(i_ + 1) * 128], i_ == 0,
                                    i_ == len(vt) - 1, [self.tk_const, tpt[si]], [tpsO[si]], inc=(i_ == len(vt) - 1))
                        sc.op("dve", lambda e: e.reciprocal(out=rd[si][pb:pb + 64, 0:128], in_=O_[pb:pb + 64, 128:256]),
                              reads=[tpsO[si]], writes=[trd[si]])
                        self.tt(o_[pb:pb + 64, m * 128:(m + 1) * 128], O_[pb:pb + 64, 0:128], rd[si][pb:pb + 64, 0:128],
                                ALU.mult, [tpsO[si], trd[si]], [to_], partial=True)
                    else:
                        for j in range(2):
                            self.mm(O_[:, 0:256], vv[ib][:, 16 + j, :], pt[si][:, j * 256:(j + 1) * 256], j == 0, j == 1,
                                    [t_, tpt[si]], [tpsO[si]], inc=False)
                        for j in range(2):
                            self.mm(O_[:, 256:512], self.onesb[:], pt[si][:, j * 256:(j + 1) * 256], j == 0, j == 1,
                                    [self.tk_const, tpt[si]], [tpsO[si]], inc=(j == 1))
                        sc.op("dve", lambda e: e.reciprocal(out=rd[si][pb:pb + 64, 0:256], in_=O_[pb:pb + 64, 256:512]),
                              reads=[tpsO[si]], writes=[trd[si]])
                        self.tt(o_[pb:pb + 64, S:T], O_[pb:pb + 64, 0:256], rd[si][pb:pb + 64, 0:256], ALU.mult,
                                [tpsO[si], trd[si]], [to_], partial=True)

                cnt0 = cnt
                front(0)
                for k in range(len(its)):
                    if k + 1 < len(its):
                        front(k + 1)
                    back(k)
                    if ada is not None and (cnt0 + k) % 10 == 5:
                        self.adaln_slot(ada)
                cnt += len(its)
                mr = 512 + hp * 128
                sc.dma("sp", Sx["MIX"][mr:mr + 128, :], o_[:], reads=[to_], writes=[tkd["MIX"]], semt=to_, partial=True)
            if ada is not None:
                self.adaln_finish(ada)

    def sgu(self, l):
        sc, I, Sx, tkd = self.sc, self.I, self.Sx, self.tkd
        with Phase(self, "sgu") as ph:
            su = ph.sb("su", [128, 4, T], BF16)
            sv = ph.sb("sv", [128, 18, 512], BF16)
            tin = ph.tk("in")
            wT = ph.sb("wT", [128, 8, 128], BF16)
            twT = ph.tk("wT")
            bs = ph.sb("bs", [1, 1024], F32)
            tbs = ph.tk("bs")
            ob = ph.sb("ob", [128, 4, T], BF16)
            tob = ph.tk("ob")
            ps = [ph.psum("ps%d" % i, [128, 512]) for i in range(2)]
            tps = [ph.tk("ps%d" % i) for i in range(2)]
            sc.dma("sp", su[:], Sx["SU"].rearrange("(c p) t -> p c t", p=128), reads=[tkd["SU"]], writes=[tin], semt=tin)
            sc.dma("sp", sv[:], Sx["SV"].rearrange("(n p) c -> p n c", p=128), reads=[tkd["SV"]], writes=[tin], semt=tin,
                   partial=True)
            sc.dma("pool", wT[:], I["swT"][l].rearrange("g q p -> q g p"), writes=[twT], semt=twT)
            sc.dma("sp", bs[:], I["sbs"][l], writes=[tbs], semt=tbs)
            cnt = 0
            for n_ in range(18):
                for gp in range(4):
                    p = cnt % 2
                    cnt += 1
                    for hh in range(2):
                        g = gp * 2 + hh
                        o = ps[p][:, hh * 128:(hh + 1) * 128]
                        self.mm(o, sv[:, n_, gp * 128:(gp + 1) * 128], wT[:, g, :], True, False, [tin, twT], [tps[p]],
                                inc=False)
                        self.mm(o, self.ones32[0:1, :], bs[0:1, g * 128:(g + 1) * 128], False, True,
                                [self.tk_const, tbs], [tps[p]], inc=(hh == 1))
                    for hh in range(2):
                        pb = hh * 64
                        self.tt(ob[pb:pb + 64, gp, n_ * 128:(n_ + 1) * 128], ps[p][pb:pb + 64, hh * 128:(hh + 1) * 128],
                                su[pb:pb + 64, gp, n_ * 128:(n_ + 1) * 128], ALU.mult, [tps[p], tin], [tob], partial=True)
            sc.dma("sp", Sx["MIX"][1536:2048, :].rearrange("(c p) t -> p c t", p=128), ob[:], reads=[tob],
                   writes=[tkd["MIX"]], semt=tob, partial=True)

    def proj_res(self, l, wname, gate_base):
        sc, I, Sx, tkd = self.sc, self.I, self.Sx, self.tkd
        last = (l == L - 1)
        with Phase(self, "pr") as ph:
            wb = [ph.sb("w%d" % i, [128, NK, 512], BF16) for i in range(2)]
            twb = [ph.tk("w%d" % i) for i in range(2)]
            xb = [ph.sb("xb%d" % i, [128, 512], F32) for i in range(6)]
            txb = [ph.tk("xb%d" % i) for i in range(6)]
            ps = [ph.psum("ps%d" % i, [128, 512]) for i in range(4)]
            tps = [ph.tk("ps%d" % i) for i in range(4)]
            tload = ph.tk("ld")
            for kc in range(NK):
                sc.dma("sp", self.actT[:, kc, :], Sx["MIX"][kc * 128:(kc + 1) * 128, :], reads=[tkd["MIX"]],
                       writes=[self.tk_act[(kc, tb)] for tb in range(5)], semt=tload)
            wv = I[wname][l].rearrange("(kc p) n -> p kc n", p=128)
            cnt = 0
            for cb in range(4):
                i = cb % 2
                sc.dma("pool", wb[i][:], wv[:, :, cb * 512:(cb + 1) * 512], writes=[twb[i]], semt=twb[i])
                for s4 in range(4):
                    oc = cb * 4 + s4
                    for tb, (t0, n) in enumerate(TBS):
                        if last and tb == 4:
                            continue
                        p = cnt % 4
                        xi = cnt % 6
                        cnt += 1
                        tkx = self.tkx[(oc, tb)]
                        sc.dma("sp", xb[xi][:, 0:n], Sx["xres"][oc * 128:(oc + 1) * 128, t0:t0 + n], reads=[tkx],
                               writes=[txb[xi]], semt=txb[xi])
                        for kc in range(NK):
                            self.mm(ps[p][:, 0:n], wb[i][:, kc, s4 * 128:(s4 + 1) * 128], self.actT[:, kc, t0:t0 + n],
                                    kc == 0, kc == NK - 1, [self.tk_act[(kc, tb)], twb[i]], [tps[p]])
                        r = 0 if tb < 4 else 1
                        self.stt(xb[xi][:, 0:n], ps[p][:, 0:n], self.mod[:, l, gate_base + oc, r:r + 1], xb[xi][:, 0:n],
                                 ALU.mult, ALU.add, [tps[p], txb[xi], self.tk_mod], [txb[xi]])
                        sc.dma("act", Sx["xres"][oc * 128:(oc + 1) * 128, t0:t0 + n], xb[xi][:, 0:n], reads=[txb[xi]],
                               writes=[tkx], semt=txb[xi])

    def ffn(self, l):
        sc, I, Sx = self.sc, self.I, self.Sx
        last = (l == L - 1)
        nblk = 4 if last else 5
        Tn = S if last else T
        with Phase(self, "ffn") as ph:
            GJ = 6
            wu = [ph.sb("wu%d" % i, [128, 2, NK, 256], BF16) for i in range(2)]
            twu = [ph.tk("wu%d" % i) for i in range(2)]
            wd = ph.sb("wd", [128, GJ, D], BF16)
            twd = ph.tk("wd")
            ag = ph.sb("ag", [128, GJ, T], BF16)
            tag = [ph.tk("ag%d" % i) for i in range(GJ)]
            E = ph.sb("E", [128, 5, 2], F32)
            tE = ph.tk("E")
            Ya = ph.sb("Ya", [128, T], F32)
            tYa = ph.tk("Ya")
            Yg = ph.sb("Yg", [128, T], F32)
            tYg = ph.tk("Yg")
            fcw = ph.sb("fcw", [128, 86, 3], F32)
            fcb = ph.sb("fcb", [128, 86], F32)
            tpar = ph.tk("par")
            xb = [ph.sb("xb%d" % i, [128, 512], F32) for i in range(6)]
            txb = [ph.tk("xb%d" % i) for i in range(6)]
            ps = [ph.psum("ps%d" % i, [128, 512]) for i in range(4)]
            tps = [ph.tk("ps%d" % i) for i in range(4)]
            sc.dma("sp", fcw[:], I["fcw"][l], writes=[tpar], semt=tpar, partial=True)
            sc.dma("sp", fcb[:], I["fcb"][l], writes=[tpar], semt=tpar, partial=True)
            uv = I["ffn_up"][l].rearrange("(kc p) n -> p kc n", p=128)
            dv = I["ffn_down"][l].rearrange("(j p) n -> p j n", p=128)
            ranges = [(0, S)] if last else [(0, S), (S, CL)]
            pc = 0
            xc = 0
            blocks = []
            j_done = 0
            gi = 0
            while j_done < NJ:
                gj = min(GJ, NJ - j_done)
                jj = 0
                while jj < gj:
                    nb = min(2, gj - jj)
                    blocks.append(dict(j0=j_done + jj, nb=nb, jj=jj, g=gi, gj=gj, gstart=j_done, first=(jj == 0),
                                       last=(jj + nb >= gj)))
                    jj += nb
                j_done += gj
                gi += 1

            def load_wu(bidx):
                bk = blocks[bidx]
                w_ = wu[bidx % 2]
                tw_ = twu[bidx % 2]
                j0, nb = bk["j0"], bk["nb"]
                sc.dma("pool", w_[:, 0, :, 0:nb * 128], uv[:, :, j0 * 128:(j0 + nb) * 128], writes=[tw_], semt=tw_)
                sc.dma("pool", w_[:, 1, :, 0:nb * 128], uv[:, :, FH + j0 * 128:FH + (j0 + nb) * 128], writes=[tw_],
                       semt=tw_, partial=True)

            load_wu(0)
            load_wu(1)
            for bidx, bk in enumerate(blocks):
                j0, nb, jj, gj = bk["j0"], bk["nb"], bk["jj"], bk["gj"]
                if bk["first"]:
                    sc.dma("pool", wd[:, 0:gj, :], dv[:, bk["gstart"]:bk["gstart"] + gj, :], writes=[twd], semt=twd)
                w_ = wu[bidx % 2]
                tw_ = twu[bidx % 2]
                if True:
                    for q in range(nb):
                        j = j0 + q
                        for half, Y_, tY_ in ((0, Ya, tYa), (1, Yg, tYg)):
                            ci = half * NJ + j
                            for tb in range(nblk):
                                t0, n = TBS[tb]
                                p = pc % 4
                                pc += 1
                                for kc in range(NK):
                                    self.mm(ps[p][:, 0:n], w_[:, half, kc, q * 128:(q + 1) * 128],
                                            self.actT[:, kc, t0:t0 + n], kc == 0, kc == NK - 1,
                                            [self.tk_act[(kc, tb)], tw_], [tps[p]])
                                P_ = ps[p]
                                self.ts(Y_[:, t0:t0 + n], P_[:, 0:n], fcw[:, ci, 1:2], fcb[:, ci:ci + 1], ALU.mult,
                                        ALU.add, [tps[p], tpar], [tY_], partial=(tb > 0))
                                self.stt(Y_[:, t0 + 1:t0 + n], P_[:, 0:n - 1], fcw[:, ci, 0:1], Y_[:, t0 + 1:t0 + n],
                                         ALU.mult, ALU.add, [tps[p], tY_, tpar], [tY_])
                                self.stt(Y_[:, t0:t0 + n - 1], P_[:, 1:n], fcw[:, ci, 2:3], Y_[:, t0:t0 + n - 1],
                                         ALU.mult, ALU.add, [tps[p], tY_, tpar], [tY_])
                                sc.op("dve", lambda e: e.tensor_copy(out=E[:, tb, 0:2], in_=P_[:, 0:n:n - 1]),
                                      reads=[tps[p]], writes=[tE])
                                if tb in (1, 2, 3):
                                    self.stt(Y_[:, t0:t0 + 1], E[:, tb - 1, 1:2], fcw[:, ci, 0:1], Y_[:, t0:t0 + 1],
                                             ALU.mult, ALU.add, [tE, tY_, tpar], [tY_])
                                    self.stt(Y_[:, t0 - 1:t0], E[:, tb, 0:1], fcw[:, ci, 2:3], Y_[:, t0 - 1:t0],
                                             ALU.mult, ALU.add, [tE, tY_, tpar], [tY_])
                        self.act(Yg[:, 0:Tn], Yg[:, 0:Tn], AF.Silu, [tYg], [tYg])
                        self.tt(ag[:, jj + q, 0:Tn], Yg[:, 0:Tn], Ya[:, 0:Tn], ALU.mult, [tYg, tYa], [tag[jj + q]])
                if bidx + 2 < len(blocks):
                    load_wu(bidx + 2)
                if not bk["last"]:
                    continue
                self.settle_x(txb)
                for oc in range(NK):
                    for tb in range(nblk):
                        t0, n = TBS[tb]
                        p = pc % 4
                        pc += 1
                        xi = xc % 6
                        xc += 1
                        tkx = self.tkx[(oc, tb)]
                        sc.dma("sp", xb[xi][:, 0:n], Sx["xres"][oc * 128:(oc + 1) * 128, t0:t0 + n], reads=[tkx],
                               writes=[txb[xi]], semt=txb[xi])
                        for q in range(gj):
                            self.mm(ps[p][:, 0:n], wd[:, q, oc * 128:(oc + 1) * 128], ag[:, q, t0:t0 + n], q == 0,
                                    q == gj - 1, [twd, tag[q]], [tps[p]])
                        r = 0 if tb < 4 else 1
                        self.stt(xb[xi][:, 0:n], ps[p][:, 0:n], self.mod[:, l, 80 + oc, r:r + 1], xb[xi][:, 0:n],
                                 ALU.mult, ALU.add, [tps[p], txb[xi], self.tk_mod], [txb[xi]])
                        sc.dma("act", Sx["xres"][oc * 128:(oc + 1) * 128, t0:t0 + n], xb[xi][:, 0:n], reads=[txb[xi]],
                               writes=[tkx], semt=txb[xi])


def _cols(v, nchunk):
    return np.ascontiguousarray(np.moveaxis(v.reshape(v.shape[:-1] + (nchunk, 128)), -1, -2))


def host_shared(inp):
    f = lambda a: np.ascontiguousarray(a, dtype=np.float32)
    sh = {}
    sh["w_ada"] = f(inp["w_ada"])
    sh["b_adaT"] = _cols(f(inp["b_ada"]), 96)
    sh["gmix"] = _cols(f(inp["norm_mix_g"]), NK)
    sh["gffn"] = _cols(f(inp["norm_ffn_g"]), NK)
    sh["gfin"] = _cols(f(inp["final_norm_g"]), NK)
    sh["w_in"] = f(inp["w_in"])
    sh["w_out"] = f(inp["w_out"])
    sh["ffn_up"] = f(inp["ffn_up"])
    sh["ffn_down"] = f(inp["ffn_down"])
    sh["lcw"] = np.ascontiguousarray(f(inp["lru_conv_w"]).reshape(L, 4, 4, 128).transpose(0, 3, 2, 1))
    sh["lcb"] = _cols(f(inp["lru_conv_b"]), 4)
    sh["lgw"] = f(inp["lru_gate_w"])
    sh["lgb"] = np.ascontiguousarray(f(inp["lru_gate_b"]).reshape(L, 2, 2, 4, 128).transpose(0, 4, 1, 2, 3))
    sh["llam"] = np.ascontiguousarray(f(inp["lru_lambda"]).reshape(L, 2, 4, 128).transpose(0, 3, 1, 2))
    rpb = f(inp["na_rpb"])
    kc = np.arange(64)[:, None]
    qc = np.arange(64)[None, :]
    cs = np.clip(qc - 8, 0, 48)
    inwin = (kc >= cs) & (kc < cs + 16)
    rel = np.clip(kc - qc + 15, 0, 30)
    tab = rpb[:, :, :, rel]
    tab = np.where(inwin[None, None, None], tab, np.float32(NEG))
    full = np.full((L, 16, 2, 16, 64, 64), NEG, np.float32)
    full[:, :, 0, :15] = tab
    full[:, :, 1, 3:11] = tab[:, :, 3:11]
    sh["natab"] = full
    t = np.arange(S)
    pos = np.stack([t // 64, t % 64], -1).astype(np.float32)
    inv = (10000.0 ** (-np.arange(16, dtype=np.float32) / 16)).astype(np.float32)
    ang = pos[:, :, None] * inv
    cos = np.cos(ang).astype(np.float32)
    sin = np.sin(ang).astype(np.float32)
    d = np.arange(128) % 64
    a_ = d // 32
    f_ = d % 16
    sh["cos"] = np.ascontiguousarray(cos[:, a_, f_].T)
    sh["sin"] = np.ascontiguousarray(sin[:, a_, f_].T)
    pm = np.zeros((128, 128), np.float32)
    for o in range(128):
        p_ = (o % 32) // 16
        partner = o + 16 if p_ == 0 else o - 16
        pm[partner, o] = -1.0 if p_ == 0 else 1.0
    sh["perm"] = pm
    sh["slng"] = f(inp["sgu_ln_g"])
    sh["slnb"] = f(inp["sgu_ln_b"])
    sh["swT"] = np.ascontiguousarray(f(inp["sgu_w"]).transpose(0, 1, 3, 2))
    sh["sbs"] = f(inp["sgu_b"]).reshape(L, 1, 1024)
    sh["fcw"] = np.ascontiguousarray(f(inp["ffn_conv_w"]).reshape(L, 3, 86, 128).transpose(0, 3, 2, 1))
    sh["fcb"] = _cols(f(inp["ffn_conv_b"]), 86)
    return sh


def host_core(inp, b):
    x = np.asarray(inp["x"][b], np.float32)
    ctx = np.asarray(inp["ctx"][b], np.float32)
    xT0 = np.ascontiguousarray(np.concatenate([x, ctx], 0).T)
    cc = np.stack([np.asarray(inp["c"][b], np.float32), np.asarray(inp["c_ctx"], np.float32)], -1)
    cT = np.ascontiguousarray(cc.reshape(NK, 128, 2).transpose(1, 0, 2))
    return {"xT0": xT0, "cT": cT}


_CACHE = {}


def kernel(**inputs):
    n = 8
    if "k" not in _CACHE:
        _CACHE["k"] = Kern(L, False)
    K = _CACHE["k"]
    sh = host_shared(inputs)
    in_maps = []
    for b in range(n):
        m = dict(sh)
        m.update(host_core(inputs, b))
        in_maps.append(m)
    res = run_bass_kernel_spmd(K.nc, in_maps, core_ids=list(range(n)))
    out = np.stack([np.ascontiguousarray(np.asarray(r["outT"]).T) for r in res.results], 0)
    return out.astype(np.float32)
```

```python
import numpy as np
import concourse.bass as bass
import concourse.mybir as mybir
from concourse.bass_utils import run_bass_kernel_spmd

F32 = mybir.dt.float32
BF16 = mybir.dt.bfloat16
AF = mybir.ActivationFunctionType
ALU = mybir.AluOpType

D = 2048
S = 2048
CL = 256
T = S + CL
L = 4
NK = 16
PROJ = 5120
FH = 5504
NJ = 43
TBS = [(0, 512), (512, 512), (1024, 512), (1536, 512), (2048, 256)]
EPS = 1e-6
NEG = -1.0e4
GC = 1.5957691216057308


class Tk:
    def __init__(self, name):
        self.name = name
        self.w = {}
        self.r = {}
        self.dsem = None
        self.dkey = None
        self.dcnt = 0


class Sched:
    def __init__(self, nc):
        self.nc = nc
        self.E = {"pe": nc.tensor, "act": nc.scalar, "dve": nc.vector, "pool": nc.gpsimd, "sp": nc.sync}
        self.esem = {k: nc.alloc_semaphore("es_" + k) for k in ("pe", "act", "dve", "pool")}
        self.ecnt = {k: 0 for k in self.esem}
        self.pend = {k: False for k in self.esem}
        self.waited = {k: {} for k in self.E}
        self.dpool = []
        self.dpool_sw = []
        self.dall = {}
        self.nd = 0
        self.ninst = 0

    def _deps(self, reads, writes, partial):
        deps = {}

        def add(d):
            for k, (sem, v) in d.items():
                if k not in deps or deps[k][1] < v:
                    deps[k] = (sem, v)
        for t in reads:
            add(t.w)
        for t in writes:
            add(t.r)
            if not partial:
                add(t.w)
        return deps

    def _wait(self, e, deps):
        for k, (sem, v) in deps.items():
            if e == "pe" and k == "e_pe":
                continue
            if k in self.dall:
                v = max(v, self.dall[k][1])
            if self.waited[e].get(k, 0) < v:
                self.E[e].wait_ge(sem, v)
                self.waited[e][k] = v
                self.ninst += 1

    def _commit(self, tok, reads, writes, partial):
        k, sem, v = tok
        for t in writes:
            if not partial:
                t.w = {}
                t.r = {}
            t.w[k] = (sem, v)
        for t in reads:
            t.r[k] = (sem, v)

    def op(self, e, fn, reads=(), writes=(), partial=False, inc=True):
        self._wait(e, self._deps(reads, writes, partial))
        ins = fn(self.E[e])
        self.ninst += 1
        if inc:
            self.ecnt[e] += 1
            ins.then_inc(self.esem[e], 1)
            v = self.ecnt[e]
            self.pend[e] = False
        else:
            assert e == "pe"
            v = self.ecnt[e] + 1
            self.pend[e] = True
        self._commit(("e_" + e, self.esem[e], v), reads, writes, partial)

    def _getsem(self, t, q):
        if t.dsem is None:
            t.dq = "hw" if q in ("sp", "act") else "sw"
            pool_ = self.dpool if t.dq == "hw" else self.dpool_sw
            if pool_:
                k, sem, cnt = pool_.pop()
            else:
                k = "d%d" % self.nd
                self.nd += 1
                sem = self.nc.alloc_semaphore("ds_" + k)
                cnt = 0
                self.dall[k] = [sem, 0]
            t.dkey, t.dsem, t.dcnt = k, sem, cnt

    def dma(self, q, out, in_, reads=(), writes=(), semt=None, partial=False, **kw):
        self._wait(q, self._deps(reads, writes, partial))
        self._getsem(semt, q)
        assert semt.dq == ("hw" if q in ("sp", "act") else "sw")
        ins = self.E[q].dma_start(out=out, in_=in_, **kw)
        self.ninst += 1
        semt.dcnt += 16
        ins.then_inc(semt.dsem, 16)
        self.dall[semt.dkey][1] = semt.dcnt
        self._commit((semt.dkey, semt.dsem, semt.dcnt), reads, writes, partial)

    def release(self, tks):
        for t in tks:
            if t.dsem is not None:
                (self.dpool if t.dq == "hw" else self.dpool_sw).append((t.dkey, t.dsem, t.dcnt))
                t.dsem = None

    def barrier(self):
        for e in self.esem:
            assert not self.pend[e]
        deps = {("e_" + k): (self.esem[k], self.ecnt[k]) for k in self.esem if self.ecnt[k] > 0}
        for k, (sem, cnt) in self.dall.items():
            if cnt > 0:
                deps[k] = (sem, cnt)
        for e in self.E:
            self._wait(e, deps)


class Phase:
    _n = [0]

    def __init__(self, K, name):
        self.K = K
        Phase._n[0] += 1
        self.name = "%s%d" % (name, Phase._n[0])
        self.cms = []
        self.tks = []

    def __enter__(self):
        return self

    def sb(self, name, shape, dt):
        cm = self.K.nc.sbuf_tensor(self.name + "_" + name, shape, dt)
        h = cm.__enter__()
        self.cms.append(cm)
        return h.ap() if hasattr(h, "ap") else h

    def psum(self, name, shape):
        cm = self.K.nc.psum_tensor(self.name + "_" + name, shape, F32)
        h = cm.__enter__()
        self.cms.append(cm)
        return h.ap() if hasattr(h, "ap") else h

    def tk(self, name):
        t = Tk(self.name + "_" + name)
        self.tks.append(t)
        return t

    def __exit__(self, *a):
        self.K.sc.barrier()
        self.K.clear_tokens()
        self.K.sc.release(self.tks)
        for cm in reversed(self.cms):
            cm.__exit__(None, None, None)
        return False


def na_tiles(m):
    if m == 0:
        return [0, 1, 2, 3], 5, 7, 0
    if m == 1:
        return [0, 1, 2, 3], 9, 5, 0
    if m == 14:
        return [12, 13, 14, 15], 13, 3, 0
    if m == 15:
        return [12, 13, 14, 15], 17, 1, 0
    return list(range(m - 2, m + 3)), 0, 3, 1


NA_PATTERNS = [(0, 5, 3, 1), (5, 4, 7, 0), (9, 4, 5, 0), (13, 4, 3, 0), (17, 4, 1, 0)]


class Kern:
    def __init__(self, nlayers=L, dbg=False):
        self.nl = nlayers
        self.dbg = dbg
        nc = self.nc = bass.Bass("TRN2", target_bir_lowering=False)
        self.sc = Sched(nc)
        di = lambda n, s, dt=F32: nc.dram_tensor(n, s, dt, kind="ExternalInput").ap()
        self.I = dict(
            xT0=di("xT0", [D, T]), cT=di("cT", [128, NK, 2]),
            w_ada=di("w_ada", [L, D, 6 * D]), b_adaT=di("b_adaT", [L, 128, 96]),
            gmix=di("gmix", [L, 128, NK]), gffn=di("gffn", [L, 128, NK]), gfin=di("gfin", [128, NK]),
            w_in=di("w_in", [L, D, PROJ]), w_out=di("w_out", [L, D, D]),
            ffn_up=di("ffn_up", [L, D, 2 * FH]), ffn_down=di("ffn_down", [L, FH, D]),
            lcw=di("lcw", [L, 128, 4, 4]), lcb=di("lcb", [L, 128, 4]),
            lgw=di("lgw", [L, 2, 2, 8, 64, 64]), lgb=di("lgb", [L, 128, 2, 2, 4]), llam=di("llam", [L, 128, 2, 4]),
            natab=di("natab", [L, 16, 2, 16, 64, 64]),
            cos=di("cos", [128, S]), sin=di("sin", [128, S]), perm=di("perm", [128, 128]),
            slng=di("slng", [L, 512]), slnb=di("slnb", [L, 512]),
            swT=di("swT", [L, 8, 128, 128]), sbs=di("sbs", [L, 1, 1024]),
            fcw=di("fcw", [L, 128, 86, 3]), fcb=di("fcb", [L, 128, 86]),
        )
        self.out = nc.dram_tensor("outT", [D, S], F32, kind="ExternalOutput").ap()
        kind = "ExternalOutput" if dbg else "Internal"
        ds = lambda n, s, dt: nc.dram_tensor(n, s, dt, kind=kind).ap()
        self.Sx = dict(
            xres=ds("xres", [D, T], F32), LX=ds("LX", [512, T], F32), LG=ds("LG", [512, T], F32),
            QR=ds("QR", [1024, S], BF16), QW=ds("QW", [1024, T], BF16), KK=ds("KK", [1024, T], BF16),
            VV=ds("VV", [T, 1024], BF16), SU=ds("SU", [512, T], BF16), SV=ds("SV", [T, 512], BF16),
            MIX=ds("MIX", [D, T], BF16),
        )
        self.tkd = {k: Tk("D_" + k) for k in self.Sx if k != "xres"}
        self.tkx = {(oc, tb): Tk("D_x%d_%d" % (oc, tb)) for oc in range(NK) for tb in range(5)}
        self.build()

    def clear_tokens(self):
        for t in self._persist():
            t.w = {}
            t.r = {}

    def _persist(self):
        out = list(self.tkx.values()) + list(self.tkd.values())
        for nm in ("tk_act",):
            if hasattr(self, nm):
                out += list(getattr(self, nm).values())
        for nm in ("tk_mod", "tk_cols", "tk_const", "tk_gains"):
            if hasattr(self, nm):
                out.append(getattr(self, nm))
        return out

    def settle_x(self, txb):
        deps = {t.dkey: (t.dsem, t.dcnt) for t in txb if t.dsem is not None}
        self.sc._wait("sp", deps)
        for t in self.tkx.values():
            t.w = {k: v for k, v in t.w.items() if k not in deps}

    def mm(self, out, lhsT, rhs, start, stop, reads, writes, inc=None):
        if inc is None:
            inc = stop
        self.sc.op("pe", lambda e: e.matmul(out, lhsT=lhsT, rhs=rhs, start=start, stop=stop),
                   reads=reads, writes=writes, partial=True, inc=inc)

    def act(self, out, in_, func, reads, writes, partial=False, **kw):
        self.sc.op("act", lambda e: e.activation(out=out, in_=in_, func=func, **kw), reads=reads, writes=writes,
                   partial=partial)

    def tt(self, out, in0, in1, op, reads, writes, partial=False, eng="dve"):
        self.sc.op(eng, lambda e: e.tensor_tensor(out=out, in0=in0, in1=in1, op=op), reads=reads, writes=writes,
                   partial=partial)

    def ts(self, out, in0, s1, s2, op0, op1, reads, writes, partial=False, eng="dve"):
        if op1 is None:
            self.sc.op(eng, lambda e: e.tensor_scalar(out=out, in0=in0, scalar1=s1, scalar2=None, op0=op0),
                       reads=reads, writes=writes, partial=partial)
        else:
            self.sc.op(eng, lambda e: e.tensor_scalar(out=out, in0=in0, scalar1=s1, scalar2=s2, op0=op0, op1=op1),
                       reads=reads, writes=writes, partial=partial)

    def stt(self, out, in0, scalar, in1, op0, op1, reads, writes, partial=False):
        self.sc.op("dve", lambda e: e.scalar_tensor_tensor(out=out, in0=in0, scalar=scalar, in1=in1, op0=op0, op1=op1),
                   reads=reads, writes=writes, partial=partial)

    def gelu(self, ph, out, src, n, reads, writes, tmp, tmp_tk, rows=slice(0, 128)):
        t = tmp[rows, 0:n]
        self.act(t, src, AF.Square, reads, [tmp_tk])
        self.ts(t, t, 0.044715, 1.0, ALU.mult, ALU.add, [tmp_tk], [tmp_tk])
        self.tt(t, t, src, ALU.mult, [tmp_tk] + list(reads), [tmp_tk])
        self.act(t, t, AF.Sigmoid, [tmp_tk], [tmp_tk], scale=GC)
        self.tt(out, t, src, ALU.mult, [tmp_tk] + list(reads), writes, partial=True)

    def build(self):
        nc, sc, I, Sx = self.nc, self.sc, self.I, self.Sx
        P0 = Phase(self, "g")
        self.P0 = P0
        self.actT = P0.sb("actT", [128, NK, T], BF16)
        self.tk_act = {(kc, tb): Tk("act%d_%d" % (kc, tb)) for kc in range(NK) for tb in range(5)}
        self.mod = P0.sb("mod", [128, L, 96, 2], F32)
        self.tk_mod = P0.tk("mod")
        self.cols = P0.sb("cols", [128, L, 4, NK, 2], F32)
        self.tk_cols = P0.tk("cols")
        self.ones32 = P0.sb("ones32", [128, 128], F32)
        self.onesb = P0.sb("onesb", [128, 128], BF16)
        self.tk_const = P0.tk("const")
        self.gm = P0.sb("gm", [128, L, NK], F32)
        self.gf = P0.sb("gf", [128, L, NK], F32)
        self.gfin = P0.sb("gfin", [128, NK], F32)
        self.zero1 = P0.sb("zero1", [128, 1], F32)
        sc.op("dve", lambda e: e.memset(self.ones32[:], 1.0), writes=[self.tk_const], partial=True)
        sc.op("dve", lambda e: e.memset(self.onesb[:], 1.0), writes=[self.tk_const], partial=True)
        sc.op("dve", lambda e: e.memset(self.zero1[:], 0.0), writes=[self.tk_const], partial=True)
        tg = P0.tk("gains")
        sc.dma("sp", self.gm[:], I["gmix"].rearrange("l p k -> p l k"), writes=[tg], semt=tg, partial=True)
        sc.dma("sp", self.gf[:], I["gffn"].rearrange("l p k -> p l k"), writes=[tg], semt=tg, partial=True)
        sc.dma("sp", self.gfin[:], I["gfin"], writes=[tg], semt=tg, partial=True)
        self.tk_gains = tg
        for oc in range(NK):
            for tb, (t0, n) in enumerate(TBS):
                tkx = self.tkx[(oc, tb)]
                sc.dma("sp", Sx["xres"][oc * 128:(oc + 1) * 128, t0:t0 + n], I["xT0"][oc * 128:(oc + 1) * 128, t0:t0 + n],
                       writes=[tkx], semt=tg)
        import os
        self.adaln_setup()
        self.adaln_standalone(0)
        for l in range(self.nl):
            self.layer(l)
        if os.environ.get("KFIN", "1") == "1":
            self.norm(None, final=True)
        P0.__exit__()
        sc.barrier()

    def adaln_setup(self):
        sc, I, P0 = self.sc, self.I, self.P0
        cs32 = P0.sb("cs32", [128, NK, 2], F32)
        self.csb = P0.sb("csb", [128, NK, 2], BF16)
        self.tcs = P0.tk("cs")
        self.bT = P0.sb("bT", [128, L, 96], F32)
        self.tbT = P0.tk("bT")
        sc.dma("sp", cs32[:], I["cT"], writes=[self.tcs], semt=self.tcs)
        sc.dma("sp", self.bT[:], I["b_adaT"].rearrange("l p k -> p l k"), writes=[self.tbT], semt=self.tbT)
        self.act(self.csb[:], cs32[:], AF.Silu, [self.tcs], [self.tcs])

    def adaln_begin(self, ph, l):
        st = dict(l=l, wb=[ph.sb("aw%d" % i, [128, NK, 512], BF16) for i in range(2)],
                  twb=[ph.tk("aw%d" % i) for i in range(2)], ps=ph.psum("aps", [128, 512]), tps=ph.tk("aps"),
                  wv=self.I["w_ada"][l].rearrange("(kc p) n -> p kc n", p=128), slot=0)
        return st

    def adaln_slot(self, st):
        s_ = st["slot"]
        if s_ > 24:
            return False
        st["slot"] += 1
        if s_ < 24:
            i = s_ % 2
            self.sc.dma("pool", st["wb"][i][:], st["wv"][:, :, s_ * 512:(s_ + 1) * 512], writes=[st["twb"][i]],
                        semt=st["twb"][i])
        if s_ >= 1:
            nb = s_ - 1
            i = nb % 2
            for s4 in range(4):
                oc = nb * 4 + s4
                for kc in range(NK):
                    self.mm(st["ps"][:, oc * 2:oc * 2 + 2], st["wb"][i][:, kc, s4 * 128:(s4 + 1) * 128],
                            self.csb[:, kc, :], kc == 0, kc == NK - 1, [st["twb"][i], self.tcs], [st["tps"]])
        return True

    def adaln_finish(self, st):
        while self.adaln_slot(st):
            pass
        l, ps, tps = st["l"], st["ps"], st["tps"]
        for r in range(2):
            self.tt(self.mod[:, l, :, r], ps[:, r:192:2], self.bT[:, l, :], ALU.add, [tps, self.tbT], [self.tk_mod],
                    partial=True)
        for r in range(2):
            self.ts(self.cols[:, l, 0, :, r], self.mod[:, l, 16:32, r], 1.0, None, ALU.add, None,
                    [self.tk_mod], [self.tk_cols], partial=True)
            self.tt(self.cols[:, l, 0, :, r], self.cols[:, l, 0, :, r], self.gm[:, l, :], ALU.mult,
                    [self.tk_cols, self.tk_gains], [self.tk_cols])
            self.ts(self.cols[:, l, 2, :, r], self.mod[:, l, 64:80, r], 1.0, None, ALU.add, None,
                    [self.tk_mod], [self.tk_cols], partial=True)
            self.tt(self.cols[:, l, 2, :, r], self.cols[:, l, 2, :, r], self.gf[:, l, :], ALU.mult,
                    [self.tk_cols, self.tk_gains], [self.tk_cols])

    def adaln_standalone(self, l):
        with Phase(self, "ada") as ph:
            st = self.adaln_begin(ph, l)
            self.adaln_finish(st)

    def norm(self, lw, final=False):
        sc, Sx = self.sc, self.Sx
        with Phase(self, "nrm") as ph:
            xt = [ph.sb("xt%d" % i, [128, NK, 512], F32) for i in range(2)]
            txtA = [ph.tk("xta%d" % i) for i in range(2)]
            txtB = [ph.tk("xtb%d" % i) for i in range(2)]
            HK = NK // 2
            sq = [ph.sb("sq%d" % i, [128, 512], F32) for i in range(2)]
            tsq = [ph.tk("sq%d" % i) for i in range(2)]
            rs = ph.sb("rs", [128, 512], F32)
            trs = ph.tk("rs")
            tmp = [ph.sb("tmp%d" % i, [128, 512], F32) for i in range(2)]
            ttmp = [ph.tk("tmp%d" % i) for i in range(2)]
            ps = ph.psum("ps", [128, 512])
            tps = ph.tk("ps")
            xv = Sx["xres"].rearrange("(kc p) t -> p kc t", p=128)
            nblk = 4 if final else 5
            cnt = 0
            for tb in range(nblk):
                t0, n = TBS[tb]
                i = tb % 2
                sc.dma("sp", xt[i][:, 0:HK, 0:n], xv[:, 0:HK, t0:t0 + n], reads=[self.tkx[(kc, tb)] for kc in range(HK)],
                       writes=[txtA[i]], semt=txtA[i])
                sc.dma("pool", xt[i][:, HK:NK, 0:n], xv[:, HK:NK, t0:t0 + n],
                       reads=[self.tkx[(kc, tb)] for kc in range(HK, NK)], writes=[txtB[i]], semt=txtB[i])
                for kc in range(NK):
                    j = kc % 2
                    txk = txtA[i] if kc < HK else txtB[i]
                    self.act(sq[j][:, 0:n], xt[i][:, kc, 0:n], AF.Square, [txk], [tsq[j]])
                    self.mm(ps[:, 0:n], self.ones32[:], sq[j][:, 0:n], kc == 0, kc == NK - 1, [tsq[j], self.tk_const],
                            [tps], inc=True)
                self.ts(rs[:, 0:n], ps[:, 0:n], 1.0 / D, EPS, ALU.mult, ALU.add, [tps], [trs])
                self.act(rs[:, 0:n], rs[:, 0:n], AF.Sqrt, [trs], [trs])
                sc.op("dve", lambda e: e.reciprocal(out=rs[:, 0:n], in_=rs[:, 0:n]), reads=[trs], writes=[trs])
                r = 0 if tb < 4 else 1
                for kc in range(NK):
                    j = cnt % 2
                    cnt += 1
                    txk = txtA[i] if kc < HK else txtB[i]
                    if final:
                        self.stt(tmp[j][:, 0:n], xt[i][:, kc, 0:n], self.gfin[:, kc:kc + 1], rs[:, 0:n], ALU.mult,
                                 ALU.mult, [txk, trs, self.tk_gains], [ttmp[j]])
                        sc.dma("sp", self.out[kc * 128:(kc + 1) * 128, t0:t0 + n], tmp[j][:, 0:n], reads=[ttmp[j]],
                               semt=ttmp[j])
                    else:
                        l, which = lw
                        self.stt(tmp[j][:, 0:n], xt[i][:, kc, 0:n], self.cols[:, l, which, kc, r:r + 1], rs[:, 0:n],
                                 ALU.mult, ALU.mult, [txk, trs, self.tk_cols], [ttmp[j]])
                        sh = 0 if which == 0 else 48
                        self.act(self.actT[:, kc, t0:t0 + n], tmp[j][:, 0:n], AF.Identity, [ttmp[j], self.tk_mod],
                                 [self.tk_act[(kc, tb)]], bias=self.mod[:, l, sh + kc, r:r + 1])

    def layer(self, l):
        import os
        stop = int(os.environ.get("KSTOP", "99"))
        steps = [lambda: self.norm((l, 0)), lambda: self.proj_in(l), lambda: self.lru(l), lambda: self.attn(l),
                 lambda: self.sgu(l), lambda: self.proj_res(l, "w_out", 32), lambda: self.norm((l, 2)),
                 lambda: self.ffn(l)]
        for i, st in enumerate(steps):
            if i < stop:
                st()

    def proj_in(self, l):
        sc, I, Sx, tkd = self.sc, self.I, self.Sx, self.tkd
        with Phase(self, "pin") as ph:
            wb = [ph.sb("w%d" % i, [128, NK, 512], BF16) for i in range(2)]
            twb = [ph.tk("w%d" % i) for i in range(2)]
            cos = ph.sb("cos", [128, S], F32)
            sin = ph.sb("sin", [128, S], F32)
            perm = ph.sb("perm", [128, 128], BF16)
            tc_ = ph.tk("c")
            sc.dma("sp", cos[:], I["cos"], writes=[tc_], semt=tc_, partial=True)
            sc.dma("sp", sin[:], I["sin"], writes=[tc_], semt=tc_, partial=True)
            tpm = ph.tk("perm")
            sc.dma("pool", perm[:], I["perm"], writes=[tpm], semt=tpm)
            lng = ph.sb("lng", [128, 512], F32)
            lnb = ph.sb("lnb", [128, 512], F32)
            tln = ph.tk("ln")
            sc.dma("sp", lng[:], I["slng"][l:l + 1, :].broadcast_to([128, 512]), writes=[tln], semt=tln, partial=True)
            sc.dma("sp", lnb[:], I["slnb"][l:l + 1, :].broadcast_to([128, 512]), writes=[tln], semt=tln, partial=True)
            st32 = [ph.sb("st32_%d" % i, [128, T], F32) for i in range(2)]
            tst32 = [ph.tk("st32_%d" % i) for i in range(2)]
            stb = [ph.sb("stb%d" % i, [128, T], BF16) for i in range(4)]
            tstb = [ph.tk("stb%d" % i) for i in range(4)]
            t1 = [ph.sb("t1_%d" % i, [128, 512], F32) for i in range(2)]
            tt1 = [ph.tk("t1_%d" % i) for i in range(2)]
            t2 = [ph.sb("t2_%d" % i, [128, 512], F32) for i in range(2)]
            tt2 = [ph.tk("t2_%d" % i) for i in range(2)]
            vst = [ph.sb("vst%d" % i, [128, 512], BF16) for i in range(2)]
            tvst = [ph.tk("vst%d" % i) for i in range(2)]
            stat = ph.sb("stat", [128, 8], F32)
            tstat = ph.tk("stat")
            ps = [ph.psum("ps%d" % i, [128, 512]) for i in range(4)]
            tps = [ph.tk("ps%d" % i) for i in range(4)]
            psw = [ph.psum("psw%d" % i, [128, 512]) for i in range(2)]
            tpsw = [ph.tk("psw%d" % i) for i in range(2)]
            wv = I["w_in"][l].rearrange("(kc p) n -> p kc n", p=128)
            pc = 0
            c32 = 0
            cb16 = 0
            cq = 0
            import os
            blks = [int(x) for x in os.environ.get("KBLK", "0,1,2,3,4,5,6,7,8,9").split(",")]
            for blk in blks:
                i = blk % 2
                sc.dma("pool", wb[i][:], wv[:, :, blk * 512:(blk + 1) * 512], writes=[twb[i]], semt=twb[i])
                if blk in (6, 7, 9):
                    for tt_ in range(18):
                        p = pc % 4
                        pc += 1
                        tb = min(tt_ // 4, 4)
                        for kc in range(NK):
                            self.mm(ps[p][:], self.actT[:, kc, tt_ * 128:(tt_ + 1) * 128], wb[i][:, kc, :], kc == 0,
                                    kc == NK - 1, [self.tk_act[(kc, tb)], twb[i]], [tps[p]])
                        j = tt_ % 2
                        if blk in (6, 7):
                            self.act(vst[j][:], ps[p][:], AF.Copy, [tps[p]], [tvst[j]])
                            c0 = (blk - 6) * 512
                            sc.dma("sp", Sx["VV"][tt_ * 128:(tt_ + 1) * 128, c0:c0 + 512], vst[j][:], reads=[tvst[j]],
                                   writes=[tkd["VV"]], semt=tvst[j], partial=True)
                        else:
                            g = t1[j]
                            self.act(g[:], ps[p][:], AF.Copy, [tps[p]], [tt1[j]])
                            self.gelu(ph, g[:], g[:], 512, [tt1[j]], [tt1[j]], t2[j], tt2[j])
                            sc.op("dve", lambda e: e.bn_stats(out=stat[:, 0:6], in_=g[:]), reads=[tt1[j]], writes=[tstat])
                            sc.op("dve", lambda e: e.bn_aggr(out=stat[:, 6:8], in_=stat[:, 0:6]), reads=[tstat],
                                  writes=[tstat])
                            self.ts(stat[:, 7:8], stat[:, 7:8], EPS, None, ALU.add, None, [tstat], [tstat])
                            self.act(stat[:, 7:8], stat[:, 7:8], AF.Sqrt, [tstat], [tstat])
                            sc.op("dve", lambda e: e.reciprocal(out=stat[:, 7:8], in_=stat[:, 7:8]), reads=[tstat],
                                  writes=[tstat])
                            self.ts(g[:], g[:], stat[:, 6:7], stat[:, 7:8], ALU.subtract, ALU.mult, [tt1[j], tstat],
                                    [tt1[j]])
                            self.tt(g[:], g[:], lng[:], ALU.mult, [tt1[j], tln], [tt1[j]])
                            self.tt(vst[j][:], g[:], lnb[:], ALU.add, [tt1[j], tln], [tvst[j]])
                            sc.dma("sp", Sx["SV"][tt_ * 128:(tt_ + 1) * 128, :], vst[j][:], reads=[tvst[j]],
                                   writes=[tkd["SV"]], semt=tvst[j], partial=True)
                    continue
                for s4 in range(4):
                    col = blk * 512 + s4 * 128
                    kind = ("lx", "lg", "q", "q", "k", "k", None, None, "su")[blk]
                    if kind in ("lx", "lg"):
                        so = st32[c32 % 2]
                        tso = tst32[c32 % 2]
                        c32 += 1
                    elif kind == "su":
                        so = stb[cb16 % 4]
                        tso = tstb[cb16 % 4]
                        cb16 += 1
                    else:
                        so = stb[cb16 % 4]
                        tso = tstb[cb16 % 4]
                        sr = stb[(cb16 + 1) % 4]
                        tsr = tstb[(cb16 + 1) % 4]
                        cb16 += 2
                    for tb, (t0, n) in enumerate(TBS):
                        p = pc % 4
                        pc += 1
                        for kc in range(NK):
                            self.mm(ps[p][:, 0:n], wb[i][:, kc, s4 * 128:(s4 + 1) * 128], self.actT[:, kc, t0:t0 + n],
                                    kc == 0, kc == NK - 1, [self.tk_act[(kc, tb)], twb[i]], [tps[p]])
                        if kind in ("lx", "lg"):
                            self.act(so[:, t0:t0 + n], ps[p][:, 0:n], AF.Copy, [tps[p]], [tso], partial=True)
                        elif kind == "su":
                            j = cq % 2
                            cq += 1
                            self.act(t1[j][:, 0:n], ps[p][:, 0:n], AF.Copy, [tps[p]], [tt1[j]])
                            self.gelu(ph, so[:, t0:t0 + n], t1[j][:, 0:n], n, [tt1[j]], [tso], t2[j], tt2[j])
                        else:
                            sclq = 0.125 if kind == "q" else 1.0
                            self.act(so[:, t0:t0 + n], ps[p][:, 0:n], AF.Copy, [tps[p]], [tso], partial=True, scale=sclq)
                            if tb < 4 and os.environ.get("KQ", "") != "raw":
                                j = cq % 2
                                cq += 1
                                self.act(t1[j][:, 0:n], ps[p][:, 0:n], AF.Copy, [tps[p]], [tt1[j]], scale=sclq)
                                self.tt(t1[j][:, 0:n], t1[j][:, 0:n], cos[:, t0:t0 + n], ALU.mult, [tt1[j], tc_], [tt1[j]])
                                self.mm(psw[j][:, 0:n], perm[:], so[:, t0:t0 + n], True, True, [tso, tpm], [tpsw[j]])
                                self.act(t2[j][:, 0:n], psw[j][:, 0:n], AF.Copy, [tpsw[j]], [tt2[j]])
                                self.tt(t2[j][:, 0:n], t2[j][:, 0:n], sin[:, t0:t0 + n], ALU.mult, [tt2[j], tc_], [tt2[j]])
                                self.tt(sr[:, t0:t0 + n], t1[j][:, 0:n], t2[j][:, 0:n], ALU.add, [tt1[j], tt2[j]],
                                        [tsr], partial=True)
                    if kind == "lx":
                        sc.dma("sp", Sx["LX"][s4 * 128:(s4 + 1) * 128, :], so[:], reads=[tso], writes=[tkd["LX"]], semt=tso,
                               partial=True)
                    elif kind == "lg":
                        sc.dma("sp", Sx["LG"][s4 * 128:(s4 + 1) * 128, :], so[:], reads=[tso], writes=[tkd["LG"]], semt=tso,
                               partial=True)
                    elif kind == "su":
                        sc.dma("sp", Sx["SU"][s4 * 128:(s4 + 1) * 128, :], so[:], reads=[tso], writes=[tkd["SU"]], semt=tso,
                               partial=True)
                    elif kind == "q":
                        r0 = (blk - 2) * 512 + s4 * 128
                        sc.dma("sp", Sx["QW"][r0:r0 + 128, :], so[:], reads=[tso], writes=[tkd["QW"]], semt=tso, partial=True)
                        if os.environ.get("KQ", "") == "":
                            sc.dma("sp", Sx["QR"][r0:r0 + 128, :], sr[:, 0:S], reads=[tsr], writes=[tkd["QR"]], semt=tsr,
                                   partial=True)
                    elif kind == "k":
                        r0 = (blk - 4) * 512 + s4 * 128
                        sc.dma("sp", Sx["KK"][r0:r0 + 128, S:T], so[:, S:T], reads=[tso], writes=[tkd["KK"]], semt=tso,
                               partial=True)
                        sc.dma("sp", Sx["KK"][r0:r0 + 128, 0:S], sr[:, 0:S], reads=[tsr], writes=[tkd["KK"]], semt=tsr,
                               partial=True)

    def lru(self, l):
        sc, I, Sx, tkd = self.sc, self.I, self.Sx, self.tkd
        with Phase(self, "lru") as ph:
            names = ["X", "Y", "XC", "Rr", "Ri", "A", "Q", "Bv", "H0", "H1"]
            tl = {n_: ph.sb(n_, [128, T], F32) for n_ in names}
            tkl = {n_: ph.tk(n_) for n_ in names}
            ob = ph.sb("ob", [128, T], BF16)
            tob = ph.tk("ob")
            bd = ph.sb("bd", [128, 16, 128], F32)
            tbd = ph.tk("bd")
            cw = ph.sb("cw", [128, 4, 4], F32)
            cb = ph.sb("cb", [128, 4], F32)
            gb = ph.sb("gb", [128, 2, 2, 4], F32)
            lam = ph.sb("lam", [128, 2, 4], F32)
            cl = ph.sb("cl", [128, 2, 4], F32)
            tpar = ph.tk("par")
            tcl = ph.tk("cl")
            ps = [ph.psum("ps%d" % i, [128, 512]) for i in range(2)]
            tps = [ph.tk("ps%d" % i) for i in range(2)]
            sc.op("dve", lambda e: e.memset(bd[:], 0.0), writes=[tbd])
            for d in range(2):
                for j in range(2):
                    for c in range(4):
                        for hb in range(2):
                            sc.dma("sp", bd[hb * 64:(hb + 1) * 64, (d * 2 + j) * 4 + c, hb * 64:(hb + 1) * 64],
                                   I["lgw"][l, d, j, 2 * c + hb], writes=[tbd], semt=tbd, partial=(not (d == 0 and j == 0 and c == 0 and hb == 0)))
            for dst, src in ((cw, "lcw"), (cb, "lcb"), (gb, "lgb"), (lam, "llam")):
                sc.dma("sp", dst[:], I[src][l], writes=[tpar], semt=tpar, partial=True)
            self.act(cl[:], lam[:], AF.Exp, [tpar], [tcl], scale=-1.0)
            self.act(cl[:], cl[:], AF.Ln, [tcl], [tcl], bias=1.0)
            self.ts(cl[:], cl[:], -8.0, None, ALU.mult, None, [tcl], [tcl])
            X, Y, XC, Rr, Ri, A, Q, Bv, H0, H1 = [tl[n_] for n_ in names]
            pcnt = 0
            for c in range(4):
                sc.dma("sp", X[:], Sx["LX"][c * 128:(c + 1) * 128, :], reads=[tkd["LX"]], writes=[tkl["X"]], semt=tkl["X"])
                sc.dma("sp", Y[:], Sx["LG"][c * 128:(c + 1) * 128, :], reads=[tkd["LG"]], writes=[tkl["Y"]], semt=tkl["Y"])
                self.ts(XC[:], X[:], cw[:, c, 2:3], cb[:, c:c + 1], ALU.mult, ALU.add, [tkl["X"], tpar], [tkl["XC"]])
                for (r0, n) in ((0, S), (S, CL)):
                    for k in (0, 1, 3):
                        off = k - 2
                        d0 = max(0, -off)
                        d1 = n - max(0, off)
                        self.stt(XC[:, r0 + d0:r0 + d1], X[:, r0 + d0 + off:r0 + d1 + off], cw[:, c, k:k + 1],
                                 XC[:, r0 + d0:r0 + d1], ALU.mult, ALU.add, [tkl["X"], tkl["XC"], tpar], [tkl["XC"]])
                for d in range(2):
                    for j, Rt, trt in ((0, Rr, tkl["Rr"]), (1, Ri, tkl["Ri"])):
                        for tb, (t0, n) in enumerate(TBS):
                            p = pcnt % 2
                            pcnt += 1
                            self.mm(ps[p][:, 0:n], bd[:, (d * 2 + j) * 4 + c, :], XC[:, t0:t0 + n], True, True,
                                    [tbd, tkl["XC"]], [tps[p]])
                            self.act(Rt[:, t0:t0 + n], ps[p][:, 0:n], AF.Sigmoid, [tps[p], tpar], [trt],
                                     partial=(tb > 0), bias=gb[:, d, j, c:c + 1])
                    self.act(A[:], Rr[:], AF.Exp, [tkl["Rr"], tcl], [tkl["A"]], scale=cl[:, d, c:c + 1])
                    self.act(Q[:], A[:], AF.Square, [tkl["A"]], [tkl["Q"]])
                    self.act(Q[:], Q[:], AF.Sqrt, [tkl["Q"]], [tkl["Q"]], scale=-1.0, bias=1.0)
                    self.tt(Bv[:], Ri[:], XC[:], ALU.mult, [tkl["Ri"], tkl["XC"]], [tkl["Bv"]])
                    self.tt(Bv[:], Bv[:], Q[:], ALU.mult, [tkl["Bv"], tkl["Q"]], [tkl["Bv"]])
                    if d == 0:
                        sc.op("dve", lambda e: e.tensor_tensor_scan(out=H0[:, S:T], data0=A[:, S:T], data1=Bv[:, S:T],
                                                                    initial=0.0, op0=ALU.mult, op1=ALU.add),
                              reads=[tkl["A"], tkl["Bv"]], writes=[tkl["H0"]])
                        sc.op("dve", lambda e: e.tensor_tensor_scan(out=H0[:, 0:S], data0=A[:, 0:S], data1=Bv[:, 0:S],
                                                                    initial=H0[:, T - 1:T], op0=ALU.mult, op1=ALU.add),
                              reads=[tkl["A"], tkl["Bv"], tkl["H0"]], writes=[tkl["H0"]], partial=True)
                    else:
                        sc.op("dve", lambda e: e.tensor_tensor_scan(out=H1[:, S:T][:, ::-1], data0=A[:, S:T][:, ::-1],
                                                                    data1=Bv[:, S:T][:, ::-1], initial=0.0,
                                                                    op0=ALU.mult, op1=ALU.add),
                              reads=[tkl["A"], tkl["Bv"]], writes=[tkl["H1"]])
                        sc.op("dve", lambda e: e.tensor_tensor_scan(out=H1[:, 0:S][:, ::-1], data0=A[:, 0:S][:, ::-1],
                                                                    data1=Bv[:, 0:S][:, ::-1], initial=H1[:, S:S + 1],
                                                                    op0=ALU.mult, op1=ALU.add),
                              reads=[tkl["A"], tkl["Bv"], tkl["H1"]], writes=[tkl["H1"]], partial=True)
                self.tt(H0[:], H0[:], H1[:], ALU.add, [tkl["H0"], tkl["H1"]], [tkl["H0"]])
                self.gelu(ph, H1[:], Y[:], T, [tkl["Y"]], [tkl["H1"]], Q, tkl["Q"])
                self.tt(ob[:], H0[:], H1[:], ALU.mult, [tkl["H0"], tkl["H1"]], [tob])
                sc.dma("sp", Sx["MIX"][c * 128:(c + 1) * 128, :], ob[:], reads=[tob], writes=[tkd["MIX"]], semt=tob,
                       partial=True)

    def attn(self, l):
        sc, I, Sx, tkd = self.sc, self.I, self.Sx, self.tkd
        need_ctx = l < L - 1
        with Phase(self, "att") as ph:
            NB = 2
            qr = [ph.sb("qr%d" % i, [128, S], BF16) for i in range(NB)]
            qw = [ph.sb("qw%d" % i, [128, T], BF16) for i in range(NB)]
            kk = [ph.sb("kk%d" % i, [128, T], BF16) for i in range(NB)]
            vv = [ph.sb("vv%d" % i, [128, 18, 128], BF16) for i in range(NB)]
            tin = [ph.tk("in%d" % i) for i in range(NB)]
            bias = [ph.sb("bias%d" % i, [128, 21, 128], F32) for i in range(2)]
            tbias = [ph.tk("bias%d" % i) for i in range(2)]
            ein = [ph.sb("ein%d" % i, [128, 640], F32) for i in range(2)]
            tein = [ph.tk("ein%d" % i) for i in range(2)]
            pt = [ph.sb("pt%d" % i, [128, 896], BF16) for i in range(2)]
            tpt = [ph.tk("pt%d" % i) for i in range(2)]
            rd = [ph.sb("rd%d" % i, [128, 256], F32) for i in range(2)]
            trd = [ph.tk("rd%d" % i) for i in range(2)]
            ob = [ph.sb("ob%d" % i, [128, T], BF16) for i in range(2)]
            tob = [ph.tk("ob%d" % i) for i in range(2)]
            psS = [ph.psum("S%d" % i, [128, 1024]) for i in range(2)]
            tpsS = [ph.tk("S%d" % i) for i in range(2)]
            psO = [ph.psum("O%d" % i, [128, 512]) for i in range(2)]
            tpsO = [ph.tk("O%d" % i) for i in range(2)]
            cnt = 0
            hcnt = 0
            ada = self.adaln_begin(ph, l + 1) if l + 1 < self.nl else None
            for hp in range(8):
                ib = hp % NB
                r0 = hp * 128
                t_ = tin[ib]
                sc.dma("sp", qr[ib][:], Sx["QR"][r0:r0 + 128, :], reads=[tkd["QR"]], writes=[t_], semt=t_)
                sc.dma("sp", qw[ib][:], Sx["QW"][r0:r0 + 128, :], reads=[tkd["QW"]], writes=[t_], semt=t_, partial=True)
                sc.dma("sp", kk[ib][:], Sx["KK"][r0:r0 + 128, :], reads=[tkd["KK"]], writes=[t_], semt=t_, partial=True)
                sc.dma("sp", vv[ib][:], Sx["VV"][:, r0:r0 + 128].rearrange("(t p) c -> p t c", p=128), reads=[tkd["VV"]],
                       writes=[t_], semt=t_, partial=True)
                o_ = ob[hp % 2]
                to_ = tob[hp % 2]
                its = []
                for hh in range(2):
                    h = hp * 2 + hh
                    bi = hh
                    first = True
                    for (slot, nt, rho0, which) in NA_PATTERNS:
                        for a in range(2):
                            for b in range(2):
                                rs_ = rho0 + a - b
                                src = I["natab"][l, h, which, rs_:rs_ + 2 * nt - 1:2, :, :].rearrange("t k q -> k t q")
                                sc.dma("sp", bias[bi][a * 64:(a + 1) * 64, slot:slot + nt, b * 64:(b + 1) * 64], src,
                                       writes=[tbias[bi]], semt=tbias[bi], partial=(not first))
                                first = False
                    for m in range(16):
                        its.append((hh, m))
                    if need_ctx:
                        its.append((hh, -1))
                    else:
                        pb = hh * 64
                        sc.op("dve", lambda e: e.memset(o_[pb:pb + 64, S:T], 0.0), writes=[to_], partial=True)

                def front(k):
                    hh, m = its[k]
                    pb = hh * 64
                    bi = hh
                    si = (cnt0 + k) % 2
                    S_ = psS[si]
                    if m >= 0:
                        tiles, slot, _, _ = na_tiles(m)
                        nt = len(tiles)
                        for i_, t in enumerate(tiles):
                            self.mm(S_[:, i_ * 128:(i_ + 1) * 128], kk[ib][pb:pb + 64, t * 128:(t + 1) * 128],
                                    qr[ib][pb:pb + 64, m * 128:(m + 1) * 128], True, True, [t_], [tpsS[si]], inc=False)
                        for j in range(2):
                            self.mm(S_[:, (nt + j) * 128:(nt + j + 1) * 128], kk[ib][pb:pb + 64, S + j * 128:S + (j + 1) * 128],
                                    qw[ib][pb:pb + 64, m * 128:(m + 1) * 128], True, True, [t_], [tpsS[si]], inc=(j == 1))
                        self.tt(ein[si][:, 0:nt * 128].rearrange("p (t q) -> p t q", q=128),
                                S_[:, 0:nt * 128].rearrange("p (t q) -> p t q", q=128),
                                bias[bi][:, slot:slot + nt, :], ALU.add, [tpsS[si], tbias[bi]], [tein[si]])
                        self.act(pt[si][:, 0:nt * 128], ein[si][:, 0:nt * 128], AF.Exp, [tein[si]], [tpt[si]])
                        self.act(pt[si][:, nt * 128:(nt + 2) * 128], S_[:, nt * 128:(nt + 2) * 128], AF.Exp,
                                 [tpsS[si], tein[si]], [tpt[si]], partial=True)
                    else:
                        for j in range(2):
                            self.mm(S_[:, j * 256:(j + 1) * 256], kk[ib][pb:pb + 64, S + j * 128:S + (j + 1) * 128],
                                    qw[ib][pb:pb + 64, S:T], True, True, [t_], [tpsS[si]], inc=(j == 1))
                        self.act(pt[si][:, 0:512], S_[:, 0:512], AF.Exp, [tpsS[si]], [tpt[si]])

                def back(k):
                    hh, m = its[k]
                    pb = hh * 64
                    si = (cnt0 + k) % 2
                    O_ = psO[si]
                    if m >= 0:
                        tiles, slot, _, _ = na_tiles(m)
                        vt = tiles + [16, 17]
                        for i_, t in enumerate(vt):
                            self.mm(O_[:, 0:128], vv[ib][:, t, :], pt[si][:, i_ * 128:(i_ + 1) * 128], i_ == 0,
                                    i_ == len(vt) - 1, [t_, tpt[si]], [tpsO[si]], inc=False)
                        for i_ in range(len(vt)):
                            self.mm(O_[:, 128:256], self.onesb[:], pt[si][:, i_ * 128:(i_ + 1) * 128], i_ == 0,
                                    i_ == len(vt) - 1, [self.tk_const, tpt[si]], [tpsO[si]], inc=(i_ == len(vt) - 1))
                        sc.op("dve", lambda e: e.reciprocal(out=rd[si][pb:pb + 64, 0:128], in_=O_[pb:pb + 64, 128:256]),
                              reads=[tpsO[si]], writes=[trd[si]])
                        self.tt(o_[pb:pb + 64, m * 128:(m + 1) * 128], O_[pb:pb + 64, 0:128], rd[si][pb:pb + 64, 0:128],
                                ALU.mult, [tpsO[si], trd[si]], [to_], partial=True)
                    else:
                        for j in range(2):
                            self.mm(O_[:, 0:256], vv[ib][:, 16 + j, :], pt[si][:, j * 256:(j + 1) * 256], j == 0, j == 1,
                                    [t_, tpt[si]], [tpsO[si]], inc=False)
                        for j in range(2):
                            self.mm(O_[:, 256:512], self.onesb[:], pt[si][:, j * 256:(j + 1) * 256], j == 0, j == 1,
                                    [self.tk_const, tpt[si]], [tpsO[si]], inc=(j == 1))
                        sc.op("dve", lambda e: e.reciprocal(out=rd[si][pb:pb + 64, 0:256], in_=O_[pb:pb + 64, 256:512]),
                              reads=[tpsO[si]], writes=[trd[si]])
                        self.tt(o_[pb:pb + 64, S:T], O_[pb:pb + 64, 0:256], rd[si][pb:pb + 64, 0:256], ALU.mult,
                                [tpsO[si], trd[si]], [to_], partial=True)

                cnt0 = cnt
                front(0)
                for k in range(len(its)):
                    if k + 1 < len(its):
                        front(k + 1)
                    back(k)
                    if ada is not None and (cnt0 + k) % 10 == 5:
                        self.adaln_slot(ada)
                cnt += len(its)
                mr = 512 + hp * 128
                sc.dma("sp", Sx["MIX"][mr:mr + 128, :], o_[:], reads=[to_], writes=[tkd["MIX"]], semt=to_, partial=True)
            if ada is not None:
                self.adaln_finish(ada)

    def sgu(self, l):
        sc, I, Sx, tkd = self.sc, self.I, self.Sx, self.tkd
        with Phase(self, "sgu") as ph:
            su = ph.sb("su", [128, 4, T], BF16)
            sv = ph.sb("sv", [128, 18, 512], BF16)
            tin = ph.tk("in")
            wT = ph.sb("wT", [128, 8, 128], BF16)
            twT = ph.tk("wT")
            bs = ph.sb("bs", [1, 1024], F32)
            tbs = ph.tk("bs")
            ob = ph.sb("ob", [128, 4, T], BF16)
            tob = ph.tk("ob")
            ps = [ph.psum("ps%d" % i, [128, 512]) for i in range(2)]
            tps = [ph.tk("ps%d" % i) for i in range(2)]
            sc.dma("sp", su[:], Sx["SU"].rearrange("(c p) t -> p c t", p=128), reads=[tkd["SU"]], writes=[tin], semt=tin)
            sc.dma("sp", sv[:], Sx["SV"].rearrange("(n p) c -> p n c", p=128), reads=[tkd["SV"]], writes=[tin], semt=tin,
                   partial=True)
            sc.dma("pool", wT[:], I["swT"][l].rearrange("g q p -> q g p"), writes=[twT], semt=twT)
            sc.dma("sp", bs[:], I["sbs"][l], writes=[tbs], semt=tbs)
            cnt = 0
            for n_ in range(18):
                for gp in range(4):
                    p = cnt % 2
                    cnt += 1
                    for hh in range(2):
                        g = gp * 2 + hh
                        o = ps[p][:, hh * 128:(hh + 1) * 128]
                        self.mm(o, sv[:, n_, gp * 128:(gp + 1) * 128], wT[:, g, :], True, False, [tin, twT], [tps[p]],
                                inc=False)
                        self.mm(o, self.ones32[0:1, :], bs[0:1, g * 128:(g + 1) * 128], False, True,
                                [self.tk_const, tbs], [tps[p]], inc=(hh == 1))
                    for hh in range(2):
                        pb = hh * 64
                        self.tt(ob[pb:pb + 64, gp, n_ * 128:(n_ + 1) * 128], ps[p][pb:pb + 64, hh * 128:(hh + 1) * 128],
                                su[pb:pb + 64, gp, n_ * 128:(n_ + 1) * 128], ALU.mult, [tps[p], tin], [tob], partial=True)
            sc.dma("sp", Sx["MIX"][1536:2048, :].rearrange("(c p) t -> p c t", p=128), ob[:], reads=[tob],
                   writes=[tkd["MIX"]], semt=tob, partial=True)

    def proj_res(self, l, wname, gate_base):
        sc, I, Sx, tkd = self.sc, self.I, self.Sx, self.tkd
        last = (l == L - 1)
        with Phase(self, "pr") as ph:
            wb = [ph.sb("w%d" % i, [128, NK, 512], BF16) for i in range(2)]
            twb = [ph.tk("w%d" % i) for i in range(2)]
            xb = [ph.sb("xb%d" % i, [128, 512], F32) for i in range(6)]
            txb = [ph.tk("xb%d" % i) for i in range(6)]
            ps = [ph.psum("ps%d" % i, [128, 512]) for i in range(4)]
            tps = [ph.tk("ps%d" % i) for i in range(4)]
            tload = ph.tk("ld")
            for kc in range(NK):
                sc.dma("sp", self.actT[:, kc, :], Sx["MIX"][kc * 128:(kc + 1) * 128, :], reads=[tkd["MIX"]],
                       writes=[self.tk_act[(kc, tb)] for tb in range(5)], semt=tload)
            wv = I[wname][l].rearrange("(kc p) n -> p kc n", p=128)
            cnt = 0
            for cb in range(4):
                i = cb % 2
                sc.dma("pool", wb[i][:], wv[:, :, cb * 512:(cb + 1) * 512], writes=[twb[i]], semt=twb[i])
                for s4 in range(4):
                    oc = cb * 4 + s4
                    for tb, (t0, n) in enumerate(TBS):
                        if last and tb == 4:
                            continue
                        p = cnt % 4
                        xi = cnt % 6
                        cnt += 1
                        tkx = self.tkx[(oc, tb)]
                        sc.dma("sp", xb[xi][:, 0:n], Sx["xres"][oc * 128:(oc + 1) * 128, t0:t0 + n], reads=[tkx],
                               writes=[txb[xi]], semt=txb[xi])
                        for kc in range(NK):
                            self.mm(ps[p][:, 0:n], wb[i][:, kc, s4 * 128:(s4 + 1) * 128], self.actT[:, kc, t0:t0 + n],
                                    kc == 0, kc == NK - 1, [self.tk_act[(kc, tb)], twb[i]], [tps[p]])
                        r = 0 if tb < 4 else 1
                        self.stt(xb[xi][:, 0:n], ps[p][:, 0:n], self.mod[:, l, gate_base + oc, r:r + 1], xb[xi][:, 0:n],
                                 ALU.mult, ALU.add, [tps[p], txb[xi], self.tk_mod], [txb[xi]])
                        sc.dma("act", Sx["xres"][oc * 128:(oc + 1) * 128, t0:t0 + n], xb[xi][:, 0:n], reads=[txb[xi]],
                               writes=[tkx], semt=txb[xi])

    def ffn(self, l):
        sc, I, Sx = self.sc, self.I, self.Sx
        last = (l == L - 1)
        nblk = 4 if last else 5
        Tn = S if last else T
        with Phase(self, "ffn") as ph:
            GJ = 6
            wu = [ph.sb("wu%d" % i, [128, 2, NK, 256], BF16) for i in range(2)]
            twu = [ph.tk("wu%d" % i) for i in range(2)]
            wd = ph.sb("wd", [128, GJ, D], BF16)
            twd = ph.tk("wd")
            ag = ph.sb("ag", [128, GJ, T], BF16)
            tag = [ph.tk("ag%d" % i) for i in range(GJ)]
            E = ph.sb("E", [128, 5, 2], F32)
            tE = ph.tk("E")
            Ya = ph.sb("Ya", [128, T], F32)
            tYa = ph.tk("Ya")
            Yg = ph.sb("Yg", [128, T], F32)
            tYg = ph.tk("Yg")
            fcw = ph.sb("fcw", [128, 86, 3], F32)
            fcb = ph.sb("fcb", [128, 86], F32)
            tpar = ph.tk("par")
            xb = [ph.sb("xb%d" % i, [128, 512], F32) for i in range(6)]
            txb = [ph.tk("xb%d" % i) for i in range(6)]
            ps = [ph.psum("ps%d" % i, [128, 512]) for i in range(4)]
            tps = [ph.tk("ps%d" % i) for i in range(4)]
            sc.dma("sp", fcw[:], I["fcw"][l], writes=[tpar], semt=tpar, partial=True)
            sc.dma("sp", fcb[:], I["fcb"][l], writes=[tpar], semt=tpar, partial=True)
            uv = I["ffn_up"][l].rearrange("(kc p) n -> p kc n", p=128)
            dv = I["ffn_down"][l].rearrange("(j p) n -> p j n", p=128)
            ranges = [(0, S)] if last else [(0, S), (S, CL)]
            pc = 0
            xc = 0
            blocks = []
            j_done = 0
            gi = 0
            while j_done < NJ:
                gj = min(GJ, NJ - j_done)
                jj = 0
                while jj < gj:
                    nb = min(2, gj - jj)
                    blocks.append(dict(j0=j_done + jj, nb=nb, jj=jj, g=gi, gj=gj, gstart=j_done, first=(jj == 0),
                                       last=(jj + nb >= gj)))
                    jj += nb
                j_done += gj
                gi += 1

            def load_wu(bidx):
                bk = blocks[bidx]
                w_ = wu[bidx % 2]
                tw_ = twu[bidx % 2]
                j0, nb = bk["j0"], bk["nb"]
                sc.dma("pool", w_[:, 0, :, 0:nb * 128], uv[:, :, j0 * 128:(j0 + nb) * 128], writes=[tw_], semt=tw_)
                sc.dma("pool", w_[:, 1, :, 0:nb * 128], uv[:, :, FH + j0 * 128:FH + (j0 + nb) * 128], writes=[tw_],
                       semt=tw_, partial=True)

            load_wu(0)
            load_wu(1)
            for bidx, bk in enumerate(blocks):
                j0, nb, jj, gj = bk["j0"], bk["nb"], bk["jj"], bk["gj"]
                if bk["first"]:
                    sc.dma("pool", wd[:, 0:gj, :], dv[:, bk["gstart"]:bk["gstart"] + gj, :], writes=[twd], semt=twd)
                w_ = wu[bidx % 2]
                tw_ = twu[bidx % 2]
                if True:
                    for q in range(nb):
                        j = j0 + q
                        for half, Y_, tY_ in ((0, Ya, tYa), (1, Yg, tYg)):
                            ci = half * NJ + j
                            for tb in range(nblk):
                                t0, n = TBS[tb]
                                p = pc % 4
                                pc += 1
                                for kc in range(NK):
                                    self.mm(ps[p][:, 0:n], w_[:, half, kc, q * 128:(q + 1) * 128],
                                            self.actT[:, kc, t0:t0 + n], kc == 0, kc == NK - 1,
                                            [self.tk_act[(kc, tb)], tw_], [tps[p]])
                                P_ = ps[p]
                                self.ts(Y_[:, t0:t0 + n], P_[:, 0:n], fcw[:, ci, 1:2], fcb[:, ci:ci + 1], ALU.mult,
                                        ALU.add, [tps[p], tpar], [tY_], partial=(tb > 0))
                                self.stt(Y_[:, t0 + 1:t0 + n], P_[:, 0:n - 1], fcw[:, ci, 0:1], Y_[:, t0 + 1:t0 + n],
                                         ALU.mult, ALU.add, [tps[p], tY_, tpar], [tY_])
                                self.stt(Y_[:, t0:t0 + n - 1], P_[:, 1:n], fcw[:, ci, 2:3], Y_[:, t0:t0 + n - 1],
                                         ALU.mult, ALU.add, [tps[p], tY_, tpar], [tY_])
                                sc.op("dve", lambda e: e.tensor_copy(out=E[:, tb, 0:2], in_=P_[:, 0:n:n - 1]),
                                      reads=[tps[p]], writes=[tE])
                                if tb in (1, 2, 3):
                                    self.stt(Y_[:, t0:t0 + 1], E[:, tb - 1, 1:2], fcw[:, ci, 0:1], Y_[:, t0:t0 + 1],
                                             ALU.mult, ALU.add, [tE, tY_, tpar], [tY_])
                                    self.stt(Y_[:, t0 - 1:t0], E[:, tb, 0:1], fcw[:, ci, 2:3], Y_[:, t0 - 1:t0],
                                             ALU.mult, ALU.add, [tE, tY_, tpar], [tY_])
                        self.act(Yg[:, 0:Tn], Yg[:, 0:Tn], AF.Silu, [tYg], [tYg])
                        self.tt(ag[:, jj + q, 0:Tn], Yg[:, 0:Tn], Ya[:, 0:Tn], ALU.mult, [tYg, tYa], [tag[jj + q]])
                if bidx + 2 < len(blocks):
                    load_wu(bidx + 2)
                if not bk["last"]:
                    continue
                self.settle_x(txb)
                for oc in range(NK):
                    for tb in range(nblk):
                        t0, n = TBS[tb]
                        p = pc % 4
                        pc += 1
                        xi = xc % 6
                        xc += 1
                        tkx = self.tkx[(oc, tb)]
                        sc.dma("sp", xb[xi][:, 0:n], Sx["xres"][oc * 128:(oc + 1) * 128, t0:t0 + n], reads=[tkx],
                               writes=[txb[xi]], semt=txb[xi])
                        for q in range(gj):
                            self.mm(ps[p][:, 0:n], wd[:, q, oc * 128:(oc + 1) * 128], ag[:, q, t0:t0 + n], q == 0,
                                    q == gj - 1, [twd, tag[q]], [tps[p]])
                        r = 0 if tb < 4 else 1
                        self.stt(xb[xi][:, 0:n], ps[p][:, 0:n], self.mod[:, l, 80 + oc, r:r + 1], xb[xi][:, 0:n],
                                 ALU.mult, ALU.add, [tps[p], txb[xi], self.tk_mod], [txb[xi]])
                        sc.dma("act", Sx["xres"][oc * 128:(oc + 1) * 128, t0:t0 + n], xb[xi][:, 0:n], reads=[txb[xi]],
                               writes=[tkx], semt=txb[xi])


def _cols(v, nchunk):
    return np.ascontiguousarray(np.moveaxis(v.reshape(v.shape[:-1] + (nchunk, 128)), -1, -2))


def host_shared(inp):
    f = lambda a: np.ascontiguousarray(a, dtype=np.float32)
    sh = {}
    sh["w_ada"] = f(inp["w_ada"])
    sh["b_adaT"] = _cols(f(inp["b_ada"]), 96)
    sh["gmix"] = _cols(f(inp["norm_mix_g"]), NK)
    sh["gffn"] = _cols(f(inp["norm_ffn_g"]), NK)
    sh["gfin"] = _cols(f(inp["final_norm_g"]), NK)
    sh["w_in"] = f(inp["w_in"])
    sh["w_out"] = f(inp["w_out"])
    sh["ffn_up"] = f(inp["ffn_up"])
    sh["ffn_down"] = f(inp["ffn_down"])
    sh["lcw"] = np.ascontiguousarray(f(inp["lru_conv_w"]).reshape(L, 4, 4, 128).transpose(0, 3, 2, 1))
    sh["lcb"] = _cols(f(inp["lru_conv_b"]), 4)
    sh["lgw"] = f(inp["lru_gate_w"])
    sh["lgb"] = np.ascontiguousarray(f(inp["lru_gate_b"]).reshape(L, 2, 2, 4, 128).transpose(0, 4, 1, 2, 3))
    sh["llam"] = np.ascontiguousarray(f(inp["lru_lambda"]).reshape(L, 2, 4, 128).transpose(0, 3, 1, 2))
    rpb = f(inp["na_rpb"])
    kc = np.arange(64)[:, None]
    qc = np.arange(64)[None, :]
    cs = np.clip(qc - 8, 0, 48)
    inwin = (kc >= cs) & (kc < cs + 16)
    rel = np.clip(kc - qc + 15, 0, 30)
    tab = rpb[:, :, :, rel]
    tab = np.where(inwin[None, None, None], tab, np.float32(NEG))
    full = np.full((L, 16, 2, 16, 64, 64), NEG, np.float32)
    full[:, :, 0, :15] = tab
    full[:, :, 1, 3:11] = tab[:, :, 3:11]
    sh["natab"] = full
    t = np.arange(S)
    pos = np.stack([t // 64, t % 64], -1).astype(np.float32)
    inv = (10000.0 ** (-np.arange(16, dtype=np.float32) / 16)).astype(np.float32)
    ang = pos[:, :, None] * inv
    cos = np.cos(ang).astype(np.float32)
    sin = np.sin(ang).astype(np.float32)
    d = np.arange(128) % 64
    a_ = d // 32
    f_ = d % 16
    sh["cos"] = np.ascontiguousarray(cos[:, a_, f_].T)
    sh["sin"] = np.ascontiguousarray(sin[:, a_, f_].T)
    pm = np.zeros((128, 128), np.float32)
    for o in range(128):
        p_ = (o % 32) // 16
        partner = o + 16 if p_ == 0 else o - 16
        pm[partner, o] = -1.0 if p_ == 0 else 1.0
    sh["perm"] = pm
    sh["slng"] = f(inp["sgu_ln_g"])
    sh["slnb"] = f(inp["sgu_ln_b"])
    sh["swT"] = np.ascontiguousarray(f(inp["sgu_w"]).transpose(0, 1, 3, 2))
    sh["sbs"] = f(inp["sgu_b"]).reshape(L, 1, 1024)
    sh["fcw"] = np.ascontiguousarray(f(inp["ffn_conv_w"]).reshape(L, 3, 86, 128).transpose(0, 3, 2, 1))
    sh["fcb"] = _cols(f(inp["ffn_conv_b"]), 86)
    return sh


def host_core(inp, b):
    x = np.asarray(inp["x"][b], np.float32)
    ctx = np.asarray(inp["ctx"][b], np.float32)
    xT0 = np.ascontiguousarray(np.concatenate([x, ctx], 0).T)
    cc = np.stack([np.asarray(inp["c"][b], np.float32), np.asarray(inp["c_ctx"], np.float32)], -1)
    cT = np.ascontiguousarray(cc.reshape(NK, 128, 2).transpose(1, 0, 2))
    return {"xT0": xT0, "cT": cT}


_CACHE = {}


def kernel(**inputs):
    n = 8
    if "k" not in _CACHE:
        _CACHE["k"] = Kern(L, False)
    K = _CACHE["k"]
    sh = host_shared(inputs)
    in_maps = []
    for b in range(n):
        m = dict(sh)
        m.update(host_core(inputs, b))
        in_maps.append(m)
    res = run_bass_kernel_spmd(K.nc, in_maps, core_ids=list(range(n)))
    out = np.stack([np.ascontiguousarray(np.asarray(r["outT"]).T) for r in res.results], 0)
    return out.astype(np.float32)
```

```python
import numpy as np
import concourse.bass as bass
import concourse.mybir as mybir
from concourse.bass_utils import run_bass_kernel_spmd

F32 = mybir.dt.float32
BF16 = mybir.dt.bfloat16
AF = mybir.ActivationFunctionType
ALU = mybir.AluOpType

D = 2048
S = 2048
CL = 256
T = S + CL
L = 4
NK = 16
PROJ = 5120
FH = 5504
NJ = 43
TBS = [(0, 512), (512, 512), (1024, 512), (1536, 512), (2048, 256)]
EPS = 1e-6
NEG = -1.0e4
GC = 1.5957691216057308


class Tk:
    def __init__(self, name):
        self.name = name
        self.w = {}
        self.r = {}
        self.dsem = None
        self.dkey = None
        self.dcnt = 0


class Sched:
    def __init__(self, nc):
        self.nc = nc
        self.E = {"pe": nc.tensor, "act": nc.scalar, "dve": nc.vector, "pool": nc.gpsimd, "sp": nc.sync}
        self.esem = {k: nc.alloc_semaphore("es_" + k) for k in ("pe", "act", "dve", "pool")}
        self.ecnt = {k: 0 for k in self.esem}
        self.pend = {k: False for k in self.esem}
        self.waited = {k: {} for k in self.E}
        self.dpool = []
        self.dpool_sw = []
        self.dall = {}
        self.nd = 0
        self.ninst = 0

    def _deps(self, reads, writes, partial):
        deps = {}

        def add(d):
            for k, (sem, v) in d.items():
                if k not in deps or deps[k][1] < v:
                    deps[k] = (sem, v)
        for t in reads:
            add(t.w)
        for t in writes:
            add(t.r)
            if not partial:
                add(t.w)
        return deps

    def _wait(self, e, deps):
        for k, (sem, v) in deps.items():
            if e == "pe" and k == "e_pe":
                continue
            if k in self.dall:
                v = max(v, self.dall[k][1])
            if self.waited[e].get(k, 0) < v:
                self.E[e].wait_ge(sem, v)
                self.waited[e][k] = v
                self.ninst += 1

    def _commit(self, tok, reads, writes, partial):
        k, sem, v = tok
        for t in writes:
            if not partial:
                t.w = {}
                t.r = {}
            t.w[k] = (sem, v)
        for t in reads:
            t.r[k] = (sem, v)

    def op(self, e, fn, reads=(), writes=(), partial=False, inc=True):
        self._wait(e, self._deps(reads, writes, partial))
        ins = fn(self.E[e])
        self.ninst += 1
        if inc:
            self.ecnt[e] += 1
            ins.then_inc(self.esem[e], 1)
            v = self.ecnt[e]
            self.pend[e] = False
        else:
            assert e == "pe"
            v = self.ecnt[e] + 1
            self.pend[e] = True
        self._commit(("e_" + e, self.esem[e], v), reads, writes, partial)

    def _getsem(self, t, q):
        if t.dsem is None:
            t.dq = "hw" if q in ("sp", "act") else "sw"
            pool_ = self.dpool if t.dq == "hw" else self.dpool_sw
            if pool_:
                k, sem, cnt = pool_.pop()
            else:
                k = "d%d" % self.nd
                self.nd += 1
                sem = self.nc.alloc_semaphore("ds_" + k)
                cnt = 0
                self.dall[k] = [sem, 0]
            t.dkey, t.dsem, t.dcnt = k, sem, cnt

    def dma(self, q, out, in_, reads=(), writes=(), semt=None, partial=False, **kw):
        self._wait(q, self._deps(reads, writes, partial))
        self._getsem(semt, q)
        assert semt.dq == ("hw" if q in ("sp", "act") else "sw")
        ins = self.E[q].dma_start(out=out, in_=in_, **kw)
        self.ninst += 1
        semt.dcnt += 16
        ins.then_inc(semt.dsem, 16)
        self.dall[semt.dkey][1] = semt.dcnt
        self._commit((semt.dkey, semt.dsem, semt.dcnt), reads, writes, partial)

    def release(self, tks):
        for t in tks:
            if t.dsem is not None:
                (self.dpool if t.dq == "hw" else self.dpool_sw).append((t.dkey, t.dsem, t.dcnt))
                t.dsem = None

    def barrier(self):
        for e in self.esem:
            assert not self.pend[e]
        deps = {("e_" + k): (self.esem[k], self.ecnt[k]) for k in self.esem if self.ecnt[k] > 0}
        for k, (sem, cnt) in self.dall.items():
            if cnt > 0:
                deps[k] = (sem, cnt)
        for e in self.E:
            self._wait(e, deps)


class Phase:
    _n = [0]

    def __init__(self, K, name):
        self.K = K
        Phase._n[0] += 1
        self.name = "%s%d" % (name, Phase._n[0])
        self.cms = []
        self.tks = []

    def __enter__(self):
        return self

    def sb(self, name, shape, dt):
        cm = self.K.nc.sbuf_tensor(self.name + "_" + name, shape, dt)
        h = cm.__enter__()
        self.cms.append(cm)
        return h.ap() if hasattr(h, "ap") else h

    def psum(self, name, shape):
        cm = self.K.nc.psum_tensor(self.name + "_" + name, shape, F32)
        h = cm.__enter__()
        self.cms.append(cm)
        return h.ap() if hasattr(h, "ap") else h

    def tk(self, name):
        t = Tk(self.name + "_" + name)
        self.tks.append(t)
        return t

    def __exit__(self, *a):
        self.K.sc.barrier()
        self.K.clear_tokens()
        self.K.sc.release(self.tks)
        for cm in reversed(self.cms):
            cm.__exit__(None, None, None)
        return False


def na_tiles(m):
    if m == 0:
        return [0, 1, 2, 3], 5, 7, 0
    if m == 1:
        return [0, 1, 2, 3], 9, 5, 0
    if m == 14:
        return [12, 13, 14, 15], 13, 3, 0
    if m == 15:
        return [12, 13, 14, 15], 17, 1, 0
    return list(range(m - 2, m + 3)), 0, 3, 1


NA_PATTERNS = [(0, 5, 3, 1), (5, 4, 7, 0), (9, 4, 5, 0), (13, 4, 3, 0), (17, 4, 1, 0)]


class Kern:
    def __init__(self, nlayers=L, dbg=False):
        self.nl = nlayers
        self.dbg = dbg
        nc = self.nc = bass.Bass("TRN2", target_bir_lowering=False)
        self.sc = Sched(nc)
        di = lambda n, s, dt=F32: nc.dram_tensor(n, s, dt, kind="ExternalInput").ap()
        self.I = dict(
            xT0=di("xT0", [D, T]), cT=di("cT", [128, NK, 2]),
            w_ada=di("w_ada", [L, D, 6 * D]), b_adaT=di("b_adaT", [L, 128, 96]),
            gmix=di("gmix", [L, 128, NK]), gffn=di("gffn", [L, 128, NK]), gfin=di("gfin", [128, NK]),
            w_in=di("w_in", [L, D, PROJ]), w_out=di("w_out", [L, D, D]),
            ffn_up=di("ffn_up", [L, D, 2 * FH]), ffn_down=di("ffn_down", [L, FH, D]),
            lcw=di("lcw", [L, 128, 4, 4]), lcb=di("lcb", [L, 128, 4]),
            lgw=di("lgw", [L, 2, 2, 8, 64, 64]), lgb=di("lgb", [L, 128, 2, 2, 4]), llam=di("llam", [L, 128, 2, 4]),
            natab=di("natab", [L, 16, 2, 16, 64, 64]),
            cos=di("cos", [128, S]), sin=di("sin", [128, S]), perm=di("perm", [128, 128]),
            slng=di("slng", [L, 512]), slnb=di("slnb", [L, 512]),
            swT=di("swT", [L, 8, 128, 128]), sbs=di("sbs", [L, 1, 1024]),
            fcw=di("fcw", [L, 128, 86, 3]), fcb=di("fcb", [L, 128, 86]),
        )
        self.out = nc.dram_tensor("outT", [D, S], F32, kind="ExternalOutput").ap()
        kind = "ExternalOutput" if dbg else "Internal"
        ds = lambda n, s, dt: nc.dram_tensor(n, s, dt, kind=kind).ap()
        self.Sx = dict(
            xres=ds("xres", [D, T], F32), LX=ds("LX", [512, T], F32), LG=ds("LG", [512, T], F32),
            QR=ds("QR", [1024, S], BF16), QW=ds("QW", [1024, T], BF16), KK=ds("KK", [1024, T], BF16),
            VV=ds("VV", [T, 1024], BF16), SU=ds("SU", [512, T], BF16), SV=ds("SV", [T, 512], BF16),
            MIX=ds("MIX", [D, T], BF16),
        )
        self.tkd = {k: Tk("D_" + k) for k in self.Sx if k != "xres"}
        self.tkx = {(oc, tb): Tk("D_x%d_%d" % (oc, tb)) for oc in range(NK) for tb in range(5)}
        self.build()

    def clear_tokens(self):
        for t in self._persist():
            t.w = {}
            t.r = {}

    def _persist(self):
        out = list(self.tkx.values()) + list(self.tkd.values())
        for nm in ("tk_act",):
            if hasattr(self, nm):
                out += list(getattr(self, nm).values())
        for nm in ("tk_mod", "tk_cols", "tk_const", "tk_gains"):
            if hasattr(self, nm):
                out.append(getattr(self, nm))
        return out

    def settle_x(self, txb):
        deps = {t.dkey: (t.dsem, t.dcnt) for t in txb if t.dsem is not None}
        self.sc._wait("sp", deps)
        for t in self.tkx.values():
            t.w = {k: v for k, v in t.w.items() if k not in deps}

    def mm(self, out, lhsT, rhs, start, stop, reads, writes, inc=None):
        if inc is None:
            inc = stop
        self.sc.op("pe", lambda e: e.matmul(out, lhsT=lhsT, rhs=rhs, start=start, stop=stop),
                   reads=reads, writes=writes, partial=True, inc=inc)

    def act(self, out, in_, func, reads, writes, partial=False, **kw):
        self.sc.op("act", lambda e: e.activation(out=out, in_=in_, func=func, **kw), reads=reads, writes=writes,
                   partial=partial)

    def tt(self, out, in0, in1, op, reads, writes, partial=False, eng="dve"):
        self.sc.op(eng, lambda e: e.tensor_tensor(out=out, in0=in0, in1=in1, op=op), reads=reads, writes=writes,
                   partial=partial)

    def ts(self, out, in0, s1, s2, op0, op1, reads, writes, partial=False, eng="dve"):
        if op1 is None:
            self.sc.op(eng, lambda e: e.tensor_scalar(out=out, in0=in0, scalar1=s1, scalar2=None, op0=op0),
                       reads=reads, writes=writes, partial=partial)
        else:
            self.sc.op(eng, lambda e: e.tensor_scalar(out=out, in0=in0, scalar1=s1, scalar2=s2, op0=op0, op1=op1),
                       reads=reads, writes=writes, partial=partial)

    def stt(self, out, in0, scalar, in1, op0, op1, reads, writes, partial=False):
        self.sc.op("dve", lambda e: e.scalar_tensor_tensor(out=out, in0=in0, scalar=scalar, in1=in1, op0=op0, op1=op1),
                   reads=reads, writes=writes, partial=partial)

    def gelu(self, ph, out, src, n, reads, writes, tmp, tmp_tk, rows=slice(0, 128)):
        t = tmp[rows, 0:n]
        self.act(t, src, AF.Square, reads, [tmp_tk])
        self.ts(t, t, 0.044715, 1.0, ALU.mult, ALU.add, [tmp_tk], [tmp_tk])
        self.tt(t, t, src, ALU.mult, [tmp_tk] + list(reads), [tmp_tk])
        self.act(t, t, AF.Sigmoid, [tmp_tk], [tmp_tk], scale=GC)
        self.tt(out, t, src, ALU.mult, [tmp_tk] + list(reads), writes, partial=True)

    def build(self):
        nc, sc, I, Sx = self.nc, self.sc, self.I, self.Sx
        P0 = Phase(self, "g")
        self.P0 = P0
        self.actT = P0.sb("actT", [128, NK, T], BF16)
        self.tk_act = {(kc, tb): Tk("act%d_%d" % (kc, tb)) for kc in range(NK) for tb in range(5)}
        self.mod = P0.sb("mod", [128, L, 96, 2], F32)
        self.tk_mod = P0.tk("mod")
        self.cols = P0.sb("cols", [128, L, 4, NK, 2], F32)
        self.tk_cols = P0.tk("cols")
        self.ones32 = P0.sb("ones32", [128, 128], F32)
        self.onesb = P0.sb("onesb", [128, 128], BF16)
        self.tk_const = P0.tk("const")
        self.gm = P0.sb("gm", [128, L, NK], F32)
        self.gf = P0.sb("gf", [128, L, NK], F32)
        self.gfin = P0.sb("gfin", [128, NK], F32)
        self.zero1 = P0.sb("zero1", [128, 1], F32)
        sc.op("dve", lambda e: e.memset(self.ones32[:], 1.0), writes=[self.tk_const], partial=True)
        sc.op("dve", lambda e: e.memset(self.onesb[:], 1.0), writes=[self.tk_const], partial=True)
        sc.op("dve", lambda e: e.memset(self.zero1[:], 0.0), writes=[self.tk_const], partial=True)
        tg = P0.tk("gains")
        sc.dma("sp", self.gm[:], I["gmix"].rearrange("l p k -> p l k"), writes=[tg], semt=tg, partial=True)
        sc.dma("sp", self.gf[:], I["gffn"].rearrange("l p k -> p l k"), writes=[tg], semt=tg, partial=True)
        sc.dma("sp", self.gfin[:], I["gfin"], writes=[tg], semt=tg, partial=True)
        self.tk_gains = tg
        for oc in range(NK):
            for tb, (t0, n) in enumerate(TBS):
                tkx = self.tkx[(oc, tb)]
                sc.dma("sp", Sx["xres"][oc * 128:(oc + 1) * 128, t0:t0 + n], I["xT0"][oc * 128:(oc + 1) * 128, t0:t0 + n],
                       writes=[tkx], semt=tg)
        import os
        self.adaln_setup()
        self.adaln_standalone(0)
        for l in range(self.nl):
            self.layer(l)
        if os.environ.get("KFIN", "1") == "1":
            self.norm(None, final=True)
        P0.__exit__()
        sc.barrier()

    def adaln_setup(self):
        sc, I, P0 = self.sc, self.I, self.P0
        cs32 = P0.sb("cs32", [128, NK, 2], F32)
        self.csb = P0.sb("csb", [128, NK, 2], BF16)
        self.tcs = P0.tk("cs")
        self.bT = P0.sb("bT", [128, L, 96], F32)
        self.tbT = P0.tk("bT")
        sc.dma("sp", cs32[:], I["cT"], writes=[self.tcs], semt=self.tcs)
        sc.dma("sp", self.bT[:], I["b_adaT"].rearrange("l p k -> p l k"), writes=[self.tbT], semt=self.tbT)
        self.act(self.csb[:], cs32[:], AF.Silu, [self.tcs], [self.tcs])

    def adaln_begin(self, ph, l):
        st = dict(l=l, wb=[ph.sb("aw%d" % i, [128, NK, 512], BF16) for i in range(2)],
                  twb=[ph.tk("aw%d" % i) for i in range(2)], ps=ph.psum("aps", [128, 512]), tps=ph.tk("aps"),
                  wv=self.I["w_ada"][l].rearrange("(kc p) n -> p kc n", p=128), slot=0)
        return st

    def adaln_slot(self, st):
        s_ = st["slot"]
        if s_ > 24:
            return False
        st["slot"] += 1
        if s_ < 24:
            i = s_ % 2
            self.sc.dma("pool", st["wb"][i][:], st["wv"][:, :, s_ * 512:(s_ + 1) * 512], writes=[st["twb"][i]],
                        semt=st["twb"][i])
        if s_ >= 1:
            nb = s_ - 1
            i = nb % 2
            for s4 in range(4):
                oc = nb * 4 + s4
                for kc in range(NK):
                    self.mm(st["ps"][:, oc * 2:oc * 2 + 2], st["wb"][i][:, kc, s4 * 128:(s4 + 1) * 128],
                            self.csb[:, kc, :], kc == 0, kc == NK - 1, [st["twb"][i], self.tcs], [st["tps"]])
        return True

    def adaln_finish(self, st):
        while self.adaln_slot(st):
            pass
        l, ps, tps = st["l"], st["ps"], st["tps"]
        for r in range(2):
            self.tt(self.mod[:, l, :, r], ps[:, r:192:2], self.bT[:, l, :], ALU.add, [tps, self.tbT], [self.tk_mod],
                    partial=True)
        for r in range(2):
            self.ts(self.cols[:, l, 0, :, r], self.mod[:, l, 16:32, r], 1.0, None, ALU.add, None,
                    [self.tk_mod], [self.tk_cols], partial=True)
            self.tt(self.cols[:, l, 0, :, r], self.cols[:, l, 0, :, r], self.gm[:, l, :], ALU.mult,
                    [self.tk_cols, self.tk_gains], [self.tk_cols])
            self.ts(self.cols[:, l, 2, :, r], self.mod[:, l, 64:80, r], 1.0, None, ALU.add, None,
                    [self.tk_mod], [self.tk_cols], partial=True)
            self.tt(self.cols[:, l, 2, :, r], self.cols[:, l, 2, :, r], self.gf[:, l, :], ALU.mult,
                    [self.tk_cols, self.tk_gains], [self.tk_cols])

    def adaln_standalone(self, l):
        with Phase(self, "ada") as ph:
            st = self.adaln_begin(ph, l)
            self.adaln_finish(st)

    def norm(self, lw, final=False):
        sc, Sx = self.sc, self.Sx
        with Phase(self, "nrm") as ph:
            xt = [ph.sb("xt%d" % i, [128, NK, 512], F32) for i in range(2)]
            txt = [ph.tk("xt%d" % i) for i in range(2)]
            sq = [ph.sb("sq%d" % i, [128, 512], F32) for i in range(2)]
            tsq = [ph.tk("sq%d" % i) for i in range(2)]
            rs = ph.sb("rs", [128, 512], F32)
            trs = ph.tk("rs")
            tmp = [ph.sb("tmp%d" % i, [128, 512], F32) for i in range(2)]
            ttmp = [ph.tk("tmp%d" % i) for i in range(2)]
            ps = ph.psum("ps", [128, 512])
            tps = ph.tk("ps")
            xv = Sx["xres"].rearrange("(kc p) t -> p kc t", p=128)
            nblk = 4 if final else 5
            cnt = 0
            for tb in range(nblk):
                t0, n = TBS[tb]
                i = tb % 2
                sc.dma("sp", xt[i][:, :, 0:n], xv[:, :, t0:t0 + n], reads=[self.tkx[(kc, tb)] for kc in range(NK)],
                       writes=[txt[i]], semt=txt[i])
                for kc in range(NK):
                    j = kc % 2
                    self.act(sq[j][:, 0:n], xt[i][:, kc, 0:n], AF.Square, [txt[i]], [tsq[j]])
                    self.mm(ps[:, 0:n], self.ones32[:], sq[j][:, 0:n], kc == 0, kc == NK - 1, [tsq[j], self.tk_const],
                            [tps], inc=True)
                self.ts(rs[:, 0:n], ps[:, 0:n], 1.0 / D, EPS, ALU.mult, ALU.add, [tps], [trs])
                self.act(rs[:, 0:n], rs[:, 0:n], AF.Sqrt, [trs], [trs])
                sc.op("dve", lambda e: e.reciprocal(out=rs[:, 0:n], in_=rs[:, 0:n]), reads=[trs], writes=[trs])
                r = 0 if tb < 4 else 1
                for kc in range(NK):
                    j = cnt % 2
                    cnt += 1
                    if final:
                        self.stt(tmp[j][:, 0:n], xt[i][:, kc, 0:n], self.gfin[:, kc:kc + 1], rs[:, 0:n], ALU.mult,
                                 ALU.mult, [txt[i], trs, self.tk_gains], [ttmp[j]])
                        sc.dma("sp", self.out[kc * 128:(kc + 1) * 128, t0:t0 + n], tmp[j][:, 0:n], reads=[ttmp[j]],
                               semt=ttmp[j])
                    else:
                        l, which = lw
                        self.stt(tmp[j][:, 0:n], xt[i][:, kc, 0:n], self.cols[:, l, which, kc, r:r + 1], rs[:, 0:n],
                                 ALU.mult, ALU.mult, [txt[i], trs, self.tk_cols], [ttmp[j]])
                        sh = 0 if which == 0 else 48
                        self.act(self.actT[:, kc, t0:t0 + n], tmp[j][:, 0:n], AF.Identity, [ttmp[j], self.tk_mod],
                                 [self.tk_act[(kc, tb)]], bias=self.mod[:, l, sh + kc, r:r + 1])

    def layer(self, l):
        import os
        stop = int(os.environ.get("KSTOP", "99"))
        steps = [lambda: self.norm((l, 0)), lambda: self.proj_in(l), lambda: self.lru(l), lambda: self.attn(l),
                 lambda: self.sgu(l), lambda: self.proj_res(l, "w_out", 32), lambda: self.norm((l, 2)),
                 lambda: self.ffn(l)]
        for i, st in enumerate(steps):
            if i < stop:
                st()

    def proj_in(self, l):
        sc, I, Sx, tkd = self.sc, self.I, self.Sx, self.tkd
        with Phase(self, "pin") as ph:
            wb = [ph.sb("w%d" % i, [128, NK, 512], BF16) for i in range(2)]
            twb = [ph.tk("w%d" % i) for i in range(2)]
            cos = ph.sb("cos", [128, S], F32)
            sin = ph.sb("sin", [128, S], F32)
            perm = ph.sb("perm", [128, 128], BF16)
            tc_ = ph.tk("c")
            sc.dma("sp", cos[:], I["cos"], writes=[tc_], semt=tc_, partial=True)
            sc.dma("sp", sin[:], I["sin"], writes=[tc_], semt=tc_, partial=True)
            tpm = ph.tk("perm")
            sc.dma("pool", perm[:], I["perm"], writes=[tpm], semt=tpm)
            lng = ph.sb("lng", [128, 512], F32)
            lnb = ph.sb("lnb", [128, 512], F32)
            tln = ph.tk("ln")
            sc.dma("sp", lng[:], I["slng"][l:l + 1, :].broadcast_to([128, 512]), writes=[tln], semt=tln, partial=True)
            sc.dma("sp", lnb[:], I["slnb"][l:l + 1, :].broadcast_to([128, 512]), writes=[tln], semt=tln, partial=True)
            st32 = [ph.sb("st32_%d" % i, [128, T], F32) for i in range(2)]
            tst32 = [ph.tk("st32_%d" % i) for i in range(2)]
            stb = [ph.sb("stb%d" % i, [128, T], BF16) for i in range(4)]
            tstb = [ph.tk("stb%d" % i) for i in range(4)]
            t1 = [ph.sb("t1_%d" % i, [128, 512], F32) for i in range(2)]
            tt1 = [ph.tk("t1_%d" % i) for i in range(2)]
            t2 = [ph.sb("t2_%d" % i, [128, 512], F32) for i in range(2)]
            tt2 = [ph.tk("t2_%d" % i) for i in range(2)]
            vst = [ph.sb("vst%d" % i, [128, 512], BF16) for i in range(2)]
            tvst = [ph.tk("vst%d" % i) for i in range(2)]
            stat = ph.sb("stat", [128, 8], F32)
            tstat = ph.tk("stat")
            ps = [ph.psum("ps%d" % i, [128, 512]) for i in range(4)]
            tps = [ph.tk("ps%d" % i) for i in range(4)]
            psw = [ph.psum("psw%d" % i, [128, 512]) for i in range(2)]
            tpsw = [ph.tk("psw%d" % i) for i in range(2)]
            wv = I["w_in"][l].rearrange("(kc p) n -> p kc n", p=128)
            pc = 0
            c32 = 0
            cb16 = 0
            cq = 0
            import os
            blks = [int(x) for x in os.environ.get("KBLK", "0,1,2,3,4,5,6,7,8,9").split(",")]
            for blk in blks:
                i = blk % 2
                sc.dma("pool", wb[i][:], wv[:, :, blk * 512:(blk + 1) * 512], writes=[twb[i]], semt=twb[i])
                if blk in (6, 7, 9):
                    for tt_ in range(18):
                        p = pc % 4
                        pc += 1
                        tb = min(tt_ // 4, 4)
                        for kc in range(NK):
                            self.mm(ps[p][:], self.actT[:, kc, tt_ * 128:(tt_ + 1) * 128], wb[i][:, kc, :], kc == 0,
                                    kc == NK - 1, [self.tk_act[(kc, tb)], twb[i]], [tps[p]])
                        j = tt_ % 2
                        if blk in (6, 7):
                            self.act(vst[j][:], ps[p][:], AF.Copy, [tps[p]], [tvst[j]])
                            c0 = (blk - 6) * 512
                            sc.dma("sp", Sx["VV"][tt_ * 128:(tt_ + 1) * 128, c0:c0 + 512], vst[j][:], reads=[tvst[j]],
                                   writes=[tkd["VV"]], semt=tvst[j], partial=True)
                        else:
                            g = t1[j]
                            self.act(g[:], ps[p][:], AF.Copy, [tps[p]], [tt1[j]])
                            self.gelu(ph, g[:], g[:], 512, [tt1[j]], [tt1[j]], t2[j], tt2[j])
                            sc.op("dve", lambda e: e.bn_stats(out=stat[:, 0:6], in_=g[:]), reads=[tt1[j]], writes=[tstat])
                            sc.op("dve", lambda e: e.bn_aggr(out=stat[:, 6:8], in_=stat[:, 0:6]), reads=[tstat],
                                  writes=[tstat])
                            self.ts(stat[:, 7:8], stat[:, 7:8], EPS, None, ALU.add, None, [tstat], [tstat])
                            self.act(stat[:, 7:8], stat[:, 7:8], AF.Sqrt, [tstat], [tstat])
                            sc.op("dve", lambda e: e.reciprocal(out=stat[:, 7:8], in_=stat[:, 7:8]), reads=[tstat],
                                  writes=[tstat])
                            self.ts(g[:], g[:], stat[:, 6:7], stat[:, 7:8], ALU.subtract, ALU.mult, [tt1[j], tstat],
                                    [tt1[j]])
                            self.tt(g[:], g[:], lng[:], ALU.mult, [tt1[j], tln], [tt1[j]])
                            self.tt(vst[j][:], g[:], lnb[:], ALU.add, [tt1[j], tln], [tvst[j]])
                            sc.dma("sp", Sx["SV"][tt_ * 128:(tt_ + 1) * 128, :], vst[j][:], reads=[tvst[j]],
                                   writes=[tkd["SV"]], semt=tvst[j], partial=True)
                    continue
                for s4 in range(4):
                    col = blk * 512 + s4 * 128
                    kind = ("lx", "lg", "q", "q", "k", "k", None, None, "su")[blk]
                    if kind in ("lx", "lg"):
                        so = st32[c32 % 2]
                        tso = tst32[c32 % 2]
                        c32 += 1
                    elif kind == "su":
                        so = stb[cb16 % 4]
                        tso = tstb[cb16 % 4]
                        cb16 += 1
                    else:
                        so = stb[cb16 % 4]
                        tso = tstb[cb16 % 4]
                        sr = stb[(cb16 + 1) % 4]
                        tsr = tstb[(cb16 + 1) % 4]
                        cb16 += 2
                    for tb, (t0, n) in enumerate(TBS):
                        p = pc % 4
                        pc += 1
                        for kc in range(NK):
                            self.mm(ps[p][:, 0:n], wb[i][:, kc, s4 * 128:(s4 + 1) * 128], self.actT[:, kc, t0:t0 + n],
                                    kc == 0, kc == NK - 1, [self.tk_act[(kc, tb)], twb[i]], [tps[p]])
                        if kind in ("lx", "lg"):
                            self.act(so[:, t0:t0 + n], ps[p][:, 0:n], AF.Copy, [tps[p]], [tso], partial=True)
                        elif kind == "su":
                            j = cq % 2
                            cq += 1
                            self.act(t1[j][:, 0:n], ps[p][:, 0:n], AF.Copy, [tps[p]], [tt1[j]])
                            self.gelu(ph, so[:, t0:t0 + n], t1[j][:, 0:n], n, [tt1[j]], [tso], t2[j], tt2[j])
                        else:
                            sclq = 0.125 if kind == "q" else 1.0
                            self.act(so[:, t0:t0 + n], ps[p][:, 0:n], AF.Copy, [tps[p]], [tso], partial=True, scale=sclq)
                            if tb < 4 and os.environ.get("KQ", "") != "raw":
                                j = cq % 2
                                cq += 1
                                self.act(t1[j][:, 0:n], ps[p][:, 0:n], AF.Copy, [tps[p]], [tt1[j]], scale=sclq)
                                self.tt(t1[j][:, 0:n], t1[j][:, 0:n], cos[:, t0:t0 + n], ALU.mult, [tt1[j], tc_], [tt1[j]])
                                self.mm(psw[j][:, 0:n], perm[:], so[:, t0:t0 + n], True, True, [tso, tpm], [tpsw[j]])
                                self.act(t2[j][:, 0:n], psw[j][:, 0:n], AF.Copy, [tpsw[j]], [tt2[j]])
                                self.tt(t2[j][:, 0:n], t2[j][:, 0:n], sin[:, t0:t0 + n], ALU.mult, [tt2[j], tc_], [tt2[j]])
                                self.tt(sr[:, t0:t0 + n], t1[j][:, 0:n], t2[j][:, 0:n], ALU.add, [tt1[j], tt2[j]],
                                        [tsr], partial=True)
                    if kind == "lx":
                        sc.dma("sp", Sx["LX"][s4 * 128:(s4 + 1) * 128, :], so[:], reads=[tso], writes=[tkd["LX"]], semt=tso,
                               partial=True)
                    elif kind == "lg":
                        sc.dma("sp", Sx["LG"][s4 * 128:(s4 + 1) * 128, :], so[:], reads=[tso], writes=[tkd["LG"]], semt=tso,
                               partial=True)
                    elif kind == "su":
                        sc.dma("sp", Sx["SU"][s4 * 128:(s4 + 1) * 128, :], so[:], reads=[tso], writes=[tkd["SU"]], semt=tso,
                               partial=True)
                    elif kind == "q":
                        r0 = (blk - 2) * 512 + s4 * 128
                        sc.dma("sp", Sx["QW"][r0:r0 + 128, :], so[:], reads=[tso], writes=[tkd["QW"]], semt=tso, partial=True)
                        if os.environ.get("KQ", "") == "":
                            sc.dma("sp", Sx["QR"][r0:r0 + 128, :], sr[:, 0:S], reads=[tsr], writes=[tkd["QR"]], semt=tsr,
                                   partial=True)
                    elif kind == "k":
                        r0 = (blk - 4) * 512 + s4 * 128
                        sc.dma("sp", Sx["KK"][r0:r0 + 128, S:T], so[:, S:T], reads=[tso], writes=[tkd["KK"]], semt=tso,
                               partial=True)
                        sc.dma("sp", Sx["KK"][r0:r0 + 128, 0:S], sr[:, 0:S], reads=[tsr], writes=[tkd["KK"]], semt=tsr,
                               partial=True)

    def lru(self, l):
        sc, I, Sx, tkd = self.sc, self.I, self.Sx, self.tkd
        with Phase(self, "lru") as ph:
            names = ["X", "Y", "XC", "Rr", "Ri", "A", "Q", "Bv", "H0", "H1"]
            tl = {n_: ph.sb(n_, [128, T], F32) for n_ in names}
            tkl = {n_: ph.tk(n_) for n_ in names}
            ob = ph.sb("ob", [128, T], BF16)
            tob = ph.tk("ob")
            bd = ph.sb("bd", [128, 16, 128], F32)
            tbd = ph.tk("bd")
            cw = ph.sb("cw", [128, 4, 4], F32)
            cb = ph.sb("cb", [128, 4], F32)
            gb = ph.sb("gb", [128, 2, 2, 4], F32)
            lam = ph.sb("lam", [128, 2, 4], F32)
            cl = ph.sb("cl", [128, 2, 4], F32)
            tpar = ph.tk("par")
            tcl = ph.tk("cl")
            ps = [ph.psum("ps%d" % i, [128, 512]) for i in range(2)]
            tps = [ph.tk("ps%d" % i) for i in range(2)]
            sc.op("dve", lambda e: e.memset(bd[:], 0.0), writes=[tbd])
            for d in range(2):
                for j in range(2):
                    for c in range(4):
                        for hb in range(2):
                            sc.dma("sp", bd[hb * 64:(hb + 1) * 64, (d * 2 + j) * 4 + c, hb * 64:(hb + 1) * 64],
                                   I["lgw"][l, d, j, 2 * c + hb], writes=[tbd], semt=tbd, partial=(not (d == 0 and j == 0 and c == 0 and hb == 0)))
            for dst, src in ((cw, "lcw"), (cb, "lcb"), (gb, "lgb"), (lam, "llam")):
                sc.dma("sp", dst[:], I[src][l], writes=[tpar], semt=tpar, partial=True)
            self.act(cl[:], lam[:], AF.Exp, [tpar], [tcl], scale=-1.0)
            self.act(cl[:], cl[:], AF.Ln, [tcl], [tcl], bias=1.0)
            self.ts(cl[:], cl[:], -8.0, None, ALU.mult, None, [tcl], [tcl])
            X, Y, XC, Rr, Ri, A, Q, Bv, H0, H1 = [tl[n_] for n_ in names]
            pcnt = 0
            for c in range(4):
                sc.dma("sp", X[:], Sx["LX"][c * 128:(c + 1) * 128, :], reads=[tkd["LX"]], writes=[tkl["X"]], semt=tkl["X"])
                sc.dma("sp", Y[:], Sx["LG"][c * 128:(c + 1) * 128, :], reads=[tkd["LG"]], writes=[tkl["Y"]], semt=tkl["Y"])
                self.ts(XC[:], X[:], cw[:, c, 2:3], cb[:, c:c + 1], ALU.mult, ALU.add, [tkl["X"], tpar], [tkl["XC"]])
                for (r0, n) in ((0, S), (S, CL)):
                    for k in (0, 1, 3):
                        off = k - 2
                        d0 = max(0, -off)
                        d1 = n - max(0, off)
                        self.stt(XC[:, r0 + d0:r0 + d1], X[:, r0 + d0 + off:r0 + d1 + off], cw[:, c, k:k + 1],
                                 XC[:, r0 + d0:r0 + d1], ALU.mult, ALU.add, [tkl["X"], tkl["XC"], tpar], [tkl["XC"]])
                for d in range(2):
                    for j, Rt, trt in ((0, Rr, tkl["Rr"]), (1, Ri, tkl["Ri"])):
                        for tb, (t0, n) in enumerate(TBS):
                            p = pcnt % 2
                            pcnt += 1
                            self.mm(ps[p][:, 0:n], bd[:, (d * 2 + j) * 4 + c, :], XC[:, t0:t0 + n], True, True,
                                    [tbd, tkl["XC"]], [tps[p]])
                            self.act(Rt[:, t0:t0 + n], ps[p][:, 0:n], AF.Sigmoid, [tps[p], tpar], [trt],
                                     partial=(tb > 0), bias=gb[:, d, j, c:c + 1])
                    self.act(A[:], Rr[:], AF.Exp, [tkl["Rr"], tcl], [tkl["A"]], scale=cl[:, d, c:c + 1])
                    self.act(Q[:], A[:], AF.Square, [tkl["A"]], [tkl["Q"]])
                    self.act(Q[:], Q[:], AF.Sqrt, [tkl["Q"]], [tkl["Q"]], scale=-1.0, bias=1.0)
                    self.tt(Bv[:], Ri[:], XC[:], ALU.mult, [tkl["Ri"], tkl["XC"]], [tkl["Bv"]])
                    self.tt(Bv[:], Bv[:], Q[:], ALU.mult, [tkl["Bv"], tkl["Q"]], [tkl["Bv"]])
                    if d == 0:
                        sc.op("dve", lambda e: e.tensor_tensor_scan(out=H0[:, S:T], data0=A[:, S:T], data1=Bv[:, S:T],
                                                                    initial=0.0, op0=ALU.mult, op1=ALU.add),
                              reads=[tkl["A"], tkl["Bv"]], writes=[tkl["H0"]])
                        sc.op("dve", lambda e: e.tensor_tensor_scan(out=H0[:, 0:S], data0=A[:, 0:S], data1=Bv[:, 0:S],
                                                                    initial=H0[:, T - 1:T], op0=ALU.mult, op1=ALU.add),
                              reads=[tkl["A"], tkl["Bv"], tkl["H0"]], writes=[tkl["H0"]], partial=True)
                    else:
                        sc.op("dve", lambda e: e.tensor_tensor_scan(out=H1[:, S:T][:, ::-1], data0=A[:, S:T][:, ::-1],
                                                                    data1=Bv[:, S:T][:, ::-1], initial=0.0,
                                                                    op0=ALU.mult, op1=ALU.add),
                              reads=[tkl["A"], tkl["Bv"]], writes=[tkl["H1"]])
                        sc.op("dve", lambda e: e.tensor_tensor_scan(out=H1[:, 0:S][:, ::-1], data0=A[:, 0:S][:, ::-1],
                                                                    data1=Bv[:, 0:S][:, ::-1], initial=H1[:, S:S + 1],
                                                                    op0=ALU.mult, op1=ALU.add),
                              reads=[tkl["A"], tkl["Bv"], tkl["H1"]], writes=[tkl["H1"]], partial=True)
                self.tt(H0[:], H0[:], H1[:], ALU.add, [tkl["H0"], tkl["H1"]], [tkl["H0"]])
                self.gelu(ph, H1[:], Y[:], T, [tkl["Y"]], [tkl["H1"]], Q, tkl["Q"])
                self.tt(ob[:], H0[:], H1[:], ALU.mult, [tkl["H0"], tkl["H1"]], [tob])
                sc.dma("sp", Sx["MIX"][c * 128:(c + 1) * 128, :], ob[:], reads=[tob], writes=[tkd["MIX"]], semt=tob,
                       partial=True)

    def attn(self, l):
        sc, I, Sx, tkd = self.sc, self.I, self.Sx, self.tkd
        need_ctx = l < L - 1
        with Phase(self, "att") as ph:
            NB = 2
            qr = [ph.sb("qr%d" % i, [128, S], BF16) for i in range(NB)]
            qw = [ph.sb("qw%d" % i, [128, T], BF16) for i in range(NB)]
            kk = [ph.sb("kk%d" % i, [128, T], BF16) for i in range(NB)]
            vv = [ph.sb("vv%d" % i, [128, 18, 128], BF16) for i in range(NB)]
            tin = [ph.tk("in%d" % i) for i in range(NB)]
            bias = [ph.sb("bias%d" % i, [128, 21, 128], F32) for i in range(2)]
            tbias = [ph.tk("bias%d" % i) for i in range(2)]
            ein = [ph.sb("ein%d" % i, [128, 640], F32) for i in range(2)]
            tein = [ph.tk("ein%d" % i) for i in range(2)]
            pt = [ph.sb("pt%d" % i, [128, 896], BF16) for i in range(2)]
            tpt = [ph.tk("pt%d" % i) for i in range(2)]
            rd = [ph.sb("rd%d" % i, [128, 256], F32) for i in range(2)]
            trd = [ph.tk("rd%d" % i) for i in range(2)]
            ob = [ph.sb("ob%d" % i, [128, T], BF16) for i in range(2)]
            tob = [ph.tk("ob%d" % i) for i in range(2)]
            psS = [ph.psum("S%d" % i, [128, 1024]) for i in range(2)]
            tpsS = [ph.tk("S%d" % i) for i in range(2)]
            psO = [ph.psum("O%d" % i, [128, 512]) for i in range(2)]
            tpsO = [ph.tk("O%d" % i) for i in range(2)]
            cnt = 0
            hcnt = 0
            ada = self.adaln_begin(ph, l + 1) if l + 1 < self.nl else None
            for hp in range(8):
                ib = hp % NB
                r0 = hp * 128
                t_ = tin[ib]
                sc.dma("sp", qr[ib][:], Sx["QR"][r0:r0 + 128, :], reads=[tkd["QR"]], writes=[t_], semt=t_)
                sc.dma("sp", qw[ib][:], Sx["QW"][r0:r0 + 128, :], reads=[tkd["QW"]], writes=[t_], semt=t_, partial=True)
                sc.dma("sp", kk[ib][:], Sx["KK"][r0:r0 + 128, :], reads=[tkd["KK"]], writes=[t_], semt=t_, partial=True)
                sc.dma("sp", vv[ib][:], Sx["VV"][:, r0:r0 + 128].rearrange("(t p) c -> p t c", p=128), reads=[tkd["VV"]],
                       writes=[t_], semt=t_, partial=True)
                o_ = ob[hp % 2]
                to_ = tob[hp % 2]
                its = []
                for hh in range(2):
                    h = hp * 2 + hh
                    bi = hh
                    first = True
                    for (slot, nt, rho0, which) in NA_PATTERNS:
                        for a in range(2):
                            for b in range(2):
                                rs_ = rho0 + a - b
                                src = I["natab"][l, h, which, rs_:rs_ + 2 * nt - 1:2, :, :].rearrange("t k q -> k t q")
                                sc.dma("sp", bias[bi][a * 64:(a + 1) * 64, slot:slot + nt, b * 64:(b + 1) * 64], src,
                                       writes=[tbias[bi]], semt=tbias[bi], partial=(not first))
                                first = False
                    for m in range(16):
                        its.append((hh, m))
                    if need_ctx:
                        its.append((hh, -1))
                    else:
                        pb = hh * 64
                        sc.op("dve", lambda e: e.memset(o_[pb:pb + 64, S:T], 0.0), writes=[to_], partial=True)

                def front(k):
                    hh, m = its[k]
                    pb = hh * 64
                    bi = hh
                    si = (cnt0 + k) % 2
                    S_ = psS[si]
                    if m >= 0:
                        tiles, slot, _, _ = na_tiles(m)
                        nt = len(tiles)
                        for i_, t in enumerate(tiles):
                            self.mm(S_[:, i_ * 128:(i_ + 1) * 128], kk[ib][pb:pb + 64, t * 128:(t + 1) * 128],
                                    qr[ib][pb:pb + 64, m * 128:(m + 1) * 128], True, True, [t_], [tpsS[si]], inc=False)
                        for j in range(2):
                            self.mm(S_[:, (nt + j) * 128:(nt + j + 1) * 128], kk[ib][pb:pb + 64, S + j * 128:S + (j + 1) * 128],
                                    qw[ib][pb:pb + 64, m * 128:(m + 1) * 128], True, True, [t_], [tpsS[si]], inc=(j == 1))
                        self.tt(ein[si][:, 0:nt * 128].rearrange("p (t q) -> p t q", q=128),
                                S_[:, 0:nt * 128].rearrange("p (t q) -> p t q", q=128),
                                bias[bi][:, slot:slot + nt, :], ALU.add, [tpsS[si], tbias[bi]], [tein[si]])
                        self.act(pt[si][:, 0:nt * 128], ein[si][:, 0:nt * 128], AF.Exp, [tein[si]], [tpt[si]])
                        self.act(pt[si][:, nt * 128:(nt + 2) * 128], S_[:, nt * 128:(nt + 2) * 128], AF.Exp,
                                 [tpsS[si], tein[si]], [tpt[si]], partial=True)
                    else:
                        for j in range(2):
                            self.mm(S_[:, j * 256:(j + 1) * 256], kk[ib][pb:pb + 64, S + j * 128:S + (j + 1) * 128],
                                    qw[ib][pb:pb + 64, S:T], True, True, [t_], [tpsS[si]], inc=(j == 1))
                        self.act(pt[si][:, 0:512], S_[:, 0:512], AF.Exp, [tpsS[si]], [tpt[si]])

                def back(k):
                    hh, m = its[k]
                    pb = hh * 64
                    si = (cnt0 + k) % 2
                    O_ = psO[si]
                    if m >= 0:
                        tiles, slot, _, _ = na_tiles(m)
                        vt = tiles + [16, 17]
                        for i_, t in enumerate(vt):
                            self.mm(O_[:, 0:128], vv[ib][:, t, :], pt[si][:, i_ * 128:(i_ + 1) * 128], i_ == 0,
                                    i_ == len(vt) - 1, [t_, tpt[si]], [tpsO[si]], inc=False)
                        for i_ in range(len(vt)):
                            self.mm(O_[:, 128:256], self.onesb[:], pt[si][:, i_ * 128:(i_ + 1) * 128], i_ == 0,
                                    i_ == len(vt) - 1, [self.tk_const, tpt[si]], [tpsO[si]], inc=(i_ == len(vt) - 1))
                        sc.op("dve", lambda e: e.reciprocal(out=rd[si][pb:pb + 64, 0:128], in_=O_[pb:pb + 64, 128:256]),
                              reads=[tpsO[si]], writes=[trd[si]])
                        self.tt(o_[pb:pb + 64, m * 128:(m + 1) * 128], O_[pb:pb + 64, 0:128], rd[si][pb:pb + 64, 0:128],
                                ALU.mult, [tpsO[si], trd[si]], [to_], partial=True)
                    else:
                        for j in range(2):
                            self.mm(O_[:, 0:256], vv[ib][:, 16 + j, :], pt[si][:, j * 256:(j + 1) * 256], j == 0, j == 1,
                                    [t_, tpt[si]], [tpsO[si]], inc=False)
                        for j in range(2):
                            self.mm(O_[:, 256:512], self.onesb[:], pt[si][:, j * 256:(j + 1) * 256], j == 0, j == 1,
                                    [self.tk_const, tpt[si]], [tpsO[si]], inc=(j == 1))
                        sc.op("dve", lambda e: e.reciprocal(out=rd[si][pb:pb + 64, 0:256], in_=O_[pb:pb + 64, 256:512]),
                              reads=[tpsO[si]], writes=[trd[si]])
                        self.tt(o_[pb:pb + 64, S:T], O_[pb:pb + 64, 0:256], rd[si][pb:pb + 64, 0:256], ALU.mult,
                                [tpsO[si], trd[si]], [to_], partial=True)

                cnt0 = cnt
                front(0)
                for k in range(len(its)):
                    if k + 1 < len(its):
                        front(k + 1)
                    back(k)
                    if ada is not None and (cnt0 + k) % 10 == 5:
                        self.adaln_slot(ada)
                cnt += len(its)
                mr = 512 + hp * 128
                sc.dma("sp", Sx["MIX"][mr:mr + 128, :], o_[:], reads=[to_], writes=[tkd["MIX"]], semt=to_, partial=True)
            if ada is not None:
                self.adaln_finish(ada)

    def sgu(self, l):
        sc, I, Sx, tkd = self.sc, self.I, self.Sx, self.tkd
        with Phase(self, "sgu") as ph:
            su = ph.sb("su", [128, 4, T], BF16)
            sv = ph.sb("sv", [128, 18, 512], BF16)
            tin = ph.tk("in")
            wT = ph.sb("wT", [128, 8, 128], BF16)
            twT = ph.tk("wT")
            bs = ph.sb("bs", [1, 1024], F32)
            tbs = ph.tk("bs")
            ob = ph.sb("ob", [128, 4, T], BF16)
            tob = ph.tk("ob")
            ps = [ph.psum("ps%d" % i, [128, 512]) for i in range(2)]
            tps = [ph.tk("ps%d" % i) for i in range(2)]
            sc.dma("sp", su[:], Sx["SU"].rearrange("(c p) t -> p c t", p=128), reads=[tkd["SU"]], writes=[tin], semt=tin)
            sc.dma("sp", sv[:], Sx["SV"].rearrange("(n p) c -> p n c", p=128), reads=[tkd["SV"]], writes=[tin], semt=tin,
                   partial=True)
            sc.dma("pool", wT[:], I["swT"][l].rearrange("g q p -> q g p"), writes=[twT], semt=twT)
            sc.dma("sp", bs[:], I["sbs"][l], writes=[tbs], semt=tbs)
            cnt = 0
            for n_ in range(18):
                for gp in range(4):
                    p = cnt % 2
                    cnt += 1
                    for hh in range(2):
                        g = gp * 2 + hh
                        o = ps[p][:, hh * 128:(hh + 1) * 128]
                        self.mm(o, sv[:, n_, gp * 128:(gp + 1) * 128], wT[:, g, :], True, False, [tin, twT], [tps[p]],
                                inc=False)
                        self.mm(o, self.ones32[0:1, :], bs[0:1, g * 128:(g + 1) * 128], False, True,
                                [self.tk_const, tbs], [tps[p]], inc=(hh == 1))
                    for hh in range(2):
                        pb = hh * 64
                        self.tt(ob[pb:pb + 64, gp, n_ * 128:(n_ + 1) * 128], ps[p][pb:pb + 64, hh * 128:(hh + 1) * 128],
                                su[pb:pb + 64, gp, n_ * 128:(n_ + 1) * 128], ALU.mult, [tps[p], tin], [tob], partial=True)
            sc.dma("sp", Sx["MIX"][1536:2048, :].rearrange("(c p) t -> p c t", p=128), ob[:], reads=[tob],
                   writes=[tkd["MIX"]], semt=tob, partial=True)

    def proj_res(self, l, wname, gate_base):
        sc, I, Sx, tkd = self.sc, self.I, self.Sx, self.tkd
        last = (l == L - 1)
        with Phase(self, "pr") as ph:
            wb = [ph.sb("w%d" % i, [128, NK, 512], BF16) for i in range(2)]
            twb = [ph.tk("w%d" % i) for i in range(2)]
            xb = [ph.sb("xb%d" % i, [128, 512], F32) for i in range(6)]
            txb = [ph.tk("xb%d" % i) for i in range(6)]
            ps = [ph.psum("ps%d" % i, [128, 512]) for i in range(4)]
            tps = [ph.tk("ps%d" % i) for i in range(4)]
            tload = ph.tk("ld")
            for kc in range(NK):
                sc.dma("sp", self.actT[:, kc, :], Sx["MIX"][kc * 128:(kc + 1) * 128, :], reads=[tkd["MIX"]],
                       writes=[self.tk_act[(kc, tb)] for tb in range(5)], semt=tload)
            wv = I[wname][l].rearrange("(kc p) n -> p kc n", p=128)
            cnt = 0
            for cb in range(4):
                i = cb % 2
                sc.dma("pool", wb[i][:], wv[:, :, cb * 512:(cb + 1) * 512], writes=[twb[i]], semt=twb[i])
                for s4 in range(4):
                    oc = cb * 4 + s4
                    for tb, (t0, n) in enumerate(TBS):
                        if last and tb == 4:
                            continue
                        p = cnt % 4
                        xi = cnt % 6
                        cnt += 1
                        tkx = self.tkx[(oc, tb)]
                        sc.dma("sp", xb[xi][:, 0:n], Sx["xres"][oc * 128:(oc + 1) * 128, t0:t0 + n], reads=[tkx],
                               writes=[txb[xi]], semt=txb[xi])
                        for kc in range(NK):
                            self.mm(ps[p][:, 0:n], wb[i][:, kc, s4 * 128:(s4 + 1) * 128], self.actT[:, kc, t0:t0 + n],
                                    kc == 0, kc == NK - 1, [self.tk_act[(kc, tb)], twb[i]], [tps[p]])
                        r = 0 if tb < 4 else 1
                        self.stt(xb[xi][:, 0:n], ps[p][:, 0:n], self.mod[:, l, gate_base + oc, r:r + 1], xb[xi][:, 0:n],
                                 ALU.mult, ALU.add, [tps[p], txb[xi], self.tk_mod], [txb[xi]])
                        sc.dma("act", Sx["xres"][oc * 128:(oc + 1) * 128, t0:t0 + n], xb[xi][:, 0:n], reads=[txb[xi]],
                               writes=[tkx], semt=txb[xi])

    def ffn(self, l):
        sc, I, Sx = self.sc, self.I, self.Sx
        last = (l == L - 1)
        nblk = 4 if last else 5
        Tn = S if last else T
        with Phase(self, "ffn") as ph:
            GJ = 6
            wu = [ph.sb("wu%d" % i, [128, 2, NK, 256], BF16) for i in range(2)]
            twu = [ph.tk("wu%d" % i) for i in range(2)]
            wd = ph.sb("wd", [128, GJ, D], BF16)
            twd = ph.tk("wd")
            ag = ph.sb("ag", [128, GJ, T], BF16)
            tag = [ph.tk("ag%d" % i) for i in range(GJ)]
            E = ph.sb("E", [128, 5, 2], F32)
            tE = ph.tk("E")
            Ya = ph.sb("Ya", [128, T], F32)
            tYa = ph.tk("Ya")
            Yg = ph.sb("Yg", [128, T], F32)
            tYg = ph.tk("Yg")
            fcw = ph.sb("fcw", [128, 86, 3], F32)
            fcb = ph.sb("fcb", [128, 86], F32)
            tpar = ph.tk("par")
            xb = [ph.sb("xb%d" % i, [128, 512], F32) for i in range(6)]
            txb = [ph.tk("xb%d" % i) for i in range(6)]
            ps = [ph.psum("ps%d" % i, [128, 512]) for i in range(4)]
            tps = [ph.tk("ps%d" % i) for i in range(4)]
            sc.dma("sp", fcw[:], I["fcw"][l], writes=[tpar], semt=tpar, partial=True)
            sc.dma("sp", fcb[:], I["fcb"][l], writes=[tpar], semt=tpar, partial=True)
            uv = I["ffn_up"][l].rearrange("(kc p) n -> p kc n", p=128)
            dv = I["ffn_down"][l].rearrange("(j p) n -> p j n", p=128)
            ranges = [(0, S)] if last else [(0, S), (S, CL)]
            pc = 0
            xc = 0
            blocks = []
            j_done = 0
            gi = 0
            while j_done < NJ:
                gj = min(GJ, NJ - j_done)
                jj = 0
                while jj < gj:
                    nb = min(2, gj - jj)
                    blocks.append(dict(j0=j_done + jj, nb=nb, jj=jj, g=gi, gj=gj, gstart=j_done, first=(jj == 0),
                                       last=(jj + nb >= gj)))
                    jj += nb
                j_done += gj
                gi += 1

            def load_wu(bidx):
                bk = blocks[bidx]
                w_ = wu[bidx % 2]
                tw_ = twu[bidx % 2]
                j0, nb = bk["j0"], bk["nb"]
                sc.dma("pool", w_[:, 0, :, 0:nb * 128], uv[:, :, j0 * 128:(j0 + nb) * 128], writes=[tw_], semt=tw_)
                sc.dma("pool", w_[:, 1, :, 0:nb * 128], uv[:, :, FH + j0 * 128:FH + (j0 + nb) * 128], writes=[tw_],
                       semt=tw_, partial=True)

            load_wu(0)
            load_wu(1)
            for bidx, bk in enumerate(blocks):
                j0, nb, jj, gj = bk["j0"], bk["nb"], bk["jj"], bk["gj"]
                if bk["first"]:
                    sc.dma("pool", wd[:, 0:gj, :], dv[:, bk["gstart"]:bk["gstart"] + gj, :], writes=[twd], semt=twd)
                w_ = wu[bidx % 2]
                tw_ = twu[bidx % 2]
                if True:
                    for q in range(nb):
                        j = j0 + q
                        for half, Y_, tY_ in ((0, Ya, tYa), (1, Yg, tYg)):
                            ci = half * NJ + j
                            for tb in range(nblk):
                                t0, n = TBS[tb]
                                p = pc % 4
                                pc += 1
                                for kc in range(NK):
                                    self.mm(ps[p][:, 0:n], w_[:, half, kc, q * 128:(q + 1) * 128],
                                            self.actT[:, kc, t0:t0 + n], kc == 0, kc == NK - 1,
                                            [self.tk_act[(kc, tb)], tw_], [tps[p]])
                                P_ = ps[p]
                                self.ts(Y_[:, t0:t0 + n], P_[:, 0:n], fcw[:, ci, 1:2], fcb[:, ci:ci + 1], ALU.mult,
                                        ALU.add, [tps[p], tpar], [tY_], partial=(tb > 0))
                                self.stt(Y_[:, t0 + 1:t0 + n], P_[:, 0:n - 1], fcw[:, ci, 0:1], Y_[:, t0 + 1:t0 + n],
                                         ALU.mult, ALU.add, [tps[p], tY_, tpar], [tY_])
                                self.stt(Y_[:, t0:t0 + n - 1], P_[:, 1:n], fcw[:, ci, 2:3], Y_[:, t0:t0 + n - 1],
                                         ALU.mult, ALU.add, [tps[p], tY_, tpar], [tY_])
                                sc.op("dve", lambda e: e.tensor_copy(out=E[:, tb, 0:2], in_=P_[:, 0:n:n - 1]),
                                      reads=[tps[p]], writes=[tE])
                            self.stt(Y_[:, 512:2048:512], E[:, 0:3, 1], fcw[:, ci, 0:1], Y_[:, 512:2048:512],
                                     ALU.mult, ALU.add, [tE, tY_, tpar], [tY_])
                            self.stt(Y_[:, 511:2047:512], E[:, 1:4, 0], fcw[:, ci, 2:3], Y_[:, 511:2047:512],
                                     ALU.mult, ALU.add, [tE, tY_, tpar], [tY_])
                        self.act(Yg[:, 0:Tn], Yg[:, 0:Tn], AF.Silu, [tYg], [tYg])
                        self.tt(ag[:, jj + q, 0:Tn], Yg[:, 0:Tn], Ya[:, 0:Tn], ALU.mult, [tYg, tYa], [tag[jj + q]])
                if bidx + 2 < len(blocks):
                    load_wu(bidx + 2)
                if not bk["last"]:
                    continue
                self.settle_x(txb)
                for oc in range(NK):
                    for tb in range(nblk):
                        t0, n = TBS[tb]
                        p = pc % 4
                        pc += 1
                        xi = xc % 6
                        xc += 1
                        tkx = self.tkx[(oc, tb)]
                        sc.dma("sp", xb[xi][:, 0:n], Sx["xres"][oc * 128:(oc + 1) * 128, t0:t0 + n], reads=[tkx],
                               writes=[txb[xi]], semt=txb[xi])
                        for q in range(gj):
                            self.mm(ps[p][:, 0:n], wd[:, q, oc * 128:(oc + 1) * 128], ag[:, q, t0:t0 + n], q == 0,
                                    q == gj - 1, [twd, tag[q]], [tps[p]])
                        r = 0 if tb < 4 else 1
                        self.stt(xb[xi][:, 0:n], ps[p][:, 0:n], self.mod[:, l, 80 + oc, r:r + 1], xb[xi][:, 0:n],
                                 ALU.mult, ALU.add, [tps[p], txb[xi], self.tk_mod], [txb[xi]])
                        sc.dma("act", Sx["xres"][oc * 128:(oc + 1) * 128, t0:t0 + n], xb[xi][:, 0:n], reads=[txb[xi]],
                               writes=[tkx], semt=txb[xi])


def _cols(v, nchunk):
    return np.ascontiguousarray(np.moveaxis(v.reshape(v.shape[:-1] + (nchunk, 128)), -1, -2))


def host_shared(inp):
    f = lambda a: np.ascontiguousarray(a, dtype=np.float32)
    sh = {}
    sh["w_ada"] = f(inp["w_ada"])
    sh["b_adaT"] = _cols(f(inp["b_ada"]), 96)
    sh["gmix"] = _cols(f(inp["norm_mix_g"]), NK)
    sh["gffn"] = _cols(f(inp["norm_ffn_g"]), NK)
    sh["gfin"] = _cols(f(inp["final_norm_g"]), NK)
    sh["w_in"] = f(inp["w_in"])
    sh["w_out"] = f(inp["w_out"])
    sh["ffn_up"] = f(inp["ffn_up"])
    sh["ffn_down"] = f(inp["ffn_down"])
    sh["lcw"] = np.ascontiguousarray(f(inp["lru_conv_w"]).reshape(L, 4, 4, 128).transpose(0, 3, 2, 1))
    sh["lcb"] = _cols(f(inp["lru_conv_b"]), 4)
    sh["lgw"] = f(inp["lru_gate_w"])
    sh["lgb"] = np.ascontiguousarray(f(inp["lru_gate_b"]).reshape(L, 2, 2, 4, 128).transpose(0, 4, 1, 2, 3))
    sh["llam"] = np.ascontiguousarray(f(inp["lru_lambda"]).reshape(L, 2, 4, 128).transpose(0, 3, 1, 2))
    rpb = f(inp["na_rpb"])
    kc = np.arange(64)[:, None]
    qc = np.arange(64)[None, :]
    cs = np.clip(qc - 8, 0, 48)
    inwin = (kc >= cs) & (kc < cs + 16)
    rel = np.clip(kc - qc + 15, 0, 30)
    tab = rpb[:, :, :, rel]
    tab = np.where(inwin[None, None, None], tab, np.float32(NEG))
    full = np.full((L, 16, 2, 16, 64, 64), NEG, np.float32)
    full[:, :, 0, :15] = tab
    full[:, :, 1, 3:11] = tab[:, :, 3:11]
    sh["natab"] = full
    t = np.arange(S)
    pos = np.stack([t // 64, t % 64], -1).astype(np.float32)
    inv = (10000.0 ** (-np.arange(16, dtype=np.float32) / 16)).astype(np.float32)
    ang = pos[:, :, None] * inv
    cos = np.cos(ang).astype(np.float32)
    sin = np.sin(ang).astype(np.float32)
    d = np.arange(128) % 64
    a_ = d // 32
    f_ = d % 16
    sh["cos"] = np.ascontiguousarray(cos[:, a_, f_].T)
    sh["sin"] = np.ascontiguousarray(sin[:, a_, f_].T)
    pm = np.zeros((128, 128), np.float32)
    for o in range(128):
        p_ = (o % 32) // 16
        partner = o + 16 if p_ == 0 else o - 16
        pm[partner, o] = -1.0 if p_ == 0 else 1.0
    sh["perm"] = pm
    sh["slng"] = f(inp["sgu_ln_g"])
    sh["slnb"] = f(inp["sgu_ln_b"])
    sh["swT"] = np.ascontiguousarray(f(inp["sgu_w"]).transpose(0, 1, 3, 2))
    sh["sbs"] = f(inp["sgu_b"]).reshape(L, 1, 1024)
    sh["fcw"] = np.ascontiguousarray(f(inp["ffn_conv_w"]).reshape(L, 3, 86, 128).transpose(0, 3, 2, 1))
    sh["fcb"] = _cols(f(inp["ffn_conv_b"]), 86)
    return sh


def host_core(inp, b):
    x = np.asarray(inp["x"][b], np.float32)
    ctx = np.asarray(inp["ctx"][b], np.float32)
    xT0 = np.ascontiguousarray(np.concatenate([x, ctx], 0).T)
    cc = np.stack([np.asarray(inp["c"][b], np.float32), np.asarray(inp["c_ctx"], np.float32)], -1)
    cT = np.ascontiguousarray(cc.reshape(NK, 128, 2).transpose(1, 0, 2))
    return {"xT0": xT0, "cT": cT}


_CACHE = {}


def kernel(**inputs):
    n = 8
    if "k" not in _CACHE:
        _CACHE["k"] = Kern(L, False)
    K = _CACHE["k"]
    sh = host_shared(inputs)
    in_maps = []
    for b in range(n):
        m = dict(sh)
        m.update(host_core(inputs, b))
        in_maps.append(m)
    res = run_bass_kernel_spmd(K.nc, in_maps, core_ids=list(range(n)))
    out = np.stack([np.ascontiguousarray(np.asarray(r["outT"]).T) for r in res.results], 0)
    return out.astype(np.float32)
```
